# Optimizing a Trainium2 kernel written in Bass

```python
import jax, jax.numpy as jnp
from jax import lax
import numpy as np

D_MODEL = 1024
BATCH = 32
SEQ = 256
DEPTH = 4
DEC_BATCH = 2
DEC_SEQ = 2048
PAST_LEN = 256

GRID_W = 64
N_MIXERS = 3
CHUNK = 128
D_GMLP = 2 * D_MODEL
N_GROUPS_A = 8
POOL_WINDOWS = (2, 4, 8, 16)
N_POOL_GROUPS = 4
D_POOL_GROUP = D_MODEL // N_POOL_GROUPS
N_HEADS = 16
N_KV_HEADS = 4
HEAD_DIM = 64
Q_PER_KV = N_HEADS // N_KV_HEADS
WINDOW = 128
BLOCK = 128
ROPE_THETA = 10000.0
D_FF = 2816
CONV_W = 3
N_A_LAYERS = (DEPTH + 2) // 3
N_B_LAYERS = (DEPTH + 1) // 3
N_C_LAYERS = DEPTH // 3
EPS = 1e-6
NEG_INF = -1e30

kernel_name = 'hybrid_dit_gmlp_pool_swa_step'


def rms_norm(x, g):
    xf = x.astype(jnp.float32)
    y = xf * lax.rsqrt(jnp.mean(xf * xf, axis=-1, keepdims=True) + EPS)
    return (y * g.astype(jnp.float32)).astype(x.dtype)


def ada_params(cond, w, b):
    m = jax.nn.silu(cond) @ w + b
    return jnp.split(m[:, None, :], 6, axis=-1)


def chunk_gmlp(h, w_in, g_v, w_s, b_s, w_out):
    B, S, _ = h.shape
    z = jax.nn.gelu(h @ w_in)
    u, v = jnp.split(z, 2, axis=-1)
    v = rms_norm(v, g_v)
    vc = v.reshape(B, S // CHUNK, CHUNK, N_GROUPS_A, D_GMLP // N_GROUPS_A)
    sv = jnp.einsum('gij,bnjgc->bnigc', w_s, vc) + b_s.T[None, None, :, :, None]
    return (u * sv.reshape(B, S, D_GMLP)) @ w_out


def multi_pool(h, w_grp, b_grp, scale):
    B, S, D = h.shape
    hf = h.astype(jnp.float32).reshape(B, S, N_POOL_GROUPS, D_POOL_GROUP)
    cs = jnp.concatenate([jnp.zeros_like(hf[:, :1]), jnp.cumsum(hf, axis=1)], axis=1)
    t = jnp.arange(S)[:, None]
    half = jnp.array(POOL_WINDOWS, dtype=jnp.int32)[None, :] // 2
    lo = jnp.clip(t - half, 0, S - 1)
    hi = jnp.clip(t + half - 1, 0, S - 1)
    gi = jnp.arange(N_POOL_GROUPS)[None, :]
    cnt = (hi - lo + 1).astype(jnp.float32)[None, :, :, None]
    pooled = (cs[:, hi + 1, gi] - cs[:, lo, gi]) / cnt
    d = (pooled - hf).astype(h.dtype)
    y = jnp.einsum('bsgi,gio->bsgo', d, w_grp).reshape(B, S, D) + b_grp
    return y * scale


def qkv_proj(h, w_qkv, g_q, g_k):
    B, S, _ = h.shape
    q, k, v = jnp.split(h @ w_qkv, [N_HEADS * HEAD_DIM, (N_HEADS + N_KV_HEADS) * HEAD_DIM], axis=-1)
    q = rms_norm(q.reshape(B, S, N_HEADS, HEAD_DIM), g_q)
    k = rms_norm(k.reshape(B, S, N_KV_HEADS, HEAD_DIM), g_k)
    return q, k, v.reshape(B, S, N_KV_HEADS, HEAD_DIM)


def axial_rope(x):
    B, S, H, _ = x.shape
    rows = S // GRID_W
    row = jnp.repeat(jnp.arange(rows), GRID_W).astype(jnp.float32)
    col = jnp.tile(jnp.arange(GRID_W), rows).astype(jnp.float32)
    n_freq = HEAD_DIM // 4
    inv = ROPE_THETA ** (-jnp.arange(n_freq, dtype=jnp.float32) / n_freq)
    ang = jnp.stack([row[:, None] * inv, col[:, None] * inv], axis=1)[:, None]
    cos, sin = jnp.cos(ang), jnp.sin(ang)
    xr = x.astype(jnp.float32).reshape(B, S, H, 2, 2, n_freq)
    a, b = xr[..., 0, :], xr[..., 1, :]
    out = jnp.stack([a * cos - b * sin, a * sin + b * cos], axis=-2)
    return out.reshape(x.shape).astype(x.dtype)


def sink_softmax(scores, sink):
    sk = jnp.broadcast_to(sink[:, :, None, None].astype(jnp.float32), scores.shape[:-1] + (1,))
    p = jax.nn.softmax(jnp.concatenate([scores, sk], axis=-1), axis=-1)
    return p[..., :-1]


def context_attention(q, k, v, sink):
    B, S = q.shape[:2]
    nb = S // BLOCK
    scale = HEAD_DIM ** -0.5
    qb = q.reshape(B, nb, BLOCK, N_KV_HEADS, Q_PER_KV, HEAD_DIM).transpose(1, 0, 2, 3, 4, 5)

    def one_block(qi):
        s = jnp.einsum('bqkgd,bskd->bkgqs', qi, k, preferred_element_type=jnp.float32) * scale
        p = sink_softmax(s, sink).astype(v.dtype)
        return jnp.einsum('bkgqs,bskd->bqkgd', p, v)

    o = lax.map(one_block, qb)
    return o.transpose(1, 0, 2, 3, 4, 5).reshape(B, S, N_HEADS * HEAD_DIM)


def latent_attention(q, k, v, ctx_k, ctx_v, sink):
    B, S = q.shape[:2]
    nb = S // BLOCK
    span = 3 * BLOCK
    scale = HEAD_DIM ** -0.5
    qb = q.reshape(B, nb, BLOCK, N_KV_HEADS, Q_PER_KV, HEAD_DIM).transpose(1, 0, 2, 3, 4, 5)
    pad = ((0, 0), (BLOCK, BLOCK), (0, 0), (0, 0))
    kp, vp = jnp.pad(k, pad), jnp.pad(v, pad)

    def one_block(args):
        qi, n = args
        start = n * BLOCK
        kw = lax.dynamic_slice_in_dim(kp, start, span, axis=1)
        vw = lax.dynamic_slice_in_dim(vp, start, span, axis=1)
        qpos = start + jnp.arange(BLOCK)
        kpos = start - BLOCK + jnp.arange(span)
        valid = (kpos[None, :] >= 0) & (kpos[None, :] < S) & (jnp.abs(qpos[:, None] - kpos[None, :]) <= WINDOW)
        s_w = jnp.einsum('bqkgd,bskd->bkgqs', qi, kw, preferred_element_type=jnp.float32) * scale
        s_w = jnp.where(valid, s_w, NEG_INF)
        s_c = jnp.einsum('bqkgd,bskd->bkgqs', qi, ctx_k, preferred_element_type=jnp.float32) * scale
        p = sink_softmax(jnp.concatenate([s_w, s_c], axis=-1), sink).astype(v.dtype)
        return (jnp.einsum('bkgqs,bskd->bqkgd', p[..., :span], vw)
                + jnp.einsum('bkgqs,bskd->bqkgd', p[..., span:], ctx_v))

    o = lax.map(one_block, (qb, jnp.arange(nb)))
    return o.transpose(1, 0, 2, 3, 4, 5).reshape(B, S, N_HEADS * HEAD_DIM)


def conv_ffn(h, w_in, conv_w, conv_b, w_out):
    a, u = jnp.split(h @ w_in, 2, axis=-1)
    ap = jnp.pad(a, ((0, 0), (1, 1), (0, 0)))
    a = ap[:, :-2] * conv_w[0] + ap[:, 1:-1] * conv_w[1] + ap[:, 2:] * conv_w[2] + conv_b
    return (jax.nn.gelu(a) * u) @ w_out


def run_trunk(x, cond, is_context, cache_k, cache_v, weights):
    (w_ada, b_ada, g_mix, g_ffn, w_ffn_in, ffn_conv_w, ffn_conv_b, w_ffn_out,
     a_w_in, a_g_v, a_w_s, a_b_s, a_w_out,
     p_w, p_b, p_scale,
     c_w_qkv, c_g_q, c_g_k, c_sink, c_w_o) = weights
    new_k, new_v = [], []
    for i in range(DEPTH):
        kind, j = i % N_MIXERS, i // N_MIXERS
        sh1, sc1, gt1, sh2, sc2, gt2 = ada_params(cond, w_ada[i], b_ada[i])
        h = rms_norm(x, g_mix[i]) * (1 + sc1) + sh1
        if kind == 0:
            y = chunk_gmlp(h, a_w_in[j], a_g_v[j], a_w_s[j], a_b_s[j], a_w_out[j])
        elif kind == 1:
            y = multi_pool(h, p_w[j], p_b[j], p_scale[j])
        else:
            q, k, v = qkv_proj(h, c_w_qkv[j], c_g_q[j], c_g_k[j])
            sink = c_sink[j].reshape(N_KV_HEADS, Q_PER_KV)
            if is_context:
                o = context_attention(q, k, v, sink)
                new_k.append(k)
                new_v.append(v)
            else:
                o = latent_attention(axial_rope(q), axial_rope(k), v, cache_k[:, j], cache_v[:, j], sink)
            y = o @ c_w_o[j]
        x = x + gt1 * y
        h = rms_norm(x, g_ffn[i]) * (1 + sc2) + sh2
        x = x + gt2 * conv_ffn(h, w_ffn_in[i], ffn_conv_w[i], ffn_conv_b[i], w_ffn_out[i])
    return x, new_k, new_v


def setup_inputs(seed: int = 0) -> dict:
    key = jax.random.key(seed)
    ks = jax.random.split(key, 32)
    nrm = jax.random.normal
    D = D_MODEL
    f32 = jnp.float32
    kvd = (N_C_LAYERS, PAST_LEN, N_KV_HEADS, HEAD_DIM)
    return {
        'x_prompt': nrm(ks[0], (BATCH, SEQ, D), f32),
        'x_sample': nrm(ks[1], (DEC_BATCH, DEC_SEQ, D), f32),
        'cache_k': nrm(ks[2], (DEC_BATCH,) + kvd, f32),
        'cache_v': nrm(ks[3], (DEC_BATCH,) + kvd, f32),
        'c': nrm(ks[4], (DEC_BATCH, D), f32),
        'c_ctx': nrm(ks[5], (D,), f32),
        'w_ada': nrm(ks[6], (DEPTH, D, 6 * D), f32) * (0.5 * D ** -0.5),
        'b_ada': nrm(ks[7], (DEPTH, 6 * D), f32) * 0.01,
        'g_mix': 1.0 + 0.1 * nrm(ks[8], (DEPTH, D), f32),
        'g_ffn': 1.0 + 0.1 * nrm(ks[9], (DEPTH, D), f32),
        'w_ffn_in': nrm(ks[10], (DEPTH, D, 2 * D_FF), f32) * D ** -0.5,
        'ffn_conv_w': nrm(ks[11], (DEPTH, CONV_W, D_FF), f32) * CONV_W ** -0.5,
        'ffn_conv_b': nrm(ks[12], (DEPTH, D_FF), f32) * 0.02,
        'w_ffn_out': nrm(ks[13], (DEPTH, D_FF, D), f32) * D_FF ** -0.5,
        'a_w_in': nrm(ks[14], (N_A_LAYERS, D, 2 * D_GMLP), f32) * D ** -0.5,
        'a_g_v': 1.0 + 0.1 * nrm(ks[15], (N_A_LAYERS, D_GMLP), f32),
        'a_w_s': nrm(ks[16], (N_A_LAYERS, N_GROUPS_A, CHUNK, CHUNK), f32) * CHUNK ** -0.5,
        'a_b_s': 1.0 + 0.1 * nrm(ks[17], (N_A_LAYERS, N_GROUPS_A, CHUNK), f32),
        'a_w_out': nrm(ks[18], (N_A_LAYERS, D_GMLP, D), f32) * D_GMLP ** -0.5,
        'p_w': nrm(ks[19], (N_B_LAYERS, N_POOL_GROUPS, D_POOL_GROUP, D_POOL_GROUP), f32) * D_POOL_GROUP ** -0.5,
        'p_b': nrm(ks[20], (N_B_LAYERS, D), f32) * 0.02,
        'p_scale': 1.0 + 0.1 * nrm(ks[21], (N_B_LAYERS, D), f32),
        'c_w_qkv': nrm(ks[22], (N_C_LAYERS, D, (N_HEADS + 2 * N_KV_HEADS) * HEAD_DIM), f32) * D ** -0.5,
        'c_g_q': 1.0 + 0.1 * nrm(ks[23], (N_C_LAYERS, HEAD_DIM), f32),
        'c_g_k': 1.0 + 0.1 * nrm(ks[24], (N_C_LAYERS, HEAD_DIM), f32),
        'c_sink': nrm(ks[25], (N_C_LAYERS, N_HEADS), f32),
        'c_w_o': nrm(ks[26], (N_C_LAYERS, N_HEADS * HEAD_DIM, D), f32) * (N_HEADS * HEAD_DIM) ** -0.5,
    }


def reference(x_prompt, x_sample, cache_k, cache_v, c, c_ctx,
              w_ada, b_ada, g_mix, g_ffn, w_ffn_in, ffn_conv_w, ffn_conv_b, w_ffn_out,
              a_w_in, a_g_v, a_w_s, a_b_s, a_w_out,
              p_w, p_b, p_scale,
              c_w_qkv, c_g_q, c_g_k, c_sink, c_w_o):
    weights = (w_ada, b_ada, g_mix, g_ffn, w_ffn_in, ffn_conv_w, ffn_conv_b, w_ffn_out,
               a_w_in, a_g_v, a_w_s, a_b_s, a_w_out,
               p_w, p_b, p_scale,
               c_w_qkv, c_g_q, c_g_k, c_sink, c_w_o)
    y_prompt, ks_ctx, vs_ctx = run_trunk(x_prompt, c_ctx[None, :], True, None, None, weights)
    new_cache_k = jnp.stack(ks_ctx, axis=1)
    new_cache_v = jnp.stack(vs_ctx, axis=1)
    y_sample, _, _ = run_trunk(x_sample, c, False, cache_k, cache_v, weights)
    return (y_prompt, y_sample, new_cache_k, new_cache_v)
```

```python
import contextlib
import numpy as np
import concourse.bass as bass
import concourse.mybir as mybir
from concourse.bass_utils import run_bass_kernel_spmd

F32 = mybir.dt.float32
BF16 = mybir.dt.bfloat16
AF = mybir.ActivationFunctionType
ALU = mybir.AluOpType
AX = mybir.AxisListType

D = 1024
DEPTH = 4
NCH = 8
DFF = 2816
NJ = 22
EPS = 1e-6
CTX_COLS = 1024
LAT_COLS = 1152
NSLOT = 5
SLOT_E = 2048
NRING = 16
PREFETCH = NSLOT - 1

RUN_LAT = True
DEBUG_STOP = None


class Plan:
    def __init__(self):
        self.q = {e: [] for e in ("pe", "act", "dve", "pool", "sp")}
        self.cnt = {e: 0 for e in ("pe", "act", "dve", "pool")}
        self.waited = {e: {} for e in self.q}
        self.lastw = {}
        self.readers = {}
        self.ring = {"sp": [0] * NRING, "pool": [0] * NRING}
        self.ring_i = {"sp": 0, "pool": 0}
        self.dry = False

    def _wait(self, eng, semkey, val):
        if semkey == "pe" and eng == "pe":
            return
        if self.waited[eng].get(semkey, 0) >= val:
            return
        self.waited[eng][semkey] = val
        self.q[eng].append(("wait", semkey, val))

    def _sync(self, eng, reads, writes):
        for k in reads:
            lw = self.lastw.get(k)
            if lw:
                self._wait(eng, lw[0], lw[1])
        for k in writes:
            lw = self.lastw.get(k)
            if lw:
                self._wait(eng, lw[0], lw[1])
            for sk, v in self.readers.get(k, {}).items():
                self._wait(eng, sk, v)

    def _commit(self, ev, reads, writes):
        sk, v = ev
        for k in reads:
            d = self.readers.setdefault(k, {})
            if d.get(sk, 0) < v:
                d[sk] = v
        for k in writes:
            self.lastw[k] = ev
            self.readers[k] = {}

    def op(self, eng, fn, reads=(), writes=()):
        if self.dry:
            return
        reads = list(reads)
        writes = list(writes)
        self._sync(eng, reads, writes)
        self.cnt[eng] += 1
        self.q[eng].append(("op", fn, True))
        self._commit((eng, self.cnt[eng]), reads, writes)

    def mm(self, fns, reads=(), writes=()):
        if self.dry:
            return
        reads = list(reads)
        writes = list(writes)
        self._sync("pe", reads, writes)
        for f in fns[:-1]:
            self.q["pe"].append(("op", f, False))
        self.cnt["pe"] += 1
        self.q["pe"].append(("op", fns[-1], True))
        self._commit(("pe", self.cnt["pe"]), reads, writes)

    def dma(self, queue, fn, reads=(), writes=()):
        if self.dry:
            return
        reads = list(reads)
        writes = list(writes)
        self._sync(queue, reads, writes)
        i = self.ring_i[queue]
        self.ring_i[queue] += 1
        slot = i % NRING
        prev = self.ring[queue][slot]
        semkey = (queue, slot)
        if prev:
            self._wait(queue, semkey, prev)
        val = prev + 16
        self.ring[queue][slot] = val
        self.q[queue].append(("dma", fn, semkey))
        self._commit((semkey, val), reads, writes)


def gran(buf, c, lo, hi):
    return [(buf, c, g) for g in range(lo // 128, (hi - 1) // 128 + 1)]


def split_tiles(lo, hi, maxn=512):
    out = []
    a = lo
    while a < hi:
        b = min(hi, (a // maxn + 1) * maxn)
        out.append((a, b))
        a = b
    return out


def even_tiles(lo, hi, maxn=512):
    n = -(-(hi - lo) // maxn)
    step = -(-(hi - lo) // n)
    step += step % 2
    out = []
    a = lo
    while a < hi:
        out.append((a, min(hi, a + step)))
        a += step
    return out


def build_program():
    nc = bass.Bass("TRN2", target_bir_lowering=False)
    P = Plan()
    dram_in = {}

    def din(name, shape):
        t = nc.dram_tensor(name, list(shape), F32, kind="ExternalInput")
        dram_in[name] = tuple(shape)
        return t.ap()

    def dout(name, shape):
        return nc.dram_tensor(name, list(shape), F32, kind="ExternalOutput").ap()

    xc_d = din("xc", [128, NCH, CTX_COLS])
    xl_d = din("xl", [128, NCH, LAT_COLS])
    cond_d = din("condT", [128, NCH, 2])
    wada_d = din("w_ada_t", [DEPTH, 24, 128, 2 * 8 * 128])
    bada_d = din("b_ada_t", [128, DEPTH, 48, 2])
    gmix_d = din("g_mix_t", [128, DEPTH, NCH])
    gffn_d = din("g_ffn_t", [128, DEPTH, NCH])
    wfin_d = din("w_ffn_in_t", [DEPTH, 44, 128, 8 * 128])
    cw_d = din("conv_w_t", [128, DEPTH, 3, NJ])
    cb_d = din("conv_b_t", [128, DEPTH, NJ])
    wfout_d = din("w_ffn_out_t", [DEPTH, 16, 128, 11 * 128])
    awu_d = din("a_w_u_t", [2, 16, 128, 8 * 128])
    awv_d = din("a_w_v_t", [2, 4, 128, 8 * 512])
    agv_d = din("a_g_v_t", [128, 2, 16])
    aws_d = din("a_w_s_t", [2, 128, 8 * 128])
    abs_d = din("a_b_s_t", [128, 2, 8, 128])
    awo_d = din("a_w_out_t", [2, 8, 128, 16 * 128])
    pw_d = din("p_w_t", [128, 8 * 2 * 128])
    pb_d = din("p_b_t", [128, NCH])
    psc_d = din("p_scale_t", [128, NCH])
    wq_d = din("c_w_q_t", [8, 128, 8 * 128])
    wk_d = din("c_w_k_t", [4, 128, 8 * 128])
    wvt_d = din("c_w_vt_t", [128, 8 * 256])
    wvf_d = din("c_w_vf_t", [2, 128, 8 * 128])
    wo_d = din("c_w_o_t", [8, 128, 8 * 128])
    gq_d = din("c_g_q_t", [128, 1])
    gk_d = din("c_g_k_t", [128, 1])
    sink_d = din("c_sink_t", [128, 16])
    ones_d = din("ones_t", [128, 128])
    blk_d = din("blk_t", [128, 128])
    pm_d = din("pm_t", [128, 128])
    cos_d = din("cos_t", [128, LAT_COLS])
    sin_d = din("sin_t", [128, LAT_COLS])
    wmask_d = din("wmask_t", [128, 2, 128])
    flags_d = din("flags_t", [128, 2])
    icc_d = din("ic_ctx_t", [128, 2, 4, 16])
    icl_d = din("ic_lat_t", [128, 2, 4, 16])
    ck_d = din("cache_k_t", [128, 4, 256])
    cv_d = din("cache_v_t", [128, 2, 256])

    yc_d = dout("yc", [128, NCH, CTX_COLS])
    yl_d = dout("yl", [128, NCH, LAT_COLS])
    nk_d = dout("nk", [256, CTX_COLS])
    nv_d = dout("nv", [256, CTX_COLS])

    es = contextlib.ExitStack()

    def sb(name, shape, dt):
        return es.enter_context(nc.sbuf_tensor(name, list(shape), dt))

    with es:
        X = sb("X", [128, NCH, LAT_COLS], F32)
        H = sb("H", [128, NCH, LAT_COLS], BF16)
        HID = sb("HID", [128, NJ, LAT_COLS], BF16)
        SCR = sb("SCR", [128, 8192], F32)
        AS = sb("AS", [128, 2, 1160], F32)
        NTMP = 6
        TMPR = sb("TMPR", [128, NTMP, 512], F32)
        NSQ = 4
        SQ = sb("SQ", [128, NSQ, 512], BF16)
        NRS = 3
        RS = sb("RS", [128, NRS, 512], F32)
        WSL = sb("WSL", [128, NSLOT, SLOT_E], BF16)
        MOD = sb("MOD", [128, DEPTH, 48, 2], F32)
        BADA = sb("BADA", [128, DEPTH, 48, 2], F32)
        G1 = sb("G1", [128, DEPTH, 2, NCH], F32)
        G2 = sb("G2", [128, DEPTH, 2, NCH], F32)
        GMIX = sb("GMIX", [128, DEPTH, NCH], F32)
        GFFN = sb("GFFN", [128, DEPTH, NCH], F32)
        CW = sb("CW", [128, DEPTH, 3, NJ], F32)
        CB = sb("CB", [128, DEPTH, NJ], F32)
        CONDT = sb("CONDT", [128, NCH, 2], F32)
        SCOND = sb("SCOND", [128, NCH, 2], BF16)
        AGV = sb("AGV", [128, 2, 16], F32)
        ABS = sb("ABS", [128, 2, 8, 128], F32)
        WST = sb("WST", [128, 2, 8 * 128], BF16)
        PB = sb("PB", [128, NCH], F32)
        PSC = sb("PSC", [128, NCH], F32)
        PA = sb("PA", [128, 2, NCH], F32)
        PBB = sb("PBB", [128, 2, NCH], F32)
        G1F = sb("G1F", [128, 2, NCH], F32)
        SH1F = sb("SH1F", [128, 2, NCH], F32)
        GQ = sb("GQ", [128, 1], F32)
        GK = sb("GK", [128, 1], F32)
        SINK = sb("SINK", [128, 16], F32)
        ESK = sb("ESK", [128, 16], F32)
        ONES = sb("ONES", [128, 128], BF16)
        BLK = sb("BLK", [128, 128], BF16)
        PM = sb("PM", [128, 128], BF16)
        WMASK = sb("WMASK", [128, 2, 128], F32)
        FLAGS = sb("FLAGS", [128, 2], F32)
        NEGB = sb("NEGB", [128, 2], F32)
        ZB = sb("ZB", [128, 1], F32)
        EPST = sb("EPST", [128, 1], F32)
        ICC = sb("ICC", [128, 2, 4, 16], F32)
        ICL = sb("ICL", [128, 2, 4, 16], F32)
        SS = sb("SS", [128, 8], F32)
        PS = es.enter_context(nc.psum_tensor("PS", [128, 4096], F32))

        WV = SCR[:, 0:8192].bitcast(BF16)
        COS = SCR[:, 0:LAT_COLS]
        SIN = SCR[:, LAT_COLS:2 * LAT_COLS]
        PT = SCR[:, 0:2560].bitcast(BF16)
        VE = SCR[:, 2560:2560 + 4224].bitcast(BF16)
        H32 = HID[:, :, :].bitcast(F32)

        ps_state = {"i": 0}

        NPSB = 7

        def ps_alloc(n=1):
            i = ps_state["i"]
            if i + n > NPSB:
                i = 0
            ps_state["i"] = (i + n) % NPSB
            return i

        def psb(b, n=512, nb=1):
            return PS[:, b * 512: b * 512 + (n if nb == 1 else nb * 512)]

        def pskeys(b, nb=1):
            return [("ps", b + i) for i in range(nb)]

        ring_state = {"tmp": 0, "sq": 0, "rs": 0, "pt": 0}

        def nxt(name, n):
            i = ring_state[name]
            ring_state[name] = (i + 1) % n
            return i

        sched = []
        wstate = {"issued": 0, "used": 0, "list": []}

        def w_issue_upto(k):
            L = wstate["list"]
            while wstate["issued"] < min(k, len(L)):
                i = wstate["issued"]
                ap, ne = L[i]
                s = i % NSLOT
                P.dma("pool", (lambda e, s=s, ap=ap, ne=ne: e.dma_start(out=WSL[:, s, 0:ne], in_=ap)),
                      reads=[], writes=[("w", s)])
                wstate["issued"] += 1

        def getw(*aps):
            n = len(aps)
            if P.dry:
                wstate["list"].extend(aps)
                return 0 if n == 1 else [0] * n
            i = wstate["used"]
            wstate["used"] += n
            w_issue_upto(i + NSLOT)
            if n == 1:
                return i % NSLOT
            return [(i + k) % NSLOT for k in range(n)]

        def load_f32(dst, src, key):
            P.dma("sp", (lambda e: e.dma_start(out=dst, in_=src)), reads=[], writes=[key])

        def load_bf16(dst, src, key):
            P.dma("pool", (lambda e: e.dma_start(out=dst, in_=src)), reads=[], writes=[key])

        def phase_consts():
            load_f32(CONDT[:, :, :], cond_d, "condt")
            load_f32(BADA[:, :, :, :], bada_d, "bada")
            load_f32(GMIX[:, :, :], gmix_d, "gmix")
            load_f32(GFFN[:, :, :], gffn_d, "gffn")
            phase_load_x(groups[0])
            load_f32(CW[:, :, :, :], cw_d, "cw")
            load_f32(CB[:, :, :], cb_d, "cb")
            load_f32(AGV[:, :, :], agv_d, "agv")
            load_f32(ABS[:, :, :, :], abs_d, "abs")
            load_f32(PB[:, :], pb_d, "pb")
            load_f32(PSC[:, :], psc_d, "psc")
            load_f32(GQ[:, :], gq_d, "gq")
            load_f32(GK[:, :], gk_d, "gk")
            load_f32(SINK[:, :], sink_d, "sink")
            load_f32(WMASK[:, :, :], wmask_d, "wmask")
            load_f32(FLAGS[:, :], flags_d, "flags")
            load_f32(ICC[:, :, :, :], icc_d, "icc")
            load_f32(ICL[:, :, :, :], icl_d, "icl")
            load_bf16(ONES[:, :], ones_d, "ones")
            load_bf16(BLK[:, :], blk_d, "blk")
            load_bf16(PM[:, :], pm_d, "pm")
            for j in range(2):
                load_bf16(WST[:, j, :], aws_d[j], "wst")
            P.op("dve", lambda e: e.memset(ZB[:, :], 0.0), writes=["zb"])
            P.op("dve", lambda e: e.memset(EPST[:, :], float(EPS)), writes=["epst"])
            P.op("act", lambda e: e.activation(out=SCOND[:, :, :], in_=CONDT[:, :, :], func=AF.Silu),
                 reads=["condt"], writes=["scond"])
            P.op("act", lambda e: e.activation(out=ESK[:, :], in_=SINK[:, :], func=AF.Exp),
                 reads=["sink"], writes=["esk"])
            P.op("dve", lambda e: e.tensor_scalar(out=NEGB[:, :], in0=FLAGS[:, :], scalar1=30000.0, scalar2=-30000.0,
                                                  op0=ALU.mult, op1=ALU.add), reads=["flags"], writes=["negb"])


        ada_pending = []

        def ada_enqueue(l):
            for i in range(24):
                ada_pending.append((l, i))

        def ada_step():
            if not ada_pending:
                return
            l, i = ada_pending.pop(0)
            b = 7
            s = getw((wada_d[l, i], 2048))
            for h in range(2):
                cc = i * 2 + h
                fns = []
                for kc in range(8):
                    fns.append(lambda e, s=s, h=h, kc=kc, cc=cc, b=b: e.matmul(
                        PS[:, b * 512 + cc * 2: b * 512 + cc * 2 + 2],
                        WSL[:, s, (h * 8 + kc) * 128:(h * 8 + kc + 1) * 128],
                        SCOND[:, kc, :], start=(kc == 0), stop=(kc == 7)))
                P.mm(fns, reads=[("w", s), "scond"], writes=pskeys(b))
            if i == 7:
                ada_finish1(l, b)
            if i == 23:
                ada_finish2(l, b)

        def ada_drain():
            while ada_pending:
                ada_step()

        def ada_finish1(l, b):
            P.op("dve", lambda e, b=b, l=l: e.tensor_tensor(
                out=MOD[:, l, 0:16, :], in0=PS[:, b * 512: b * 512 + 32].rearrange("p (c t) -> p c t", t=2),
                in1=BADA[:, l, 0:16, :], op=ALU.add), reads=pskeys(b) + ["bada"], writes=[("mod1", l)])
            for ci in range(2):
                P.op("dve", lambda e, l=l, ci=ci: e.scalar_tensor_tensor(
                    out=G1[:, l, ci, :], in0=MOD[:, l, 8:16, ci], scalar=1.0, in1=GMIX[:, l, :],
                    op0=ALU.add, op1=ALU.mult), reads=[("mod1", l), "gmix"], writes=[("g1", l)])
            if l == 1:
                ada_masked_tables()

        def ada_masked_tables():
            for fl in range(2):
                P.op("dve", lambda e, fl=fl: e.tensor_scalar(out=G1F[:, fl, :], in0=G1[:, 1, 1, :], scalar1=FLAGS[:, fl:fl + 1],
                                                             scalar2=None, op0=ALU.mult),
                     reads=[("g1", 1), "flags"], writes=["g1f"])
                P.op("dve", lambda e, fl=fl: e.tensor_scalar(out=SH1F[:, fl, :], in0=MOD[:, 1, 0:8, 1], scalar1=FLAGS[:, fl:fl + 1],
                                                             scalar2=None, op0=ALU.mult),
                     reads=[("mod1", 1), "flags"], writes=["g1f"])

        def ada_finish2(l, b):
            P.op("dve", lambda e, b=b, l=l: e.tensor_tensor(
                out=MOD[:, l, 16:48, :], in0=PS[:, b * 512 + 32: b * 512 + 96].rearrange("p (c t) -> p c t", t=2),
                in1=BADA[:, l, 16:48, :], op=ALU.add), reads=pskeys(b) + ["bada"], writes=[("mod", l)])
            for ci in range(2):
                P.op("dve", lambda e, l=l, ci=ci: e.scalar_tensor_tensor(
                    out=G2[:, l, ci, :], in0=MOD[:, l, 32:40, ci], scalar=1.0, in1=GFFN[:, l, :],
                    op0=ALU.add, op1=ALU.mult), reads=[("mod", l), "gffn"], writes=[("g2", l)])
            if l == 1:
                for ci in range(2):
                    P.op("dve", lambda e, ci=ci: e.tensor_tensor(out=PA[:, ci, :], in0=MOD[:, 1, 16:24, ci],
                                                                  in1=PSC[:, :], op=ALU.mult),
                         reads=[("mod", 1), "psc"], writes=["pa"])
                    P.op("dve", lambda e, ci=ci: e.tensor_tensor(out=PBB[:, ci, :], in0=PA[:, ci, :],
                                                                  in1=PB[:, :], op=ALU.mult),
                         reads=["pa", "pb"], writes=["pbb"])

        class Grp:
            pass

        GC = Grp()
        GC.name = "c"; GC.ci = 0; GC.cols = CTX_COLS; GC.nseg = 4; GC.L = 256
        GC.tiles = [(0, 512), (512, 1024)]
        GC.x_d = xc_d; GC.y_d = yc_d; GC.lat = False
        GL = Grp()
        GL.name = "l"; GL.ci = 1; GL.cols = LAT_COLS; GL.nseg = 1; GL.L = LAT_COLS
        GL.tiles = [(0, 512), (512, 1024), (1024, 1152)]
        GL.x_d = xl_d; GL.y_d = yl_d; GL.lat = True
        GC.win_f = [(0, 1024)] * 4
        GC.win_m = [(0, 1024)] * 4
        GC.qwin = (0, 1024)
        GL.win_f = [(117, 1035), (126, 1026), (255, 897), (254, 771)]
        GL.win_m = [(0, 1152), (118, 1034), (0, 1152), (256, 896)]
        GL.qwin = (128, 1024)

        def v3(ap2d, g, n):
            if g.nseg > 1:
                return ap2d.rearrange("p (s l) -> p s l", l=g.L)
            return ap2d

        def as_view(g, par, lo, hi, sh):
            if g.nseg > 1:
                base = AS[:, par, 0:g.nseg * (g.L + 2)].rearrange("p (s l) -> p s l", l=g.L + 2)
                return base[:, lo // g.L: hi // g.L, 1 + sh: 1 + sh + g.L]
            return AS[:, par, 1 + lo + sh: 1 + hi + sh]

        def as_keys(par, lo, hi):
            return [("as", par, k) for k in range(max(lo, 0) // 128, (hi - 1) // 128 + 1)]

        def phase_load_x(g):
            for c in range(NCH):
                P.dma("sp", (lambda e, c=c: e.dma_start(out=X[:, c, 0:g.cols], in_=g.x_d[:, c, :])),
                      reads=[], writes=gran("x", c, 0, g.cols))
            P.op("pool", lambda e: e.memset(AS[:, :, :], 0.0), writes=as_keys(0, 0, 1160) + as_keys(1, 0, 1160))

        def phase_store_x(g):
            for c in range(NCH):
                P.dma("sp", (lambda e, c=c: e.dma_start(out=g.y_d[:, c, :], in_=X[:, c, 0:g.cols])),
                      reads=gran("x", c, 0, g.cols), writes=[])

        pre_stats = {}

        def norm_stats_tile(lo, hi, all_act):
            n = hi - lo
            b = ps_alloc()
            for c in range(NCH):
                s = nxt("sq", NSQ)
                if c % 2 == 0 or all_act:
                    P.op("act", lambda e, s=s, c=c, lo=lo, hi=hi, n=n: e.activation(
                        out=SQ[:, s, 0:n], in_=X[:, c, lo:hi], func=AF.Square),
                        reads=gran("x", c, lo, hi), writes=[("sq", s)])
                else:
                    P.op("dve", lambda e, s=s, c=c, lo=lo, hi=hi, n=n: e.tensor_tensor(
                        out=SQ[:, s, 0:n], in0=X[:, c, lo:hi], in1=X[:, c, lo:hi], op=ALU.mult),
                        reads=gran("x", c, lo, hi), writes=[("sq", s)])
                P.mm([lambda e, s=s, c=c, b=b, n=n: e.matmul(PS[:, b * 512: b * 512 + n], ONES[:, :], SQ[:, s, 0:n],
                                                              start=(c == 0), stop=(c == NCH - 1))],
                     reads=[("sq", s), "ones"], writes=pskeys(b))
            r = nxt("rs", NRS)
            P.op("act", lambda e, r=r, b=b, n=n: e.activation(
                out=RS[:, r, 0:n], in_=PS[:, b * 512: b * 512 + n], func=AF.Ln, scale=1.0 / D,
                bias=EPST[:, 0:1]), reads=pskeys(b) + ["epst"], writes=[("rs", r)])
            P.op("act", lambda e, r=r, n=n: e.activation(
                out=RS[:, r, 0:n], in_=RS[:, r, 0:n], func=AF.Exp, scale=-0.5),
                reads=[("rs", r)], writes=[("rs", r)])
            return r

        def norm_stats(tiles):
            out = []
            for ti_, (lo, hi) in enumerate(tiles):
                if (lo, hi) in pre_stats:
                    out.append(pre_stats.pop((lo, hi)))
                else:
                    out.append(norm_stats_tile(lo, hi, ti_ == 0))
            return out

        def pre_norm_a(nt0):
            lo, hi = nt0
            n = hi - lo
            for c in range(NCH):
                P.op("act", lambda e, c=c, lo=lo, hi=hi, n=n: e.activation(
                    out=H[:, c, 0:n], in_=X[:, c, lo:hi], func=AF.Square),
                    reads=gran("x", c, lo, hi), writes=gran("h", c, 0, n))

        def pre_norm_b(nt0):
            lo, hi = nt0
            n = hi - lo
            b = ps_alloc()
            for c in range(NCH):
                P.mm([lambda e, c=c, b=b, n=n: e.matmul(PS[:, b * 512: b * 512 + n], ONES[:, :], H[:, c, 0:n],
                                                        start=(c == 0), stop=(c == NCH - 1))],
                     reads=gran("h", c, 0, n) + ["ones"], writes=pskeys(b))
            r = nxt("rs", NRS)
            P.op("act", lambda e, r=r, b=b, n=n: e.activation(
                out=RS[:, r, 0:n], in_=PS[:, b * 512: b * 512 + n], func=AF.Ln, scale=1.0 / D,
                bias=EPST[:, 0:1]), reads=pskeys(b) + ["epst"], writes=[("rs", r)])
            P.op("act", lambda e, r=r, n=n: e.activation(
                out=RS[:, r, 0:n], in_=RS[:, r, 0:n], func=AF.Exp, scale=-0.5),
                reads=[("rs", r)], writes=[("rs", r)])
            pre_stats[nt0] = r

        def norm_mod(g, tiles, gsc, shf, dep_keys, out_ap_fn, out_keys_fn, tmp_view=None, flagged=None):
            if tiles[0] in pre_stats:
                r0 = pre_stats.pop(tiles[0])
                norm_apply(g, [tiles[0]], [r0], gsc, shf, dep_keys, out_ap_fn, out_keys_fn, tmp_view, flagged)
                rest = tiles[1:]
                norm_apply(g, rest, norm_stats(rest), gsc, shf, dep_keys, out_ap_fn, out_keys_fn, tmp_view, flagged)
                return
            norm_apply(g, tiles, norm_stats(tiles), gsc, shf, dep_keys, out_ap_fn, out_keys_fn, tmp_view, flagged)

        def norm_apply(g, tiles, rs, gsc, shf, dep_keys, out_ap_fn, out_keys_fn, tmp_view, flagged):
            for ti, (lo, hi) in enumerate(tiles):
                n = hi - lo
                r = rs[ti]
                for c in range(NCH):
                    t = nxt("tmp", NTMP)
                    P.op("dve", lambda e, t=t, c=c, r=r, lo=lo, hi=hi, n=n: e.tensor_tensor(
                        out=TMPR[:, t, 0:n], in0=X[:, c, lo:hi], in1=RS[:, r, 0:n], op=ALU.mult),
                        reads=gran("x", c, lo, hi) + [("rs", r)], writes=[("tmp", t)])
                    if flagged is not None:
                        gf, sf = flagged
                        for (a0, a1, fl) in ((0, 256, 0), (256, 768, None), (768, LAT_COLS, 1)):
                            s0, s1 = max(a0, lo), min(a1, hi)
                            if s0 >= s1:
                                continue
                            sc_ap = gsc[:, c:c + 1] if fl is None else gf[:, fl, c:c + 1]
                            bi_ap = shf(c) if fl is None else sf[:, fl, c:c + 1]
                            P.op("act", lambda e, t=t, c=c, s0=s0, s1=s1, lo=lo, sc_ap=sc_ap, bi_ap=bi_ap: e.activation(
                                out=out_ap_fn(c, s0, s1), in_=TMPR[:, t, s0 - lo:s1 - lo],
                                func=AF.Identity, scale=sc_ap, bias=bi_ap),
                                reads=[("tmp", t)] + dep_keys, writes=out_keys_fn(c, s0, s1))
                        continue
                    P.op("act", lambda e, t=t, c=c, lo=lo, hi=hi, n=n: e.activation(
                        out=out_ap_fn(c, lo, hi),
                        in_=(TMPR[:, t, 0:n] if tmp_view is None else tmp_view(TMPR[:, t, 0:n], n)),
                        func=AF.Identity, scale=gsc[:, c:c + 1], bias=shf(c)),
                        reads=[("tmp", t)] + dep_keys, writes=out_keys_fn(c, lo, hi))

        def h_out(c, lo, hi):
            return H[:, c, lo:hi]

        def h_keys(c, lo, hi):
            return gran("h", c, lo, hi)

        def out_proj(tiles, wspec, nk, per_slot, src, src_keys, l, ci, gate0, nt0=None):
            def one(nn, slots, lo, hi):
                n = hi - lo
                b = ps_alloc()
                fns = []
                for kk in range(nk):
                    sl = slots[kk // per_slot]
                    ko = kk % per_slot
                    fns.append(lambda e, sl=sl, ko=ko, kk=kk, b=b, lo=lo, hi=hi, n=n: e.matmul(
                        PS[:, b * 512: b * 512 + n], WSL[:, sl, ko * 128:(ko + 1) * 128], src(kk, lo, hi),
                        start=(kk == 0), stop=(kk == nk - 1)))
                rd = [("w", sl) for sl in slots]
                for kk in range(nk):
                    rd += src_keys(kk, lo, hi)
                P.mm(fns, reads=rd, writes=pskeys(b))
                P.op("dve", lambda e, b=b, lo=lo, hi=hi, n=n, nn=nn: e.scalar_tensor_tensor(
                    out=X[:, nn, lo:hi], in0=PS[:, b * 512: b * 512 + n], scalar=MOD[:, l, gate0 + nn: gate0 + nn + 1, ci],
                    in1=X[:, nn, lo:hi], op0=ALU.mult, op1=ALU.add),
                    reads=pskeys(b) + gran("x", nn, lo, hi) + [("mod", l)], writes=gran("x", nn, lo, hi))

            def as_list(x):
                return x if isinstance(x, list) else [x]

            for nn in range(NCH - 2):
                slots = as_list(getw(*wspec(nn)))
                for (lo, hi) in tiles:
                    one(nn, slots, lo, hi)
            sp6 = wspec(NCH - 2)
            sp7 = wspec(NCH - 1)
            allslots = as_list(getw(*(tuple(sp6) + tuple(sp7))))
            sl6 = allslots[:len(sp6)]
            sl7 = allslots[len(sp6):]
            hooked = 0
            for k_, (lo, hi) in enumerate(tiles):
                one(NCH - 2, sl6, lo, hi)
                if hooked == 1:
                    pre_norm_b(nt0)
                    hooked = 2
                one(NCH - 1, sl7, lo, hi)
                if (nt0 is not None and hooked == 0 and k_ + 1 < len(tiles)
                        and tiles[0][0] <= nt0[0] and nt0[1] <= hi):
                    pre_norm_a(nt0)
                    hooked = 1

        def ffn_weights(l):
            ws = [(wfin_d[l, m], 1024) for m in range(44)]
            ws += [(wfout_d[l, i], 1408) for i in range(16)]
            return ws

        def evac_a(g, b, par, lo, hi):
            n = hi - lo
            if not g.lat:
                P.op("act", lambda e: e.activation(out=as_view(g, par, lo, hi, 0), in_=v3(PS[:, b * 512: b * 512 + n], g, n),
                                                   func=AF.Copy),
                     reads=pskeys(b), writes=as_keys(par, lo, hi))
                return
            pieces = []
            for (a0, a1, fl) in ((0, 256, 0), (256, 768, None), (768, LAT_COLS, 1)):
                s0, s1 = max(a0, lo), min(a1, hi)
                if s0 < s1:
                    pieces.append((s0, s1, fl))
            for (s0, s1, fl) in pieces:
                if fl is None:
                    P.op("act", lambda e, s0=s0, s1=s1: e.activation(
                        out=AS[:, par, 1 + s0: 1 + s1], in_=PS[:, b * 512 + s0 - lo: b * 512 + s1 - lo], func=AF.Copy),
                        reads=pskeys(b), writes=as_keys(par, s0, s1))
                else:
                    P.op("act", lambda e, s0=s0, s1=s1, fl=fl: e.activation(
                        out=AS[:, par, 1 + s0: 1 + s1], in_=PS[:, b * 512 + s0 - lo: b * 512 + s1 - lo], func=AF.Copy,
                        scale=FLAGS[:, fl:fl + 1]),
                        reads=pskeys(b) + ["flags"], writes=as_keys(par, s0, s1))

        def phase_ffn(g, l, nt0=None):
            ci = g.ci
            tiles = even_tiles(*g.win_f[l])
            if not g.lat and l + 1 < DEPTH:
                ada_enqueue(l + 1)
            norm_mod(g, tiles, G2[:, l, ci, :], (lambda c: MOD[:, l, 24 + c: 25 + c, ci]),
                     [("g2", l), ("mod", l)], h_out, h_keys)
            pending_gelu = []

            def flush_gelu():
                for (t3, lo, hi, n, jj) in pending_gelu:
                    P.op("act", lambda e, t3=t3, lo=lo, hi=hi, n=n, jj=jj: e.activation(
                        out=HID[:, jj, lo:hi], in_=TMPR[:, t3, 0:n], func=AF.Gelu_apprx_tanh),
                        reads=[("tmp", t3)], writes=gran("hid", jj, lo, hi))
                del pending_gelu[:]

            for j in range(NJ):
                s = getw((wfin_d[l, j], 1024))
                par = j % 2
                c1slots = []
                for (lo, hi) in tiles:
                    n = hi - lo
                    b = ps_alloc()
                    fns = [(lambda e, kc=kc, b=b, lo=lo, hi=hi, n=n, s=s: e.matmul(
                        PS[:, b * 512: b * 512 + n], WSL[:, s, kc * 128:(kc + 1) * 128], H[:, kc, lo:hi],
                        start=(kc == 0), stop=(kc == 7))) for kc in range(8)]
                    rd = [("w", s)]
                    for kc in range(8):
                        rd += gran("h", kc, lo, hi)
                    P.mm(fns, reads=rd, writes=pskeys(b))
                    evac_a(g, b, par, lo, hi)
                    t1 = nxt("tmp", NTMP)
                    c1slots.append(t1)
                    P.op("act", lambda e, t1=t1, b=b, n=n, j=j: e.activation(
                        out=TMPR[:, t1, 0:n], in_=PS[:, b * 512: b * 512 + n], func=AF.Identity,
                        scale=CW[:, l, 1, j:j + 1], bias=CB[:, l, j:j + 1]),
                        reads=pskeys(b) + ["cw", "cb"], writes=[("tmp", t1)])
                flush_gelu()
                for ti, (lo, hi) in enumerate(tiles):
                    n = hi - lo
                    t3 = c1slots[ti]
                    P.op("dve", lambda e, t3=t3, lo=lo, hi=hi, n=n, j=j, par=par: e.scalar_tensor_tensor(
                        out=v3(TMPR[:, t3, 0:n], g, n), in0=as_view(g, par, lo, hi, -1), scalar=CW[:, l, 0, j:j + 1],
                        in1=v3(TMPR[:, t3, 0:n], g, n), op0=ALU.mult, op1=ALU.add),
                        reads=as_keys(par, lo - 1, hi) + [("tmp", t3), "cw"], writes=[("tmp", t3)])
                    P.op("dve", lambda e, t3=t3, lo=lo, hi=hi, n=n, j=j, par=par: e.scalar_tensor_tensor(
                        out=v3(TMPR[:, t3, 0:n], g, n), in0=as_view(g, par, lo, hi, 1), scalar=CW[:, l, 2, j:j + 1],
                        in1=v3(TMPR[:, t3, 0:n], g, n), op0=ALU.mult, op1=ALU.add),
                        reads=as_keys(par, lo, hi + 1) + [("tmp", t3), "cw"], writes=[("tmp", t3)])
                    pending_gelu.append((t3, lo, hi, n, j))
                ada_step()
            flush_gelu()
            for j in range(NJ):
                s = getw((wfin_d[l, 22 + j], 1024))
                for (lo, hi) in tiles:
                    n = hi - lo
                    b = ps_alloc()
                    fns = [(lambda e, kc=kc, b=b, lo=lo, hi=hi, n=n, s=s: e.matmul(
                        PS[:, b * 512: b * 512 + n], WSL[:, s, kc * 128:(kc + 1) * 128], H[:, kc, lo:hi],
                        start=(kc == 0), stop=(kc == 7))) for kc in range(8)]
                    rd = [("w", s)]
                    for kc in range(8):
                        rd += gran("h", kc, lo, hi)
                    P.mm(fns, reads=rd, writes=pskeys(b))
                    P.op("dve", lambda e, b=b, lo=lo, hi=hi, n=n, j=j: e.tensor_tensor(
                        out=HID[:, j, lo:hi], in0=HID[:, j, lo:hi], in1=PS[:, b * 512: b * 512 + n], op=ALU.mult),
                        reads=gran("hid", j, lo, hi) + pskeys(b), writes=gran("hid", j, lo, hi))
                ada_step()
            out_proj(tiles, (lambda nn: ((wfout_d[l, 2 * nn], 1408), (wfout_d[l, 2 * nn + 1], 1408))), NJ, 11,
                     (lambda kk, lo, hi: HID[:, kk, lo:hi]), (lambda kk, lo, hi: gran("hid", kk, lo, hi)), l, ci, 40,
                     nt0=nt0)
            ada_drain()

        def gmlp_weights(j):
            ws = [(awu_d[j, m], 1024) for m in range(16)]
            ws += [(awo_d[j, nn], 2048) for nn in range(NCH)]
            return ws

        def load_wv(j):
            for cb in range(4):
                P.dma("pool", (lambda e, cb=cb: e.dma_start(out=WV[:, cb * 4096:(cb + 1) * 4096], in_=awv_d[j, cb])),
                      reads=[], writes=["scr"])

        def phase_gmlp(g, l, nt0=None):
            ci = g.ci
            j = l // 3
            tiles = even_tiles(*g.win_m[l])
            norm_mod(g, tiles, G1[:, l, ci, :], (lambda c: MOD[:, l, c: c + 1, ci]),
                     [("g1", l), ("mod1", l)], h_out, h_keys)
            U = HID
            for m in range(16):
                s = getw((awu_d[j, m], 1024))
                for (lo, hi) in tiles:
                    n = hi - lo
                    b = ps_alloc()
                    fns = [(lambda e, kc=kc, b=b, lo=lo, hi=hi, n=n, s=s: e.matmul(
                        PS[:, b * 512: b * 512 + n], WSL[:, s, kc * 128:(kc + 1) * 128], H[:, kc, lo:hi],
                        start=(kc == 0), stop=(kc == 7))) for kc in range(8)]
                    rd = [("w", s)]
                    for kc in range(8):
                        rd += gran("h", kc, lo, hi)
                    P.mm(fns, reads=rd, writes=pskeys(b))
                    P.op("act", lambda e, b=b, lo=lo, hi=hi, n=n, m=m: e.activation(
                        out=U[:, m, lo:hi], in_=PS[:, b * 512: b * 512 + n], func=AF.Gelu_apprx_tanh),
                        reads=pskeys(b), writes=gran("hid", m, lo, hi))
                ada_step()
            ada_drain()
            HF = HID[:, 16:22, :].rearrange("p a b -> p (a b)")
            VGb = [HF[:, 0:2048], HF[:, 2048:4096]]
            WSQ = [HF[:, 4096:5120], HF[:, 5120:6144]]
            JUNK = TMPR[:, 0:2, :].rearrange("p a b -> p (a b)").bitcast(BF16)
            chunks = list(range(g.win_m[l][0] // 128, g.win_m[l][1] // 128))

            def emit_v(q):
                pp = q % 2
                c0 = q * 128
                b4 = ps_alloc(4)
                for cb in range(4):
                    fns = [(lambda e, kc=kc, cb=cb, b4=b4, c0=c0: e.matmul(
                        PS[:, (b4 + cb) * 512:(b4 + cb + 1) * 512], H[:, kc, c0:c0 + 128],
                        WV[:, cb * 4096 + kc * 512: cb * 4096 + (kc + 1) * 512],
                        start=(kc == 0), stop=(kc == 7))) for kc in range(8)]
                    rd = ["scr"]
                    for kc in range(8):
                        rd += gran("h", kc, c0, c0 + 128)
                    P.mm(fns, reads=rd, writes=pskeys(b4 + cb))
                P.op("act", lambda e, b4=b4, pp=pp: e.activation(out=VGb[pp], in_=PS[:, b4 * 512:(b4 + 4) * 512],
                                                                 func=AF.Gelu_apprx_tanh),
                     reads=pskeys(b4, 4), writes=[("vgb", pp)])
                P.op("act", lambda e, pp=pp: e.activation(out=JUNK, in_=VGb[pp], func=AF.Square,
                                                          accum_out=SS[:, pp * 4: pp * 4 + 1]),
                     reads=[("vgb", pp)], writes=[("tmp", 0), ("tmp", 1), ("ss", pp)])
                P.op("act", lambda e, pp=pp: e.activation(out=SS[:, pp * 4 + 1: pp * 4 + 2], in_=SS[:, pp * 4: pp * 4 + 1],
                                                          func=AF.Sqrt, scale=1.0 / 2048.0, bias=EPST[:, 0:1]),
                     reads=[("ss", pp), "epst"], writes=[("ss1", pp)])
                P.op("dve", lambda e, pp=pp: e.reciprocal(out=SS[:, pp * 4 + 2: pp * 4 + 3], in_=SS[:, pp * 4 + 1: pp * 4 + 2]),
                     reads=[("ss1", pp)], writes=[("ss2", pp)])
                P.op("dve", lambda e, pp=pp: e.tensor_scalar(out=WSQ[pp], in0=WST[:, j, :], scalar1=SS[:, pp * 4 + 2: pp * 4 + 3],
                                                             scalar2=None, op0=ALU.mult),
                     reads=[("ss2", pp), "wst"], writes=[("wsq", pp)])

            def emit_spatial(q):
                pp = q % 2
                c0 = q * 128
                for bq in range(4):
                    b = ps_alloc()
                    fns = []
                    for i in range(4):
                        fc = bq * 4 + i
                        fns.append(lambda e, fc=fc, i=i, b=b, pp=pp: e.matmul(
                            PS[:, b * 512 + i * 128: b * 512 + (i + 1) * 128], VGb[pp][:, fc * 128:(fc + 1) * 128],
                            WSQ[pp][:, (fc // 2) * 128:(fc // 2 + 1) * 128], start=True, stop=True))
                    P.mm(fns, reads=[("vgb", pp), ("wsq", pp)], writes=pskeys(b))
                    t = 4 + (bq % 2)
                    for i in range(4):
                        fc = bq * 4 + i
                        P.op("dve", lambda e, fc=fc, i=i, b=b, t=t: e.scalar_tensor_tensor(
                            out=TMPR[:, t, i * 128:(i + 1) * 128], in0=PS[:, b * 512 + i * 128: b * 512 + (i + 1) * 128],
                            scalar=AGV[:, j, fc:fc + 1], in1=ABS[:, j, fc // 2, :], op0=ALU.mult, op1=ALU.add),
                            reads=pskeys(b) + ["agv", "abs"], writes=[("tmp", t)])
                    P.op("dve", lambda e, bq=bq, t=t, c0=c0: e.tensor_tensor(
                        out=U[:, bq * 4:(bq + 1) * 4, c0:c0 + 128], in0=U[:, bq * 4:(bq + 1) * 4, c0:c0 + 128],
                        in1=TMPR[:, t, :].rearrange("p (a b) -> p a b", b=128), op=ALU.mult),
                        reads=[("tmp", t)] + [k for fc in range(bq * 4, bq * 4 + 4) for k in gran("hid", fc, c0, c0 + 128)],
                        writes=[k for fc in range(bq * 4, bq * 4 + 4) for k in gran("hid", fc, c0, c0 + 128)])

            emit_v(chunks[0])
            for qi, q in enumerate(chunks):
                if qi + 1 < len(chunks):
                    emit_v(chunks[qi + 1])
                emit_spatial(q)
            ring_state["tmp"] = 0
            out_proj(tiles, (lambda nn: ((awo_d[j, nn], 2048),)), 16, 16,
                     (lambda kk, lo, hi: U[:, kk, lo:hi]), (lambda kk, lo, hi: gran("hid", kk, lo, hi)), l, ci, 16,
                     nt0=nt0)

        PADW = 16

        def pool_weights():
            return [(pw_d[:, :], 2048)]

        def phase_pool(g, l):
            ci = g.ci
            tiles = even_tiles(*g.win_m[l])
            L = g.L
            Lp = L + 2 * PADW
            nseg = g.nseg
            H32f = H32.rearrange("p a b -> p (a b)")
            W = nseg * Lp

            def hb(c):
                return H32f[:, c * W:(c + 1) * W].rearrange("p (s l) -> p s l", l=Lp)

            SA = H32f[:, 8 * W: 9 * W].rearrange("p (s l) -> p s l", l=Lp)
            SB = H32f[:, 9 * W: 10 * W].rearrange("p (s l) -> p s l", l=Lp)
            wlo, whi = g.win_m[l]
            hall = H32f[:, 0:8 * W].rearrange("p (s l) -> p s l", l=Lp)
            zl = PADW + (wlo if nseg == 1 else 0)
            zr = PADW + (whi if nseg == 1 else L)
            hidk = [k for jj in range(NJ) for k in gran("hid", jj, 0, g.cols)]
            P.op("pool", lambda e: e.memset(hall[:, :, 0:zl], 0.0), reads=[], writes=["h32all"] + hidk)
            P.op("pool", lambda e: e.memset(hall[:, :, zr:Lp], 0.0), reads=[], writes=["h32all"] + hidk)
            P.op("pool", lambda e: e.memset(SA[:, :, 0:1], 0.0), reads=[], writes=[("sab", 0)])

            def h32_out(c, lo, hi):
                if nseg > 1:
                    return hb(c)[:, lo // L: hi // L, PADW:PADW + L]
                return hb(c)[:, 0, PADW + lo:PADW + hi]

            def tmp3(ap, n):
                return v3(ap, g, n)

            def nm_out(c, lo, hi):
                return h32_out(c, lo, hi)

            def nm_in_fix(fnlam):
                return fnlam

            norm_mod(g, tiles, G1[:, l, ci, :], (lambda c: MOD[:, l, c:c + 1, ci]),
                     [("g1", l), ("mod1", l), "h32all", "g1f"], h32_out, (lambda c, lo, hi: [("h32", c)]), tmp_view=tmp3,
                     flagged=((G1F, SH1F) if g.lat else None))
            s = getw((pw_d[:, :], 2048))
            IC = ICL if g.lat else ICC
            for c in range(NCH):
                gi = c // 2
                w = (2, 4, 8, 16)[gi]
                src = hb(c)
                bufs = [SA, SB]
                cur = src
                cur_key = ("h32", c)
                r0 = 1
                r1 = Lp
                steps = [(1, 0), (1, 1), (2, 2), (4, 4)][: gi + 1]
                for si, (sl, sr) in enumerate(steps):
                    dst = bufs[si % 2]
                    dkey = ("sab", si % 2)
                    a = r0 + sl if si > 0 else 1
                    bnd = (r1 - sr) if si > 0 else Lp
                    if si == 0:
                        a, bnd = 1, Lp
                    P.op("dve", lambda e, cur=cur, dst=dst, a=a, bnd=bnd, sl=sl, sr=sr: e.tensor_tensor(
                        out=dst[:, :, a:bnd], in0=cur[:, :, a - sl:bnd - sl], in1=cur[:, :, a + sr:bnd + sr], op=ALU.add),
                        reads=[cur_key], writes=[dkey])
                    cur = dst
                    cur_key = dkey
                    r0, r1 = a, bnd
                for (lo, hi) in tiles:
                    n = hi - lo
                    if nseg > 1:
                        sv = cur[:, lo // L: hi // L, PADW:PADW + L]
                    else:
                        sv = cur[:, 0, PADW + lo:PADW + hi]
                    P.op("dve", lambda e, sv=sv, c=c, lo=lo, hi=hi, n=n, w=w: e.scalar_tensor_tensor(
                        out=v3(H[:, c, lo:hi], g, n), in0=sv, scalar=1.0 / w, in1=h32_out(c, lo, hi),
                        op0=ALU.mult, op1=ALU.subtract),
                        reads=[cur_key, ("h32", c)], writes=gran("h", c, lo, hi))
                if nseg > 1:
                    locs = [(0, 0), (L - 16, 1)]
                    for (off, wh) in locs:
                        t = nxt("tmp", NTMP)
                        P.op("dve", lambda e, cur=cur, off=off, wh=wh, t=t, gi=gi: e.tensor_tensor(
                            out=TMPR[:, t, 0:nseg * 16].rearrange("p (s l) -> p s l", l=16),
                            in0=cur[:, :, PADW + off:PADW + off + 16],
                            in1=IC[:, wh, gi, :].unsqueeze(1).broadcast_to([128, nseg, 16]), op=ALU.mult),
                            reads=[cur_key, "icc"], writes=[("tmp", t)])
                        P.op("dve", lambda e, off=off, t=t, c=c: e.tensor_tensor(
                            out=H[:, c, 0:g.cols].rearrange("p (s l) -> p s l", l=L)[:, :, off:off + 16],
                            in0=TMPR[:, t, 0:nseg * 16].rearrange("p (s l) -> p s l", l=16),
                            in1=hb(c)[:, :, PADW + off:PADW + off + 16], op=ALU.subtract),
                            reads=[("tmp", t), ("h32", c)], writes=gran("h", c, 0, g.cols))
                else:
                    for (off, wh) in ((256, 0), (752, 1)):
                        t = nxt("tmp", NTMP)
                        P.op("dve", lambda e, cur=cur, off=off, wh=wh, t=t, gi=gi: e.tensor_tensor(
                            out=TMPR[:, t, 0:16], in0=cur[:, 0, PADW + off:PADW + off + 16],
                            in1=IC[:, wh, gi, :], op=ALU.mult),
                            reads=[cur_key, "icl"], writes=[("tmp", t)])
                        P.op("dve", lambda e, off=off, t=t, c=c: e.tensor_tensor(
                            out=H[:, c, off:off + 16], in0=TMPR[:, t, 0:16],
                            in1=hb(c)[:, 0, PADW + off:PADW + off + 16], op=ALU.subtract),
                            reads=[("tmp", t), ("h32", c)], writes=gran("h", c, off, off + 16))
            for oc8 in range(NCH):
                gi = oc8 // 2
                for (lo, hi) in tiles:
                    n = hi - lo
                    b = ps_alloc()
                    fns = [(lambda e, ic=ic, b=b, lo=lo, hi=hi, n=n, oc8=oc8, gi=gi: e.matmul(
                        PS[:, b * 512: b * 512 + n], WSL[:, s, (oc8 * 2 + ic) * 128:(oc8 * 2 + ic + 1) * 128],
                        H[:, gi * 2 + ic, lo:hi], start=(ic == 0), stop=(ic == 1))) for ic in range(2)]
                    rd = [("w", s)] + gran("h", gi * 2, lo, hi) + gran("h", gi * 2 + 1, lo, hi)
                    P.mm(fns, reads=rd, writes=pskeys(b))
                    t = nxt("tmp", NTMP)
                    P.op("dve", lambda e, b=b, t=t, n=n, oc8=oc8: e.tensor_scalar(
                        out=TMPR[:, t, 0:n], in0=PS[:, b * 512: b * 512 + n], scalar1=PA[:, ci, oc8:oc8 + 1],
                        scalar2=PBB[:, ci, oc8:oc8 + 1], op0=ALU.mult, op1=ALU.add),
                        reads=pskeys(b) + ["pa", "pbb"], writes=[("tmp", t)])
                    P.op("dve", lambda e, t=t, lo=lo, hi=hi, n=n, oc8=oc8: e.tensor_tensor(
                        out=X[:, oc8, lo:hi], in0=X[:, oc8, lo:hi], in1=TMPR[:, t, 0:n], op=ALU.add),
                        reads=[("tmp", t)] + gran("x", oc8, lo, hi), writes=gran("x", oc8, lo, hi))

        def attn_weights(g):
            ws = [(wq_d[c], 1024) for c in range(8)]
            ws += [(wk_d[c], 1024) for c in range(4)]
            ws += [(wvt_d[:, :], 2048)]
            if not g.lat:
                ws += [(wvf_d[c], 1024) for c in range(2)]
            ws += [(wo_d[c], 1024) for c in range(8)]
            return ws

        def phase_attn(g, l, nt0=None):
            ci = g.ci
            qtiles = even_tiles(*g.qwin)
            norm_mod(g, g.tiles, G1[:, l, ci, :], (lambda c: MOD[:, l, c: c + 1, ci]),
                     [("g1", l), ("mod1", l)], h_out, h_keys)
            QT = HID
            KT0 = 8
            O0 = 12
            nkt_lat = g.cols // 128
            VE3 = VE.rearrange("p (t k d) -> p t k d", k=4, d=192)
            P.op("pool", lambda e: e.memset(VE, 1.0), reads=[], writes=["ve", "scr"])
            if g.lat:
                P.dma("sp", lambda e: e.dma_start(out=COS, in_=cos_d), reads=["scr"], writes=["cos"])
                P.dma("sp", lambda e: e.dma_start(out=SIN, in_=sin_d), reads=["scr"], writes=["sin"])
                P.dma("pool", lambda e: e.dma_start(out=HID[:, 20, 0:1024], in_=ck_d.rearrange("p a b -> p (a b)")),
                      reads=[], writes=["ckt"] + gran("hid", 20, 0, 1024))
                for kt in range(2):
                    P.dma("pool", lambda e, kt=kt: e.dma_start(
                        out=VE3[:, 9 + kt, :, 64:128], in_=cv_d[:, kt, :].rearrange("p (k d) -> p k d", d=64)),
                        reads=["ve", "scr"], writes=[("vet", 9 + kt)])

            qk_units = [("q", c, ti, lo, hi) for c in range(8) for ti, (lo, hi) in enumerate(qtiles)]
            qk_units += [("k", c, ti, lo, hi) for c in range(4) for ti, (lo, hi) in enumerate(g.tiles)]
            qst = {}
            qslot = {}

            def qk_A(u):
                kind, c, ti, lo, hi = qk_units[u]
                n = hi - lo
                if ti == 0:
                    qslot[(kind, c)] = getw(((wq_d if kind == "q" else wk_d)[c], 1024))
                s = qslot[(kind, c)]
                b = ps_alloc()
                fns = [(lambda e, kc=kc, b=b, lo=lo, hi=hi, n=n, s=s: e.matmul(
                    PS[:, b * 512: b * 512 + n], WSL[:, s, kc * 128:(kc + 1) * 128], H[:, kc, lo:hi],
                    start=(kc == 0), stop=(kc == 7))) for kc in range(8)]
                rd = [("w", s)]
                for kc in range(8):
                    rd += gran("h", kc, lo, hi)
                P.mm(fns, reads=rd, writes=pskeys(b))
                sq = nxt("sq", NSQ)
                P.op("act", lambda e, sq=sq, b=b, n=n: e.activation(out=SQ[:, sq, 0:n], in_=PS[:, b * 512: b * 512 + n],
                                                                    func=AF.Square),
                     reads=pskeys(b), writes=[("sq", sq)])
                qst[u] = {"b": b, "sq": sq}

            def qk_B(u):
                kind, c, ti, lo, hi = qk_units[u]
                n = hi - lo
                dst = c if kind == "q" else KT0 + c
                gsc = GQ if kind == "q" else GK
                b = qst[u]["b"]
                sq = qst[u]["sq"]
                b2 = ps_alloc()
                P.mm([lambda e, sq=sq, b2=b2, n=n: e.matmul(PS[:, b2 * 512: b2 * 512 + n], BLK[:, :], SQ[:, sq, 0:n],
                                                            start=True, stop=True)],
                     reads=[("sq", sq), "blk"], writes=pskeys(b2))
                r = nxt("rs", NRS)
                P.op("act", lambda e, r=r, b2=b2, n=n: e.activation(
                    out=RS[:, r, 0:n], in_=PS[:, b2 * 512: b2 * 512 + n], func=AF.Ln, scale=1.0 / 64.0,
                    bias=EPST[:, 0:1]), reads=pskeys(b2) + ["epst"], writes=[("rs", r)])
                P.op("act", lambda e, r=r, n=n: e.activation(out=RS[:, r, 0:n], in_=RS[:, r, 0:n], func=AF.Exp,
                                                             scale=-0.5), reads=[("rs", r)], writes=[("rs", r)])
                if not g.lat and kind == "q":
                    P.op("dve", lambda e, b=b, r=r, lo=lo, hi=hi, n=n, dst=dst, gsc=gsc: e.scalar_tensor_tensor(
                        out=HID[:, dst, lo:hi], in0=PS[:, b * 512: b * 512 + n], scalar=gsc[:, 0:1],
                        in1=RS[:, r, 0:n], op0=ALU.mult, op1=ALU.mult),
                        reads=pskeys(b) + [("rs", r), "gq"], writes=gran("hid", dst, lo, hi))
                    return
                t = nxt("tmp", NTMP)
                P.op("dve", lambda e, b=b, r=r, t=t, n=n, gsc=gsc: e.scalar_tensor_tensor(
                    out=TMPR[:, t, 0:n], in0=PS[:, b * 512: b * 512 + n], scalar=gsc[:, 0:1],
                    in1=RS[:, r, 0:n], op0=ALU.mult, op1=ALU.mult),
                    reads=pskeys(b) + [("rs", r), "gq", "gk"], writes=[("tmp", t)])
                if not g.lat:
                    P.op("act", lambda e, t=t, lo=lo, hi=hi, n=n, dst=dst: e.activation(
                        out=HID[:, dst, lo:hi], in_=TMPR[:, t, 0:n], func=AF.Copy),
                        reads=[("tmp", t)], writes=gran("hid", dst, lo, hi))
                    P.dma("sp", lambda e, t=t, c=c, lo=lo, hi=hi, n=n: e.dma_start(
                        out=nk_d[c * 64:(c + 1) * 64, lo:hi], in_=TMPR[0:64, t, 0:n]),
                        reads=[("tmp", t)], writes=[])
                    return
                sq2 = nxt("sq", NSQ)
                P.op("act", lambda e, t=t, sq2=sq2, n=n: e.activation(out=SQ[:, sq2, 0:n], in_=TMPR[:, t, 0:n],
                                                                      func=AF.Copy),
                     reads=[("tmp", t)], writes=[("sq", sq2)])
                qst[u]["t"] = t
                qst[u]["sq2"] = sq2

            def qk_C(u):
                kind, c, ti, lo, hi = qk_units[u]
                n = hi - lo
                dst = c if kind == "q" else KT0 + c
                t = qst[u]["t"]
                sq2 = qst[u]["sq2"]
                b3 = ps_alloc()
                P.mm([lambda e, sq2=sq2, b3=b3, n=n: e.matmul(PS[:, b3 * 512: b3 * 512 + n], PM[:, :],
                                                              SQ[:, sq2, 0:n], start=True, stop=True)],
                     reads=[("sq", sq2), "pm"], writes=pskeys(b3))
                t2 = nxt("tmp", NTMP)
                P.op("dve", lambda e, b3=b3, t2=t2, lo=lo, hi=hi, n=n: e.tensor_tensor(
                    out=TMPR[:, t2, 0:n], in0=PS[:, b3 * 512: b3 * 512 + n], in1=SIN[:, lo:hi], op=ALU.mult),
                    reads=pskeys(b3) + ["sin", "scr"], writes=[("tmp", t2)])
                P.op("pool", lambda e, t=t, lo=lo, hi=hi, n=n: e.tensor_tensor(
                    out=TMPR[:, t, 0:n], in0=TMPR[:, t, 0:n], in1=COS[:, lo:hi], op=ALU.mult),
                    reads=[("tmp", t), "cos", "scr"], writes=[("tmp", t)])
                P.op("dve", lambda e, t=t, t2=t2, lo=lo, hi=hi, n=n, dst=dst: e.tensor_tensor(
                    out=HID[:, dst, lo:hi], in0=TMPR[:, t, 0:n], in1=TMPR[:, t2, 0:n], op=ALU.add),
                    reads=[("tmp", t), ("tmp", t2)], writes=gran("hid", dst, lo, hi))

            nu = len(qk_units)
            for i in range(nu + 2):
                if i < nu:
                    qk_A(i)
                if 0 <= i - 1 < nu:
                    qk_B(i - 1)
                if g.lat and 0 <= i - 2 < nu:
                    qk_C(i - 2)
            s = getw((wvt_d[:, :], 2048))
            for q in range(nkt_lat):
                c0 = q * 128
                b = ps_alloc()
                fns = [(lambda e, kc=kc, b=b, c0=c0, s=s: e.matmul(
                    PS[:, b * 512: b * 512 + 256], H[:, kc, c0:c0 + 128], WSL[:, s, kc * 256:(kc + 1) * 256],
                    start=(kc == 0), stop=(kc == 7))) for kc in range(8)]
                rd = [("w", s)]
                for kc in range(8):
                    rd += gran("h", kc, c0, c0 + 128)
                P.mm(fns, reads=rd, writes=pskeys(b))
                P.op("act", lambda e, b=b, q=q: e.activation(
                    out=VE3[:, q, :, 64:128], in_=PS[:, b * 512: b * 512 + 256].rearrange("p (k d) -> p k d", d=64),
                    func=AF.Copy), reads=pskeys(b) + ["ve", "scr"], writes=[("vet", q)])
            if not g.lat:
                for c in range(2):
                    s = getw((wvf_d[c], 1024))
                    for (lo, hi) in g.tiles:
                        n = hi - lo
                        b = ps_alloc()
                        fns = [(lambda e, kc=kc, b=b, lo=lo, hi=hi, n=n, s=s: e.matmul(
                            PS[:, b * 512: b * 512 + n], WSL[:, s, kc * 128:(kc + 1) * 128], H[:, kc, lo:hi],
                            start=(kc == 0), stop=(kc == 7))) for kc in range(8)]
                        rd = [("w", s)]
                        for kc in range(8):
                            rd += gran("h", kc, lo, hi)
                        P.mm(fns, reads=rd, writes=pskeys(b))
                        t = nxt("tmp", NTMP)
                        P.op("act", lambda e, b=b, t=t, n=n: e.activation(out=TMPR[:, t, 0:n], in_=PS[:, b * 512: b * 512 + n],
                                                                          func=AF.Copy),
                             reads=pskeys(b), writes=[("tmp", t)])
                        P.dma("sp", lambda e, t=t, c=c, lo=lo, hi=hi, n=n: e.dma_start(
                            out=nv_d[c * 128:(c + 1) * 128, lo:hi], in_=TMPR[:, t, 0:n]),
                            reads=[("tmp", t)], writes=[])

            PT2 = PT.rearrange("p (s c) -> p s c", s=2)
            if DEBUG_STOP is not None and DEBUG_STOP[1] == "attn_ve":
                Xf = X[:, :, :].rearrange("p a b -> p (a b)")
                P.op("dve", lambda e: e.tensor_copy(out=Xf[:, 0:8448], in_=VE),
                     reads=[("vet", t_) for t_ in range(11)] + ["ve"], writes=[k for c in range(8) for k in gran("x", c, 0, g.cols)])
                return

            def ve_lhs(tile, kv, h):
                if h % 2 == 0:
                    return VE3[:, tile, kv, 64:192]
                return VE3[:, tile, kv, 0:128]

            def normalize(b, h, qlo, nq):
                off = (h % 2) * 64
                dn = 64 - off
                t = nxt("tmp", NTMP)
                P.op("dve", lambda e, b=b, t=t, nq=nq, h=h, off=off, dn=dn: e.tensor_scalar(
                    out=TMPR[off:off + 64, t, 0:nq], in0=PS[dn:dn + 64, b * 512: b * 512 + nq],
                    scalar1=ESK[dn:dn + 64, h:h + 1], scalar2=None, op0=ALU.add),
                    reads=pskeys(b) + ["esk"], writes=[("tmp", t)])
                P.op("act", lambda e, t=t, nq=nq, off=off: e.activation(
                    out=TMPR[off:off + 64, t, 0:nq], in_=TMPR[off:off + 64, t, 0:nq], func=AF.Ln),
                    reads=[("tmp", t)], writes=[("tmp", t)])
                P.op("act", lambda e, t=t, nq=nq, off=off: e.activation(
                    out=TMPR[off:off + 64, t, 0:nq], in_=TMPR[off:off + 64, t, 0:nq], func=AF.Exp, scale=-1.0),
                    reads=[("tmp", t)], writes=[("tmp", t)])
                P.op("dve", lambda e, b=b, t=t, nq=nq, h=h, off=off, qlo=qlo: e.tensor_tensor(
                    out=HID[off:off + 64, O0 + h // 2, qlo:qlo + nq], in0=PS[off:off + 64, b * 512: b * 512 + nq],
                    in1=TMPR[off:off + 64, t, 0:nq], op=ALU.mult),
                    reads=pskeys(b) + [("tmp", t)], writes=gran("hid", O0 + h // 2, qlo, qlo + nq))

            if not g.lat:
                units = [(sq_i, h) for sq_i in range(4) for h in range(16)]
                st = {}

                def c_s1(u):
                    sq_i, h = units[u]
                    base = sq_i * 256
                    kv = h // 4
                    off = (h % 2) * 64
                    b = ps_alloc()
                    fns = [(lambda e, kt=kt, b=b, off=off, kv=kv, h=h, base=base: e.matmul(
                        PS[:, b * 512 + kt * 256: b * 512 + (kt + 1) * 256],
                        HID[off:off + 64, KT0 + kv, base + kt * 128: base + (kt + 1) * 128],
                        HID[off:off + 64, h // 2, base:base + 256], start=True, stop=True)) for kt in range(2)]
                    rd = gran("hid", KT0 + kv, base, base + 256) + gran("hid", h // 2, base, base + 256)
                    P.mm(fns, reads=rd, writes=pskeys(b))
                    ps_ = nxt("pt", 2)
                    P.op("act", lambda e, b=b, ps_=ps_: e.activation(
                        out=PT2[:, ps_, 0:512], in_=PS[:, b * 512:(b + 1) * 512], func=AF.Exp, scale=0.125),
                        reads=pskeys(b) + ["scr"], writes=[("pt", ps_, 0), "cos", "sin"])
                    st[u] = ps_

                def c_s2(u):
                    sq_i, h = units[u]
                    base = sq_i * 256
                    kv = h // 4
                    ps_ = st.pop(u)
                    b2 = ps_alloc()
                    fns = [(lambda e, kt=kt, b2=b2, kv=kv, ps_=ps_, sq_i=sq_i, h=h: e.matmul(
                        PS[:, b2 * 512: b2 * 512 + 256], ve_lhs(sq_i * 2 + kt, kv, h),
                        PT2[:, ps_, kt * 256:(kt + 1) * 256], start=(kt == 0), stop=(kt == 1))) for kt in range(2)]
                    P.mm(fns, reads=[("pt", ps_, 0), ("vet", sq_i * 2), ("vet", sq_i * 2 + 1), "scr"], writes=pskeys(b2))
                    normalize(b2, h, base, 256)

                c_s1(0)
                for u in range(len(units)):
                    if u + 1 < len(units):
                        c_s1(u + 1)
                    c_s2(u)
            else:
                CKT = HID[:, 20, 0:1024].rearrange("p (k t) -> p k t", t=256)
                units = [(n0, nbk, h) for (n0, nbk) in ((1, 4), (5, 3)) for h in range(16)]
                st = {}

                def l_s1a(u):
                    n0, nbk, h = units[u]
                    qlo = n0 * 128
                    nq = nbk * 128
                    kv = h // 4
                    off = (h % 2) * 64
                    ps_ = nxt("pt", 2)
                    st[u] = ps_
                    for rel in range(3):
                        b = ps_alloc()
                        fns = []
                        for i in range(nbk):
                            kt = n0 + i - 1 + rel
                            fns.append(lambda e, i=i, kt=kt, b=b, off=off, kv=kv, h=h, n0=n0: e.matmul(
                                PS[:, b * 512 + i * 128: b * 512 + (i + 1) * 128],
                                HID[off:off + 64, KT0 + kv, kt * 128:(kt + 1) * 128],
                                HID[off:off + 64, h // 2, (n0 + i) * 128:(n0 + i + 1) * 128], start=True, stop=True))
                        rd = gran("hid", KT0 + kv, (n0 - 1 + rel) * 128, (n0 + nbk - 1 + rel) * 128) + \
                            gran("hid", h // 2, qlo, qlo + nq)
                        P.mm(fns, reads=rd, writes=pskeys(b))
                        runs = []
                        for i in range(nbk):
                            kt = n0 + i - 1 + rel
                            fl = 0 if kt < 2 else (1 if kt >= 6 else None)
                            if runs and runs[-1][2] == fl:
                                runs[-1][1] = i + 1
                            else:
                                runs.append([i, i + 1, fl])
                        for (i0, i1, fl) in runs:
                            bias = ZB[:, 0:1] if fl is None else NEGB[:, fl:fl + 1]
                            P.op("act", lambda e, b=b, i0=i0, i1=i1, rel=rel, ps_=ps_, bias=bias: e.activation(
                                out=PT2[:, ps_, rel * 512 + i0 * 128: rel * 512 + i1 * 128],
                                in_=PS[:, b * 512 + i0 * 128: b * 512 + i1 * 128], func=AF.Exp,
                                scale=0.125, bias=bias),
                                reads=pskeys(b) + ["negb", "zb", "scr"], writes=[("pt", ps_, rel), "cos", "sin"])
                        if rel != 1:
                            mi = 0 if rel == 0 else 1
                            P.op("dve", lambda e, rel=rel, ps_=ps_, mi=mi, nq=nq, nbk=nbk: e.tensor_tensor(
                                out=PT2[:, ps_, rel * 512: rel * 512 + nq].rearrange("p (a b) -> p a b", b=128),
                                in0=PT2[:, ps_, rel * 512: rel * 512 + nq].rearrange("p (a b) -> p a b", b=128),
                                in1=WMASK[:, mi, :].unsqueeze(1).broadcast_to([128, nbk, 128]), op=ALU.mult),
                                reads=[("pt", ps_, rel), "wmask"], writes=[("pt", ps_, rel)])

                def l_s1b(u):
                    n0, nbk, h = units[u]
                    qlo = n0 * 128
                    nq = nbk * 128
                    kv = h // 4
                    off = (h % 2) * 64
                    ps_ = st[u]
                    for kt in range(2):
                        b = ps_alloc()
                        P.mm([lambda e, kt=kt, b=b, off=off, kv=kv, h=h, nq=nq, qlo=qlo: e.matmul(
                            PS[:, b * 512: b * 512 + nq], CKT[off:off + 64, kv, kt * 128:(kt + 1) * 128],
                            HID[off:off + 64, h // 2, qlo:qlo + nq], start=True, stop=True)],
                            reads=["ckt"] + gran("hid", h // 2, qlo, qlo + nq), writes=pskeys(b))
                        P.op("act", lambda e, b=b, kt=kt, ps_=ps_, nq=nq: e.activation(
                            out=PT2[:, ps_, (3 + kt) * 512:(3 + kt) * 512 + nq], in_=PS[:, b * 512: b * 512 + nq],
                            func=AF.Exp, scale=0.125), reads=pskeys(b) + ["scr"],
                            writes=[("pt", ps_, 3 + kt), "cos", "sin"])

                def l_s2(u):
                    n0, nbk, h = units[u]
                    qlo = n0 * 128
                    nq = nbk * 128
                    kv = h // 4
                    ps_ = st.pop(u)
                    b2 = ps_alloc()
                    fns = []
                    for kt in range(2):
                        fns.append(lambda e, kt=kt, b2=b2, kv=kv, ps_=ps_, h=h, nq=nq: e.matmul(
                            PS[:, b2 * 512: b2 * 512 + nq], ve_lhs(9 + kt, kv, h),
                            PT2[:, ps_, (3 + kt) * 512:(3 + kt) * 512 + nq], start=(kt == 0), stop=False))
                    for rel in range(3):
                        for i in range(nbk):
                            kt = n0 + i - 1 + rel
                            fns.append(lambda e, i=i, kt=kt, rel=rel, b2=b2, kv=kv, ps_=ps_, h=h, nbk=nbk: e.matmul(
                                PS[:, b2 * 512 + i * 128: b2 * 512 + (i + 1) * 128], ve_lhs(kt, kv, h),
                                PT2[:, ps_, rel * 512 + i * 128: rel * 512 + (i + 1) * 128], start=False,
                                stop=(rel == 2 and i == nbk - 1)))
                    rd = [("pt", ps_, r_) for r_ in range(5)] + [("vet", t_) for t_ in range(11)] + ["scr"]
                    P.mm(fns, reads=rd, writes=pskeys(b2))
                    normalize(b2, h, qlo, nq)

                l_s1a(0)
                l_s1b(0)
                for u in range(len(units)):
                    if u + 1 < len(units):
                        l_s1a(u + 1)
                    l_s2(u)
                    if u + 1 < len(units):
                        l_s1b(u + 1)
            if DEBUG_STOP is not None and DEBUG_STOP[1] in ("attn_o", "attn_q", "attn_k"):
                src0 = {"attn_o": O0, "attn_q": 0, "attn_k": KT0}[DEBUG_STOP[1]]
                for c in range(8 if src0 != KT0 else 4):
                    P.op("dve", lambda e, c=c: e.tensor_copy(out=X[:, c, 0:g.cols], in_=HID[:, src0 + c, 0:g.cols]),
                         reads=gran("hid", src0 + c, 0, g.cols), writes=gran("x", c, 0, g.cols))
                return
            out_proj(qtiles, (lambda nn: ((wo_d[nn], 1024),)), 8, 8,
                     (lambda kk, lo, hi: HID[:, O0 + kk, lo:hi]), (lambda kk, lo, hi: gran("hid", O0 + kk, lo, hi)),
                     l, ci, 16, nt0=nt0)

        groups = [GC] + ([GL] if RUN_LAT else [])

        def run_all():
            ps_state["i"] = 0
            for k_ in ring_state:
                ring_state[k_] = 0
            del ada_pending[:]
            phase_consts()
            ada_enqueue(0)
            for _ in range(8):
                ada_step()
            load_wv(0)
            for gi_, g in enumerate(groups):
                if gi_ > 0:
                    phase_load_x(g)
                stop = False
                def mixer_tile0(l_):
                    if l_ >= DEPTH:
                        return None
                    if l_ % 3 == 2:
                        return g.tiles[0]
                    return even_tiles(*g.win_m[l_])[0]

                for l in range(DEPTH):
                    kind = l % 3
                    f_t0 = even_tiles(*g.win_f[l])[0]
                    if kind == 0:
                        phase_gmlp(g, l, nt0=f_t0)
                    elif kind == 1:
                        phase_pool(g, l)
                    else:
                        phase_attn(g, l, nt0=f_t0)
                    if DEBUG_STOP is not None and DEBUG_STOP[0] == l and DEBUG_STOP[1] != "ffn":
                        stop = True
                        break
                    if l == 2:
                        load_wv(1)
                    if l == 3 and gi_ + 1 < len(groups):
                        load_wv(0)
                    phase_ffn(g, l, nt0=(None if DEBUG_STOP is not None else mixer_tile0(l + 1)))
                    if DEBUG_STOP == (l, "ffn"):
                        stop = True
                        break
                ada_drain()
                phase_store_x(g)

        P.dry = True
        run_all()
        P.dry = False
        run_all()
        assert wstate["used"] == len(wstate["list"]), (wstate["used"], len(wstate["list"]))

        sem_names = ["pe", "act", "dve", "pool"]
        sems = {}
        for nme in sem_names:
            sems[nme] = es.enter_context(nc.semaphore("s_" + nme))
        for qn in ("sp", "pool"):
            for i in range(NRING):
                sems[(qn, i)] = es.enter_context(nc.semaphore(f"d_{qn}_{i}"))
        print("sbuf bytes remaining", nc.sbuf_bytes_remaining, "counts", P.cnt, flush=True)
        block = es.enter_context(nc.Block())

        def runner(engname):
            def run(e):
                for item in P.q[engname]:
                    if item[0] == "wait":
                        e.wait_ge(sems[item[1]], item[2])
                    elif item[0] == "op":
                        ins = item[1](e)
                        if item[2]:
                            ins.then_inc(sems[engname], 1)
                    else:
                        ins = item[1](e)
                        ins.then_inc(sems[item[2]], 16)
                if engname in ("sp", "pool"):
                    for i in range(NRING):
                        v = P.ring[engname][i]
                        if v:
                            e.wait_ge(sems[(engname, i)], v)
            return run

        block.sync(runner("sp"))
        block.gpsimd(runner("pool"))
        block.scalar(runner("act"))
        block.vector(runner("dve"))
        block.tensor(runner("pe"))
    return nc


def _tile_w(w, kdim_chunks, cols_per):
    K, N = w.shape
    kc = K // 128
    nb = N // cols_per
    t = w.reshape(kc, 128, nb, cols_per).transpose(2, 1, 0, 3)
    return np.ascontiguousarray(t).reshape(nb, 128, kc * cols_per)


def _pp(v):
    return np.ascontiguousarray(v.reshape(-1, 128).T)


_NC_CACHE = {}


def kernel(x_prompt, x_sample, cache_k, cache_v, c, c_ctx,
           w_ada, b_ada, g_mix, g_ffn, w_ffn_in, ffn_conv_w, ffn_conv_b, w_ffn_out,
           a_w_in, a_g_v, a_w_s, a_b_s, a_w_out,
           p_w, p_b, p_scale,
           c_w_qkv, c_g_q, c_g_k, c_sink, c_w_o):
    f = np.float32
    A = lambda z: np.ascontiguousarray(np.asarray(z, dtype=f))
    x_prompt, x_sample, cache_k, cache_v, c, c_ctx = map(A, (x_prompt, x_sample, cache_k, cache_v, c, c_ctx))
    w_ada, b_ada, g_mix, g_ffn, w_ffn_in = map(A, (w_ada, b_ada, g_mix, g_ffn, w_ffn_in))
    ffn_conv_w, ffn_conv_b, w_ffn_out = map(A, (ffn_conv_w, ffn_conv_b, w_ffn_out))
    a_w_in, a_g_v, a_w_s, a_b_s, a_w_out = map(A, (a_w_in, a_g_v, a_w_s, a_b_s, a_w_out))
    p_w, p_b, p_scale = map(A, (p_w, p_b, p_scale))
    c_w_qkv, c_g_q, c_g_k, c_sink, c_w_o = map(A, (c_w_qkv, c_g_q, c_g_k, c_sink, c_w_o))

    shared = {}
    shared["w_ada_t"] = np.stack([_tile_w(w_ada[l], 8, 128).reshape(24, 2, 128, 1024).transpose(0, 2, 1, 3)
                                  .reshape(24, 128, 2048) for l in range(DEPTH)])
    ba = b_ada.reshape(DEPTH, 48, 128).transpose(2, 0, 1)
    shared["b_ada_t"] = np.ascontiguousarray(np.repeat(ba[:, :, :, None], 2, axis=3))
    shared["g_mix_t"] = np.ascontiguousarray(g_mix.reshape(DEPTH, 8, 128).transpose(2, 0, 1))
    shared["g_ffn_t"] = np.ascontiguousarray(g_ffn.reshape(DEPTH, 8, 128).transpose(2, 0, 1))
    shared["w_ffn_in_t"] = np.stack([_tile_w(w_ffn_in[l], 8, 128) for l in range(DEPTH)])
    shared["conv_w_t"] = np.ascontiguousarray(ffn_conv_w.reshape(DEPTH, 3, NJ, 128).transpose(3, 0, 1, 2))
    shared["conv_b_t"] = np.ascontiguousarray(ffn_conv_b.reshape(DEPTH, NJ, 128).transpose(2, 0, 1))
    wo = []
    for l in range(DEPTH):
        t = w_ffn_out[l].reshape(NJ, 128, 8, 128).transpose(2, 1, 0, 3)
        t = t.reshape(8, 128, 2, 11 * 128).transpose(0, 2, 1, 3).reshape(16, 128, 11 * 128)
        wo.append(t)
    shared["w_ffn_out_t"] = np.ascontiguousarray(np.stack(wo))
    shared["a_w_u_t"] = np.stack([_tile_w(a_w_in[j][:, :2048], 8, 128) for j in range(2)])
    shared["a_w_v_t"] = np.stack([_tile_w(a_w_in[j][:, 2048:], 8, 512) for j in range(2)])
    shared["a_g_v_t"] = np.ascontiguousarray(a_g_v.reshape(2, 16, 128).transpose(2, 0, 1))
    shared["a_w_s_t"] = np.ascontiguousarray(a_w_s.transpose(0, 3, 1, 2)).reshape(2, 128, 8 * 128)
    shared["a_b_s_t"] = np.ascontiguousarray(np.broadcast_to(a_b_s[None], (128, 2, 8, 128)))
    shared["a_w_out_t"] = np.stack([np.ascontiguousarray(a_w_out[j].reshape(16, 128, 8, 128).transpose(2, 1, 0, 3))
                                    .reshape(8, 128, 16 * 128) for j in range(2)])
    pw = p_w[0].reshape(4, 2, 128, 2, 128).transpose(2, 0, 3, 1, 4)
    shared["p_w_t"] = np.ascontiguousarray(pw).reshape(128, 8 * 2 * 128)
    shared["p_b_t"] = _pp(p_b[0])
    shared["p_scale_t"] = _pp(p_scale[0])
    wqkv = c_w_qkv[0]
    shared["c_w_q_t"] = _tile_w(wqkv[:, :1024], 8, 128)
    kd = np.concatenate([np.concatenate([wqkv[:, 1024 + kh * 64:1024 + (kh + 1) * 64]] * 2, axis=1) for kh in range(4)],
                        axis=1)
    shared["c_w_k_t"] = _tile_w(np.ascontiguousarray(kd), 8, 128)
    shared["c_w_vt_t"] = _tile_w(np.ascontiguousarray(wqkv[:, 1280:1536]), 8, 256)[0]
    shared["c_w_vf_t"] = _tile_w(np.ascontiguousarray(wqkv[:, 1280:1536]), 8, 128)
    shared["c_w_o_t"] = _tile_w(c_w_o[0], 8, 128)
    shared["c_g_q_t"] = np.ascontiguousarray(np.tile(c_g_q[0], 2)[:, None])
    shared["c_g_k_t"] = np.ascontiguousarray(np.tile(c_g_k[0], 2)[:, None])
    shared["c_sink_t"] = np.ascontiguousarray(np.broadcast_to(c_sink[0][None], (128, 16)))
    shared["ones_t"] = np.ones((128, 128), f)
    blk = np.zeros((128, 128), f)
    blk[:64, :64] = 1
    blk[64:, 64:] = 1
    shared["blk_t"] = blk
    pm = np.zeros((128, 128), f)
    for m in range(128):
        if m % 32 < 16:
            pm[m + 16, m] = -1.0
        else:
            pm[m - 16, m] = 1.0
    shared["pm_t"] = pm
    kk = np.arange(128)[:, None]
    qq = np.arange(128)[None, :]
    shared["wmask_t"] = np.ascontiguousarray(np.stack([(kk >= qq), (kk <= qq)], axis=1).astype(f))

    def ic_tables(S):
        out = np.zeros((2, 4, 16), f)
        for gi, w in enumerate((2, 4, 8, 16)):
            half = w // 2
            for i in range(16):
                t = i
                lo = max(t - half, 0); hi = min(t + half - 1, S - 1)
                out[0, gi, i] = np.float32(1.0) / np.float32(hi - lo + 1)
                t = S - 16 + i
                lo = max(t - half, 0); hi = min(t + half - 1, S - 1)
                out[1, gi, i] = np.float32(1.0) / np.float32(hi - lo + 1)
        return out

    ic_c = ic_tables(256)
    ic_l = ic_tables(2048)
    mid = np.zeros((4, 16), f)
    for gi, w in enumerate((2, 4, 8, 16)):
        mid[gi, :] = np.float32(1.0) / np.float32(w)
    shared["ic_ctx_t"] = np.ascontiguousarray(np.broadcast_to(ic_c[None], (128, 2, 4, 16)))

    inv = (10000.0 ** (-np.arange(16, dtype=np.float32) / 16)).astype(f)

    in_maps = []
    for core in range(8):
        m = dict(shared)
        b_lat = core // 4
        cq = core % 4
        xc = x_prompt[4 * core:4 * core + 4].reshape(1024, D)
        m["xc"] = np.ascontiguousarray(xc.T.reshape(8, 128, 1024).transpose(1, 0, 2))
        start = 512 * cq - 256
        pos = start + np.arange(LAT_COLS)
        valid = (pos >= 0) & (pos < 2048)
        xl = np.zeros((LAT_COLS, D), f)
        xl[valid] = x_sample[b_lat][pos[valid]]
        m["xl"] = np.ascontiguousarray(xl.T.reshape(8, 128, LAT_COLS).transpose(1, 0, 2))
        cond2 = np.stack([c_ctx, c[b_lat]], axis=1)
        m["condT"] = np.ascontiguousarray(cond2.reshape(8, 128, 2).transpose(1, 0, 2))
        row = (pos // 64).astype(f)
        col = (pos % 64).astype(f)
        p = np.arange(128)
        fr = p % 16
        is_col = ((p % 64) // 32) == 1
        ang = np.where(is_col[:, None], col[None, :] * inv[fr][:, None], row[None, :] * inv[fr][:, None]).astype(f)
        m["cos_t"] = np.cos(ang).astype(f)
        m["sin_t"] = np.sin(ang).astype(f)
        fl = np.array([1.0 if cq > 0 else 0.0, 1.0 if cq < 3 else 0.0], f)
        m["flags_t"] = np.ascontiguousarray(np.broadcast_to(fl[None], (128, 2)))
        icl = np.stack([ic_l[0] if cq == 0 else mid, ic_l[1] if cq == 3 else mid], axis=0)
        m["ic_lat_t"] = np.ascontiguousarray(np.broadcast_to(icl[None], (128, 2, 4, 16)))
        ck = cache_k[b_lat, 0]
        ckt = ck.transpose(1, 2, 0)
        m["cache_k_t"] = np.ascontiguousarray(np.concatenate([ckt, ckt], axis=1).transpose(1, 0, 2))
        cv = cache_v[b_lat, 0].reshape(2, 128, 256)
        m["cache_v_t"] = np.ascontiguousarray(cv.transpose(1, 0, 2))
        in_maps.append(m)

    if "nc" not in _NC_CACHE:
        _NC_CACHE["nc"] = build_program()
    nc = _NC_CACHE["nc"]
    res = run_bass_kernel_spmd(nc, in_maps, core_ids=list(range(8)))

    y_prompt = np.zeros((32, 256, D), f)
    y_sample = np.zeros((2, 2048, D), f)
    new_k = np.zeros((32, 1, 256, 4, 64), f)
    new_v = np.zeros((32, 1, 256, 4, 64), f)
    for core in range(8):
        r = res.results[core]
        yc = np.asarray(r["yc"]).transpose(1, 0, 2).reshape(D, 1024).T
        y_prompt[4 * core:4 * core + 4] = yc.reshape(4, 256, D)
        nk = np.asarray(r["nk"]).T.reshape(4, 256, 4, 64)
        nv = np.asarray(r["nv"]).T.reshape(4, 256, 4, 64)
        new_k[4 * core:4 * core + 4, 0] = nk
        new_v[4 * core:4 * core + 4, 0] = nv
        yl = np.asarray(r["yl"]).transpose(1, 0, 2).reshape(D, LAT_COLS).T
        b_lat = core // 4
        cq = core % 4
        lo_col = 256 if cq == 0 else 257
        hi_col = 768 if cq == 3 else 769
        tok0 = 512 * cq - 256
        y_sample[b_lat, tok0 + lo_col: tok0 + hi_col] = yl[lo_col:hi_col]
    return (y_prompt, y_sample, new_k, new_v)
```

```python
import contextlib
import numpy as np
import concourse.bass as bass
import concourse.mybir as mybir
from concourse.bass_utils import run_bass_kernel_spmd

F32 = mybir.dt.float32
BF16 = mybir.dt.bfloat16
AF = mybir.ActivationFunctionType
ALU = mybir.AluOpType
AX = mybir.AxisListType

D = 1024
DEPTH = 4
NCH = 8
DFF = 2816
NJ = 22
EPS = 1e-6
CTX_COLS = 1024
LAT_COLS = 1152
NSLOT = 5
SLOT_E = 2048
NRING = 16
PREFETCH = NSLOT - 1

RUN_LAT = True
DEBUG_STOP = None


class Plan:
    def __init__(self):
        self.q = {e: [] for e in ("pe", "act", "dve", "pool", "sp")}
        self.cnt = {e: 0 for e in ("pe", "act", "dve", "pool")}
        self.waited = {e: {} for e in self.q}
        self.lastw = {}
        self.readers = {}
        self.ring = {"sp": [0] * NRING, "pool": [0] * NRING}
        self.ring_i = {"sp": 0, "pool": 0}
        self.dry = False

    def _wait(self, eng, semkey, val):
        if semkey == "pe" and eng == "pe":
            return
        if self.waited[eng].get(semkey, 0) >= val:
            return
        self.waited[eng][semkey] = val
        self.q[eng].append(("wait", semkey, val))

    def _sync(self, eng, reads, writes):
        for k in reads:
            lw = self.lastw.get(k)
            if lw:
                self._wait(eng, lw[0], lw[1])
        for k in writes:
            lw = self.lastw.get(k)
            if lw:
                self._wait(eng, lw[0], lw[1])
            for sk, v in self.readers.get(k, {}).items():
                self._wait(eng, sk, v)

    def _commit(self, ev, reads, writes):
        sk, v = ev
        for k in reads:
            d = self.readers.setdefault(k, {})
            if d.get(sk, 0) < v:
                d[sk] = v
        for k in writes:
            self.lastw[k] = ev
            self.readers[k] = {}

    def op(self, eng, fn, reads=(), writes=()):
        if self.dry:
            return
        reads = list(reads)
        writes = list(writes)
        self._sync(eng, reads, writes)
        self.cnt[eng] += 1
        self.q[eng].append(("op", fn, True))
        self._commit((eng, self.cnt[eng]), reads, writes)

    def mm(self, fns, reads=(), writes=()):
        if self.dry:
            return
        reads = list(reads)
        writes = list(writes)
        self._sync("pe", reads, writes)
        for f in fns[:-1]:
            self.q["pe"].append(("op", f, False))
        self.cnt["pe"] += 1
        self.q["pe"].append(("op", fns[-1], True))
        self._commit(("pe", self.cnt["pe"]), reads, writes)

    def dma(self, queue, fn, reads=(), writes=()):
        if self.dry:
            return
        reads = list(reads)
        writes = list(writes)
        self._sync(queue, reads, writes)
        i = self.ring_i[queue]
        self.ring_i[queue] += 1
        slot = i % NRING
        prev = self.ring[queue][slot]
        semkey = (queue, slot)
        if prev:
            self._wait(queue, semkey, prev)
        val = prev + 16
        self.ring[queue][slot] = val
        self.q[queue].append(("dma", fn, semkey))
        self._commit((semkey, val), reads, writes)


def gran(buf, c, lo, hi):
    return [(buf, c, g) for g in range(lo // 128, (hi - 1) // 128 + 1)]


def split_tiles(lo, hi, maxn=512):
    out = []
    a = lo
    while a < hi:
        b = min(hi, (a // maxn + 1) * maxn)
        out.append((a, b))
        a = b
    return out


def even_tiles(lo, hi, maxn=512):
    n = -(-(hi - lo) // maxn)
    step = -(-(hi - lo) // n)
    step += step % 2
    out = []
    a = lo
    while a < hi:
        out.append((a, min(hi, a + step)))
        a += step
    return out


def build_program():
    nc = bass.Bass("TRN2", target_bir_lowering=False)
    P = Plan()
    dram_in = {}

    def din(name, shape):
        t = nc.dram_tensor(name, list(shape), F32, kind="ExternalInput")
        dram_in[name] = tuple(shape)
        return t.ap()

    def dout(name, shape):
        return nc.dram_tensor(name, list(shape), F32, kind="ExternalOutput").ap()

    xc_d = din("xc", [128, NCH, CTX_COLS])
    xl_d = din("xl", [128, NCH, LAT_COLS])
    cond_d = din("condT", [128, NCH, 2])
    wada_d = din("w_ada_t", [DEPTH, 24, 128, 2 * 8 * 128])
    bada_d = din("b_ada_t", [128, DEPTH, 48, 2])
    gmix_d = din("g_mix_t", [128, DEPTH, NCH])
    gffn_d = din("g_ffn_t", [128, DEPTH, NCH])
    wfin_d = din("w_ffn_in_t", [DEPTH, 44, 128, 8 * 128])
    cw_d = din("conv_w_t", [128, DEPTH, 3, NJ])
    cb_d = din("conv_b_t", [128, DEPTH, NJ])
    wfout_d = din("w_ffn_out_t", [DEPTH, 16, 128, 11 * 128])
    awu_d = din("a_w_u_t", [2, 16, 128, 8 * 128])
    awv_d = din("a_w_v_t", [2, 4, 128, 8 * 512])
    agv_d = din("a_g_v_t", [128, 2, 16])
    aws_d = din("a_w_s_t", [2, 128, 8 * 128])
    abs_d = din("a_b_s_t", [128, 2, 8, 128])
    awo_d = din("a_w_out_t", [2, 8, 128, 16 * 128])
    pw_d = din("p_w_t", [128, 8 * 2 * 128])
    pb_d = din("p_b_t", [128, NCH])
    psc_d = din("p_scale_t", [128, NCH])
    wq_d = din("c_w_q_t", [8, 128, 8 * 128])
    wk_d = din("c_w_k_t", [4, 128, 8 * 128])
    wvt_d = din("c_w_vt_t", [128, 8 * 256])
    wvf_d = din("c_w_vf_t", [2, 128, 8 * 128])
    wo_d = din("c_w_o_t", [8, 128, 8 * 128])
    gq_d = din("c_g_q_t", [128, 1])
    gk_d = din("c_g_k_t", [128, 1])
    sink_d = din("c_sink_t", [128, 16])
    ones_d = din("ones_t", [128, 128])
    blk_d = din("blk_t", [128, 128])
    pm_d = din("pm_t", [128, 128])
    cos_d = din("cos_t", [128, LAT_COLS])
    sin_d = din("sin_t", [128, LAT_COLS])
    wmask_d = din("wmask_t", [128, 2, 128])
    flags_d = din("flags_t", [128, 2])
    icc_d = din("ic_ctx_t", [128, 2, 4, 16])
    icl_d = din("ic_lat_t", [128, 2, 4, 16])
    ck_d = din("cache_k_t", [128, 4, 256])
    cv_d = din("cache_v_t", [128, 2, 256])

    yc_d = dout("yc", [128, NCH, CTX_COLS])
    yl_d = dout("yl", [128, NCH, LAT_COLS])
    nk_d = dout("nk", [256, CTX_COLS])
    nv_d = dout("nv", [256, CTX_COLS])

    es = contextlib.ExitStack()

    def sb(name, shape, dt):
        return es.enter_context(nc.sbuf_tensor(name, list(shape), dt))

    with es:
        X = sb("X", [128, NCH, LAT_COLS], F32)
        H = sb("H", [128, NCH, LAT_COLS], BF16)
        HID = sb("HID", [128, NJ, LAT_COLS], BF16)
        SCR = sb("SCR", [128, 8192], F32)
        AS = sb("AS", [128, 2, 1160], F32)
        NTMP = 6
        TMPR = sb("TMPR", [128, NTMP, 512], F32)
        NSQ = 4
        SQ = sb("SQ", [128, NSQ, 512], BF16)
        NRS = 3
        RS = sb("RS", [128, NRS, 512], F32)
        WSL = sb("WSL", [128, NSLOT, SLOT_E], BF16)
        MOD = sb("MOD", [128, DEPTH, 48, 2], F32)
        BADA = sb("BADA", [128, DEPTH, 48, 2], F32)
        G1 = sb("G1", [128, DEPTH, 2, NCH], F32)
        G2 = sb("G2", [128, DEPTH, 2, NCH], F32)
        GMIX = sb("GMIX", [128, DEPTH, NCH], F32)
        GFFN = sb("GFFN", [128, DEPTH, NCH], F32)
        CW = sb("CW", [128, DEPTH, 3, NJ], F32)
        CB = sb("CB", [128, DEPTH, NJ], F32)
        CONDT = sb("CONDT", [128, NCH, 2], F32)
        SCOND = sb("SCOND", [128, NCH, 2], BF16)
        AGV = sb("AGV", [128, 2, 16], F32)
        ABS = sb("ABS", [128, 2, 8, 128], F32)
        WST = sb("WST", [128, 2, 8 * 128], BF16)
        PB = sb("PB", [128, NCH], F32)
        PSC = sb("PSC", [128, NCH], F32)
        PA = sb("PA", [128, 2, NCH], F32)
        PBB = sb("PBB", [128, 2, NCH], F32)
        G1F = sb("G1F", [128, 2, NCH], F32)
        SH1F = sb("SH1F", [128, 2, NCH], F32)
        GQ = sb("GQ", [128, 1], F32)
        GK = sb("GK", [128, 1], F32)
        SINK = sb("SINK", [128, 16], F32)
        ESK = sb("ESK", [128, 16], F32)
        ONES = sb("ONES", [128, 128], BF16)
        BLK = sb("BLK", [128, 128], BF16)
        PM = sb("PM", [128, 128], BF16)
        WMASK = sb("WMASK", [128, 2, 128], F32)
        FLAGS = sb("FLAGS", [128, 2], F32)
        NEGB = sb("NEGB", [128, 2], F32)
        ZB = sb("ZB", [128, 1], F32)
        EPST = sb("EPST", [128, 1], F32)
        ICC = sb("ICC", [128, 2, 4, 16], F32)
        ICL = sb("ICL", [128, 2, 4, 16], F32)
        SS = sb("SS", [128, 8], F32)
        PS = es.enter_context(nc.psum_tensor("PS", [128, 4096], F32))

        WV = SCR[:, 0:8192].bitcast(BF16)
        COS = SCR[:, 0:LAT_COLS]
        SIN = SCR[:, LAT_COLS:2 * LAT_COLS]
        PT = SCR[:, 0:2560].bitcast(BF16)
        VE = SCR[:, 2560:2560 + 4224].bitcast(BF16)
        H32 = HID[:, :, :].bitcast(F32)

        ps_state = {"i": 0}

        NPSB = 7

        def ps_alloc(n=1):
            i = ps_state["i"]
            if i + n > NPSB:
                i = 0
            ps_state["i"] = (i + n) % NPSB
            return i

        def psb(b, n=512, nb=1):
            return PS[:, b * 512: b * 512 + (n if nb == 1 else nb * 512)]

        def pskeys(b, nb=1):
            return [("ps", b + i) for i in range(nb)]

        ring_state = {"tmp": 0, "sq": 0, "rs": 0, "pt": 0}

        def nxt(name, n):
            i = ring_state[name]
            ring_state[name] = (i + 1) % n
            return i

        sched = []
        wstate = {"issued": 0, "used": 0, "list": []}

        def w_issue_upto(k):
            L = wstate["list"]
            while wstate["issued"] < min(k, len(L)):
                i = wstate["issued"]
                ap, ne = L[i]
                s = i % NSLOT
                P.dma("pool", (lambda e, s=s, ap=ap, ne=ne: e.dma_start(out=WSL[:, s, 0:ne], in_=ap)),
                      reads=[], writes=[("w", s)])
                wstate["issued"] += 1

        def getw(*aps):
            n = len(aps)
            if P.dry:
                wstate["list"].extend(aps)
                return 0 if n == 1 else [0] * n
            i = wstate["used"]
            wstate["used"] += n
            w_issue_upto(i + NSLOT)
            if n == 1:
                return i % NSLOT
            return [(i + k) % NSLOT for k in range(n)]

        def load_f32(dst, src, key):
            P.dma("sp", (lambda e: e.dma_start(out=dst, in_=src)), reads=[], writes=[key])

        def load_bf16(dst, src, key):
            P.dma("pool", (lambda e: e.dma_start(out=dst, in_=src)), reads=[], writes=[key])

        def phase_consts():
            load_f32(CONDT[:, :, :], cond_d, "condt")
            load_f32(BADA[:, :, :, :], bada_d, "bada")
            load_f32(GMIX[:, :, :], gmix_d, "gmix")
            load_f32(GFFN[:, :, :], gffn_d, "gffn")
            phase_load_x(groups[0])
            load_f32(CW[:, :, :, :], cw_d, "cw")
            load_f32(CB[:, :, :], cb_d, "cb")
            load_f32(AGV[:, :, :], agv_d, "agv")
            load_f32(ABS[:, :, :, :], abs_d, "abs")
            load_f32(PB[:, :], pb_d, "pb")
            load_f32(PSC[:, :], psc_d, "psc")
            load_f32(GQ[:, :], gq_d, "gq")
            load_f32(GK[:, :], gk_d, "gk")
            load_f32(SINK[:, :], sink_d, "sink")
            load_f32(WMASK[:, :, :], wmask_d, "wmask")
            load_f32(FLAGS[:, :], flags_d, "flags")
            load_f32(ICC[:, :, :, :], icc_d, "icc")
            load_f32(ICL[:, :, :, :], icl_d, "icl")
            load_bf16(ONES[:, :], ones_d, "ones")
            load_bf16(BLK[:, :], blk_d, "blk")
            load_bf16(PM[:, :], pm_d, "pm")
            for j in range(2):
                load_bf16(WST[:, j, :], aws_d[j], "wst")
            P.op("dve", lambda e: e.memset(ZB[:, :], 0.0), writes=["zb"])
            P.op("dve", lambda e: e.memset(EPST[:, :], float(EPS)), writes=["epst"])
            P.op("act", lambda e: e.activation(out=SCOND[:, :, :], in_=CONDT[:, :, :], func=AF.Silu),
                 reads=["condt"], writes=["scond"])
            P.op("act", lambda e: e.activation(out=ESK[:, :], in_=SINK[:, :], func=AF.Exp),
                 reads=["sink"], writes=["esk"])
            P.op("dve", lambda e: e.tensor_scalar(out=NEGB[:, :], in0=FLAGS[:, :], scalar1=30000.0, scalar2=-30000.0,
                                                  op0=ALU.mult, op1=ALU.add), reads=["flags"], writes=["negb"])


        ada_pending = []

        def ada_enqueue(l):
            for i in range(24):
                ada_pending.append((l, i))

        def ada_step():
            if not ada_pending:
                return
            l, i = ada_pending.pop(0)
            b = 7
            s = getw((wada_d[l, i], 2048))
            for h in range(2):
                cc = i * 2 + h
                fns = []
                for kc in range(8):
                    fns.append(lambda e, s=s, h=h, kc=kc, cc=cc, b=b: e.matmul(
                        PS[:, b * 512 + cc * 2: b * 512 + cc * 2 + 2],
                        WSL[:, s, (h * 8 + kc) * 128:(h * 8 + kc + 1) * 128],
                        SCOND[:, kc, :], start=(kc == 0), stop=(kc == 7)))
                P.mm(fns, reads=[("w", s), "scond"], writes=pskeys(b))
            if i == 7:
                ada_finish1(l, b)
            if i == 23:
                ada_finish2(l, b)

        def ada_drain():
            while ada_pending:
                ada_step()

        def ada_finish1(l, b):
            P.op("dve", lambda e, b=b, l=l: e.tensor_tensor(
                out=MOD[:, l, 0:16, :], in0=PS[:, b * 512: b * 512 + 32].rearrange("p (c t) -> p c t", t=2),
                in1=BADA[:, l, 0:16, :], op=ALU.add), reads=pskeys(b) + ["bada"], writes=[("mod1", l)])
            for ci in range(2):
                P.op("dve", lambda e, l=l, ci=ci: e.scalar_tensor_tensor(
                    out=G1[:, l, ci, :], in0=MOD[:, l, 8:16, ci], scalar=1.0, in1=GMIX[:, l, :],
                    op0=ALU.add, op1=ALU.mult), reads=[("mod1", l), "gmix"], writes=[("g1", l)])
            if l == 1:
                ada_masked_tables()

        def ada_masked_tables():
            for fl in range(2):
                P.op("dve", lambda e, fl=fl: e.tensor_scalar(out=G1F[:, fl, :], in0=G1[:, 1, 1, :], scalar1=FLAGS[:, fl:fl + 1],
                                                             scalar2=None, op0=ALU.mult),
                     reads=[("g1", 1), "flags"], writes=["g1f"])
                P.op("dve", lambda e, fl=fl: e.tensor_scalar(out=SH1F[:, fl, :], in0=MOD[:, 1, 0:8, 1], scalar1=FLAGS[:, fl:fl + 1],
                                                             scalar2=None, op0=ALU.mult),
                     reads=[("mod1", 1), "flags"], writes=["g1f"])

        def ada_finish2(l, b):
            P.op("dve", lambda e, b=b, l=l: e.tensor_tensor(
                out=MOD[:, l, 16:48, :], in0=PS[:, b * 512 + 32: b * 512 + 96].rearrange("p (c t) -> p c t", t=2),
                in1=BADA[:, l, 16:48, :], op=ALU.add), reads=pskeys(b) + ["bada"], writes=[("mod", l)])
            for ci in range(2):
                P.op("dve", lambda e, l=l, ci=ci: e.scalar_tensor_tensor(
                    out=G2[:, l, ci, :], in0=MOD[:, l, 32:40, ci], scalar=1.0, in1=GFFN[:, l, :],
                    op0=ALU.add, op1=ALU.mult), reads=[("mod", l), "gffn"], writes=[("g2", l)])
            if l == 1:
                for ci in range(2):
                    P.op("dve", lambda e, ci=ci: e.tensor_tensor(out=PA[:, ci, :], in0=MOD[:, 1, 16:24, ci],
                                                                  in1=PSC[:, :], op=ALU.mult),
                         reads=[("mod", 1), "psc"], writes=["pa"])
                    P.op("dve", lambda e, ci=ci: e.tensor_tensor(out=PBB[:, ci, :], in0=PA[:, ci, :],
                                                                  in1=PB[:, :], op=ALU.mult),
                         reads=["pa", "pb"], writes=["pbb"])

        class Grp:
            pass

        GC = Grp()
        GC.name = "c"; GC.ci = 0; GC.cols = CTX_COLS; GC.nseg = 4; GC.L = 256
        GC.tiles = [(0, 512), (512, 1024)]
        GC.x_d = xc_d; GC.y_d = yc_d; GC.lat = False
        GL = Grp()
        GL.name = "l"; GL.ci = 1; GL.cols = LAT_COLS; GL.nseg = 1; GL.L = LAT_COLS
        GL.tiles = [(0, 512), (512, 1024), (1024, 1152)]
        GL.x_d = xl_d; GL.y_d = yl_d; GL.lat = True
        GC.win_f = [(0, 1024)] * 4
        GC.win_m = [(0, 1024)] * 4
        GC.qwin = (0, 1024)
        GL.win_f = [(117, 1035), (126, 1026), (255, 897), (254, 771)]
        GL.win_m = [(0, 1152), (118, 1034), (0, 1152), (256, 896)]
        GL.qwin = (128, 1024)

        def v3(ap2d, g, n):
            if g.nseg > 1:
                return ap2d.rearrange("p (s l) -> p s l", l=g.L)
            return ap2d

        def as_view(g, par, lo, hi, sh):
            if g.nseg > 1:
                base = AS[:, par, 0:g.nseg * (g.L + 2)].rearrange("p (s l) -> p s l", l=g.L + 2)
                return base[:, lo // g.L: hi // g.L, 1 + sh: 1 + sh + g.L]
            return AS[:, par, 1 + lo + sh: 1 + hi + sh]

        def as_keys(par, lo, hi):
            return [("as", par, k) for k in range(max(lo, 0) // 128, (hi - 1) // 128 + 1)]

        def phase_load_x(g):
            for c in range(NCH):
                P.dma("sp", (lambda e, c=c: e.dma_start(out=X[:, c, 0:g.cols], in_=g.x_d[:, c, :])),
                      reads=[], writes=gran("x", c, 0, g.cols))
            P.op("pool", lambda e: e.memset(AS[:, :, :], 0.0), writes=as_keys(0, 0, 1160) + as_keys(1, 0, 1160))

        def phase_store_x(g):
            for c in range(NCH):
                P.dma("sp", (lambda e, c=c: e.dma_start(out=g.y_d[:, c, :], in_=X[:, c, 0:g.cols])),
                      reads=gran("x", c, 0, g.cols), writes=[])

        pre_stats = {}

        def norm_stats_tile(lo, hi, all_act):
            n = hi - lo
            b = ps_alloc()
            for c in range(NCH):
                s = nxt("sq", NSQ)
                if c % 2 == 0 or all_act:
                    P.op("act", lambda e, s=s, c=c, lo=lo, hi=hi, n=n: e.activation(
                        out=SQ[:, s, 0:n], in_=X[:, c, lo:hi], func=AF.Square),
                        reads=gran("x", c, lo, hi), writes=[("sq", s)])
                else:
                    P.op("dve", lambda e, s=s, c=c, lo=lo, hi=hi, n=n: e.tensor_tensor(
                        out=SQ[:, s, 0:n], in0=X[:, c, lo:hi], in1=X[:, c, lo:hi], op=ALU.mult),
                        reads=gran("x", c, lo, hi), writes=[("sq", s)])
                P.mm([lambda e, s=s, c=c, b=b, n=n: e.matmul(PS[:, b * 512: b * 512 + n], ONES[:, :], SQ[:, s, 0:n],
                                                              start=(c == 0), stop=(c == NCH - 1))],
                     reads=[("sq", s), "ones"], writes=pskeys(b))
            r = nxt("rs", NRS)
            P.op("act", lambda e, r=r, b=b, n=n: e.activation(
                out=RS[:, r, 0:n], in_=PS[:, b * 512: b * 512 + n], func=AF.Ln, scale=1.0 / D,
                bias=EPST[:, 0:1]), reads=pskeys(b) + ["epst"], writes=[("rs", r)])
            P.op("act", lambda e, r=r, n=n: e.activation(
                out=RS[:, r, 0:n], in_=RS[:, r, 0:n], func=AF.Exp, scale=-0.5),
                reads=[("rs", r)], writes=[("rs", r)])
            return r

        def norm_stats(tiles):
            out = []
            for ti_, (lo, hi) in enumerate(tiles):
                if (lo, hi) in pre_stats:
                    out.append(pre_stats.pop((lo, hi)))
                else:
                    out.append(norm_stats_tile(lo, hi, ti_ == 0))
            return out

        def pre_norm_a(nt0):
            lo, hi = nt0
            n = hi - lo
            for c in range(NCH):
                P.op("act", lambda e, c=c, lo=lo, hi=hi, n=n: e.activation(
                    out=H[:, c, 0:n], in_=X[:, c, lo:hi], func=AF.Square),
                    reads=gran("x", c, lo, hi), writes=gran("h", c, 0, n))

        def pre_norm_b(nt0):
            lo, hi = nt0
            n = hi - lo
            b = ps_alloc()
            for c in range(NCH):
                P.mm([lambda e, c=c, b=b, n=n: e.matmul(PS[:, b * 512: b * 512 + n], ONES[:, :], H[:, c, 0:n],
                                                        start=(c == 0), stop=(c == NCH - 1))],
                     reads=gran("h", c, 0, n) + ["ones"], writes=pskeys(b))
            r = nxt("rs", NRS)
            P.op("act", lambda e, r=r, b=b, n=n: e.activation(
                out=RS[:, r, 0:n], in_=PS[:, b * 512: b * 512 + n], func=AF.Ln, scale=1.0 / D,
                bias=EPST[:, 0:1]), reads=pskeys(b) + ["epst"], writes=[("rs", r)])
            P.op("act", lambda e, r=r, n=n: e.activation(
                out=RS[:, r, 0:n], in_=RS[:, r, 0:n], func=AF.Exp, scale=-0.5),
                reads=[("rs", r)], writes=[("rs", r)])
            pre_stats[nt0] = r

        def norm_mod(g, tiles, gsc, shf, dep_keys, out_ap_fn, out_keys_fn, tmp_view=None, flagged=None):
            if tiles[0] in pre_stats:
                r0 = pre_stats.pop(tiles[0])
                norm_apply(g, [tiles[0]], [r0], gsc, shf, dep_keys, out_ap_fn, out_keys_fn, tmp_view, flagged)
                rest = tiles[1:]
                norm_apply(g, rest, norm_stats(rest), gsc, shf, dep_keys, out_ap_fn, out_keys_fn, tmp_view, flagged)
                return
            norm_apply(g, tiles, norm_stats(tiles), gsc, shf, dep_keys, out_ap_fn, out_keys_fn, tmp_view, flagged)

        def norm_apply(g, tiles, rs, gsc, shf, dep_keys, out_ap_fn, out_keys_fn, tmp_view, flagged):
            for ti, (lo, hi) in enumerate(tiles):
                n = hi - lo
                r = rs[ti]
                for c in range(NCH):
                    t = nxt("tmp", NTMP)
                    P.op("dve", lambda e, t=t, c=c, r=r, lo=lo, hi=hi, n=n: e.tensor_tensor(
                        out=TMPR[:, t, 0:n], in0=X[:, c, lo:hi], in1=RS[:, r, 0:n], op=ALU.mult),
                        reads=gran("x", c, lo, hi) + [("rs", r)], writes=[("tmp", t)])
                    if flagged is not None:
                        gf, sf = flagged
                        for (a0, a1, fl) in ((0, 256, 0), (256, 768, None), (768, LAT_COLS, 1)):
                            s0, s1 = max(a0, lo), min(a1, hi)
                            if s0 >= s1:
                                continue
                            sc_ap = gsc[:, c:c + 1] if fl is None else gf[:, fl, c:c + 1]
                            bi_ap = shf(c) if fl is None else sf[:, fl, c:c + 1]
                            P.op("act", lambda e, t=t, c=c, s0=s0, s1=s1, lo=lo, sc_ap=sc_ap, bi_ap=bi_ap: e.activation(
                                out=out_ap_fn(c, s0, s1), in_=TMPR[:, t, s0 - lo:s1 - lo],
                                func=AF.Identity, scale=sc_ap, bias=bi_ap),
                                reads=[("tmp", t)] + dep_keys, writes=out_keys_fn(c, s0, s1))
                        continue
                    P.op("act", lambda e, t=t, c=c, lo=lo, hi=hi, n=n: e.activation(
                        out=out_ap_fn(c, lo, hi),
                        in_=(TMPR[:, t, 0:n] if tmp_view is None else tmp_view(TMPR[:, t, 0:n], n)),
                        func=AF.Identity, scale=gsc[:, c:c + 1], bias=shf(c)),
                        reads=[("tmp", t)] + dep_keys, writes=out_keys_fn(c, lo, hi))

        def h_out(c, lo, hi):
            return H[:, c, lo:hi]

        def h_keys(c, lo, hi):
            return gran("h", c, lo, hi)

        def out_proj(tiles, wspec, nk, per_slot, src, src_keys, l, ci, gate0, nt0=None):
            def one(nn, slots, lo, hi):
                n = hi - lo
                b = ps_alloc()
                fns = []
                for kk in range(nk):
                    sl = slots[kk // per_slot]
                    ko = kk % per_slot
                    fns.append(lambda e, sl=sl, ko=ko, kk=kk, b=b, lo=lo, hi=hi, n=n: e.matmul(
                        PS[:, b * 512: b * 512 + n], WSL[:, sl, ko * 128:(ko + 1) * 128], src(kk, lo, hi),
                        start=(kk == 0), stop=(kk == nk - 1)))
                rd = [("w", sl) for sl in slots]
                for kk in range(nk):
                    rd += src_keys(kk, lo, hi)
                P.mm(fns, reads=rd, writes=pskeys(b))
                P.op("dve", lambda e, b=b, lo=lo, hi=hi, n=n, nn=nn: e.scalar_tensor_tensor(
                    out=X[:, nn, lo:hi], in0=PS[:, b * 512: b * 512 + n], scalar=MOD[:, l, gate0 + nn: gate0 + nn + 1, ci],
                    in1=X[:, nn, lo:hi], op0=ALU.mult, op1=ALU.add),
                    reads=pskeys(b) + gran("x", nn, lo, hi) + [("mod", l)], writes=gran("x", nn, lo, hi))

            def as_list(x):
                return x if isinstance(x, list) else [x]

            for nn in range(NCH - 2):
                slots = as_list(getw(*wspec(nn)))
                for (lo, hi) in tiles:
                    one(nn, slots, lo, hi)
            sp6 = wspec(NCH - 2)
            sp7 = wspec(NCH - 1)
            allslots = as_list(getw(*(tuple(sp6) + tuple(sp7))))
            sl6 = allslots[:len(sp6)]
            sl7 = allslots[len(sp6):]
            hooked = 0
            for k_, (lo, hi) in enumerate(tiles):
                one(NCH - 2, sl6, lo, hi)
                if hooked == 1:
                    pre_norm_b(nt0)
                    hooked = 2
                one(NCH - 1, sl7, lo, hi)
                if (nt0 is not None and hooked == 0 and k_ + 1 < len(tiles)
                        and tiles[0][0] <= nt0[0] and nt0[1] <= hi):
                    pre_norm_a(nt0)
                    hooked = 1

        def ffn_weights(l):
            ws = [(wfin_d[l, m], 1024) for m in range(44)]
            ws += [(wfout_d[l, i], 1408) for i in range(16)]
            return ws

        def evac_a(g, b, par, lo, hi):
            n = hi - lo
            if not g.lat:
                P.op("act", lambda e: e.activation(out=as_view(g, par, lo, hi, 0), in_=v3(PS[:, b * 512: b * 512 + n], g, n),
                                                   func=AF.Copy),
                     reads=pskeys(b), writes=as_keys(par, lo, hi))
                return
            pieces = []
            for (a0, a1, fl) in ((0, 256, 0), (256, 768, None), (768, LAT_COLS, 1)):
                s0, s1 = max(a0, lo), min(a1, hi)
                if s0 < s1:
                    pieces.append((s0, s1, fl))
            for (s0, s1, fl) in pieces:
                if fl is None:
                    P.op("act", lambda e, s0=s0, s1=s1: e.activation(
                        out=AS[:, par, 1 + s0: 1 + s1], in_=PS[:, b * 512 + s0 - lo: b * 512 + s1 - lo], func=AF.Copy),
                        reads=pskeys(b), writes=as_keys(par, s0, s1))
                else:
                    P.op("act", lambda e, s0=s0, s1=s1, fl=fl: e.activation(
                        out=AS[:, par, 1 + s0: 1 + s1], in_=PS[:, b * 512 + s0 - lo: b * 512 + s1 - lo], func=AF.Copy,
                        scale=FLAGS[:, fl:fl + 1]),
                        reads=pskeys(b) + ["flags"], writes=as_keys(par, s0, s1))

        def phase_ffn(g, l, nt0=None):
            ci = g.ci
            tiles = even_tiles(*g.win_f[l])
            if not g.lat and l + 1 < DEPTH:
                ada_enqueue(l + 1)
            norm_mod(g, tiles, G2[:, l, ci, :], (lambda c: MOD[:, l, 24 + c: 25 + c, ci]),
                     [("g2", l), ("mod", l)], h_out, h_keys)
            pending_gelu = []

            def flush_gelu():
                for (t3, lo, hi, n, jj) in pending_gelu:
                    P.op("act", lambda e, t3=t3, lo=lo, hi=hi, n=n, jj=jj: e.activation(
                        out=HID[:, jj, lo:hi], in_=TMPR[:, t3, 0:n], func=AF.Gelu_apprx_tanh),
                        reads=[("tmp", t3)], writes=gran("hid", jj, lo, hi))
                del pending_gelu[:]

            for j in range(NJ):
                s = getw((wfin_d[l, j], 1024))
                par = j % 2
                c1slots = []
                for (lo, hi) in tiles:
                    n = hi - lo
                    b = ps_alloc()
                    fns = [(lambda e, kc=kc, b=b, lo=lo, hi=hi, n=n, s=s: e.matmul(
                        PS[:, b * 512: b * 512 + n], WSL[:, s, kc * 128:(kc + 1) * 128], H[:, kc, lo:hi],
                        start=(kc == 0), stop=(kc == 7))) for kc in range(8)]
                    rd = [("w", s)]
                    for kc in range(8):
                        rd += gran("h", kc, lo, hi)
                    P.mm(fns, reads=rd, writes=pskeys(b))
                    evac_a(g, b, par, lo, hi)
                    t1 = nxt("tmp", NTMP)
                    c1slots.append(t1)
                    P.op("act", lambda e, t1=t1, b=b, n=n, j=j: e.activation(
                        out=TMPR[:, t1, 0:n], in_=PS[:, b * 512: b * 512 + n], func=AF.Identity,
                        scale=CW[:, l, 1, j:j + 1], bias=CB[:, l, j:j + 1]),
                        reads=pskeys(b) + ["cw", "cb"], writes=[("tmp", t1)])
                flush_gelu()
                for ti, (lo, hi) in enumerate(tiles):
                    n = hi - lo
                    t3 = c1slots[ti]
                    P.op("dve", lambda e, t3=t3, lo=lo, hi=hi, n=n, j=j, par=par: e.scalar_tensor_tensor(
                        out=v3(TMPR[:, t3, 0:n], g, n), in0=as_view(g, par, lo, hi, -1), scalar=CW[:, l, 0, j:j + 1],
                        in1=v3(TMPR[:, t3, 0:n], g, n), op0=ALU.mult, op1=ALU.add),
                        reads=as_keys(par, lo - 1, hi) + [("tmp", t3), "cw"], writes=[("tmp", t3)])
                    P.op("dve", lambda e, t3=t3, lo=lo, hi=hi, n=n, j=j, par=par: e.scalar_tensor_tensor(
                        out=v3(TMPR[:, t3, 0:n], g, n), in0=as_view(g, par, lo, hi, 1), scalar=CW[:, l, 2, j:j + 1],
                        in1=v3(TMPR[:, t3, 0:n], g, n), op0=ALU.mult, op1=ALU.add),
                        reads=as_keys(par, lo, hi + 1) + [("tmp", t3), "cw"], writes=[("tmp", t3)])
                    pending_gelu.append((t3, lo, hi, n, j))
                ada_step()
            flush_gelu()
            for j in range(NJ):
                s = getw((wfin_d[l, 22 + j], 1024))
                for (lo, hi) in tiles:
                    n = hi - lo
                    b = ps_alloc()
                    fns = [(lambda e, kc=kc, b=b, lo=lo, hi=hi, n=n, s=s: e.matmul(
                        PS[:, b * 512: b * 512 + n], WSL[:, s, kc * 128:(kc + 1) * 128], H[:, kc, lo:hi],
                        start=(kc == 0), stop=(kc == 7))) for kc in range(8)]
                    rd = [("w", s)]
                    for kc in range(8):
                        rd += gran("h", kc, lo, hi)
                    P.mm(fns, reads=rd, writes=pskeys(b))
                    P.op("dve", lambda e, b=b, lo=lo, hi=hi, n=n, j=j: e.tensor_tensor(
                        out=HID[:, j, lo:hi], in0=HID[:, j, lo:hi], in1=PS[:, b * 512: b * 512 + n], op=ALU.mult),
                        reads=gran("hid", j, lo, hi) + pskeys(b), writes=gran("hid", j, lo, hi))
                ada_step()
            out_proj(tiles, (lambda nn: ((wfout_d[l, 2 * nn], 1408), (wfout_d[l, 2 * nn + 1], 1408))), NJ, 11,
                     (lambda kk, lo, hi: HID[:, kk, lo:hi]), (lambda kk, lo, hi: gran("hid", kk, lo, hi)), l, ci, 40,
                     nt0=nt0)
            ada_drain()

        def gmlp_weights(j):
            ws = [(awu_d[j, m], 1024) for m in range(16)]
            ws += [(awo_d[j, nn], 2048) for nn in range(NCH)]
            return ws

        def load_wv(j):
            for cb in range(4):
                P.dma("pool", (lambda e, cb=cb: e.dma_start(out=WV[:, cb * 4096:(cb + 1) * 4096], in_=awv_d[j, cb])),
                      reads=[], writes=["scr"])

        def phase_gmlp(g, l, nt0=None):
            ci = g.ci
            j = l // 3
            tiles = even_tiles(*g.win_m[l])
            norm_mod(g, tiles, G1[:, l, ci, :], (lambda c: MOD[:, l, c: c + 1, ci]),
                     [("g1", l), ("mod1", l)], h_out, h_keys)
            U = HID
            for m in range(16):
                s = getw((awu_d[j, m], 1024))
                for (lo, hi) in tiles:
                    n = hi - lo
                    b = ps_alloc()
                    fns = [(lambda e, kc=kc, b=b, lo=lo, hi=hi, n=n, s=s: e.matmul(
                        PS[:, b * 512: b * 512 + n], WSL[:, s, kc * 128:(kc + 1) * 128], H[:, kc, lo:hi],
                        start=(kc == 0), stop=(kc == 7))) for kc in range(8)]
                    rd = [("w", s)]
                    for kc in range(8):
                        rd += gran("h", kc, lo, hi)
                    P.mm(fns, reads=rd, writes=pskeys(b))
                    P.op("act", lambda e, b=b, lo=lo, hi=hi, n=n, m=m: e.activation(
                        out=U[:, m, lo:hi], in_=PS[:, b * 512: b * 512 + n], func=AF.Gelu_apprx_tanh),
                        reads=pskeys(b), writes=gran("hid", m, lo, hi))
                ada_step()
            ada_drain()
            HF = HID[:, 16:22, :].rearrange("p a b -> p (a b)")
            VGb = [HF[:, 0:2048], HF[:, 2048:4096]]
            WSQ = [HF[:, 4096:5120], HF[:, 5120:6144]]
            JUNK = TMPR[:, 0:2, :].rearrange("p a b -> p (a b)").bitcast(BF16)
            chunks = list(range(g.win_m[l][0] // 128, g.win_m[l][1] // 128))

            def emit_v(q):
                pp = q % 2
                c0 = q * 128
                b4 = ps_alloc(4)
                for cb in range(4):
                    fns = [(lambda e, kc=kc, cb=cb, b4=b4, c0=c0: e.matmul(
                        PS[:, (b4 + cb) * 512:(b4 + cb + 1) * 512], H[:, kc, c0:c0 + 128],
                        WV[:, cb * 4096 + kc * 512: cb * 4096 + (kc + 1) * 512],
                        start=(kc == 0), stop=(kc == 7))) for kc in range(8)]
                    rd = ["scr"]
                    for kc in range(8):
                        rd += gran("h", kc, c0, c0 + 128)
                    P.mm(fns, reads=rd, writes=pskeys(b4 + cb))
                P.op("act", lambda e, b4=b4, pp=pp: e.activation(out=VGb[pp], in_=PS[:, b4 * 512:(b4 + 4) * 512],
                                                                 func=AF.Gelu_apprx_tanh),
                     reads=pskeys(b4, 4), writes=[("vgb", pp)])
                P.op("act", lambda e, pp=pp: e.activation(out=JUNK, in_=VGb[pp], func=AF.Square,
                                                          accum_out=SS[:, pp * 4: pp * 4 + 1]),
                     reads=[("vgb", pp)], writes=[("tmp", 0), ("tmp", 1), ("ss", pp)])
                P.op("act", lambda e, pp=pp: e.activation(out=SS[:, pp * 4 + 1: pp * 4 + 2], in_=SS[:, pp * 4: pp * 4 + 1],
                                                          func=AF.Sqrt, scale=1.0 / 2048.0, bias=EPST[:, 0:1]),
                     reads=[("ss", pp), "epst"], writes=[("ss1", pp)])
                P.op("dve", lambda e, pp=pp: e.reciprocal(out=SS[:, pp * 4 + 2: pp * 4 + 3], in_=SS[:, pp * 4 + 1: pp * 4 + 2]),
                     reads=[("ss1", pp)], writes=[("ss2", pp)])
                P.op("dve", lambda e, pp=pp: e.tensor_scalar(out=WSQ[pp], in0=WST[:, j, :], scalar1=SS[:, pp * 4 + 2: pp * 4 + 3],
                                                             scalar2=None, op0=ALU.mult),
                     reads=[("ss2", pp), "wst"], writes=[("wsq", pp)])

            def emit_spatial(q):
                pp = q % 2
                c0 = q * 128
                for bq in range(4):
                    b = ps_alloc()
                    fns = []
                    for i in range(4):
                        fc = bq * 4 + i
                        fns.append(lambda e, fc=fc, i=i, b=b, pp=pp: e.matmul(
                            PS[:, b * 512 + i * 128: b * 512 + (i + 1) * 128], VGb[pp][:, fc * 128:(fc + 1) * 128],
                            WSQ[pp][:, (fc // 2) * 128:(fc // 2 + 1) * 128], start=True, stop=True))
                    P.mm(fns, reads=[("vgb", pp), ("wsq", pp)], writes=pskeys(b))
                    t = 4 + (bq % 2)
                    if bq % 2 == 0:
                        for i in range(4):
                            fc = bq * 4 + i
                            P.op("act", lambda e, fc=fc, i=i, b=b, t=t: e.activation(
                                out=TMPR[:, t, i * 128:(i + 1) * 128], in_=PS[:, b * 512 + i * 128: b * 512 + (i + 1) * 128],
                                func=AF.Copy, scale=AGV[:, j, fc:fc + 1]),
                                reads=pskeys(b) + ["agv"], writes=[("tmp", t)])
                        for gg in range(2):
                            P.op("dve", lambda e, gg=gg, bq=bq, t=t: e.tensor_tensor(
                                out=TMPR[:, t, gg * 256:(gg + 1) * 256].rearrange("p (r i) -> p r i", i=128),
                                in0=TMPR[:, t, gg * 256:(gg + 1) * 256].rearrange("p (r i) -> p r i", i=128),
                                in1=ABS[:, j, bq * 2 + gg, :].unsqueeze(1).broadcast_to([128, 2, 128]), op=ALU.add),
                                reads=[("tmp", t), "abs"], writes=[("tmp", t)])
                    else:
                        for i in range(4):
                            fc = bq * 4 + i
                            P.op("dve", lambda e, fc=fc, i=i, b=b, t=t: e.scalar_tensor_tensor(
                                out=TMPR[:, t, i * 128:(i + 1) * 128], in0=PS[:, b * 512 + i * 128: b * 512 + (i + 1) * 128],
                                scalar=AGV[:, j, fc:fc + 1], in1=ABS[:, j, fc // 2, :], op0=ALU.mult, op1=ALU.add),
                                reads=pskeys(b) + ["agv", "abs"], writes=[("tmp", t)])
                    P.op("dve", lambda e, bq=bq, t=t, c0=c0: e.tensor_tensor(
                        out=U[:, bq * 4:(bq + 1) * 4, c0:c0 + 128], in0=U[:, bq * 4:(bq + 1) * 4, c0:c0 + 128],
                        in1=TMPR[:, t, :].rearrange("p (a b) -> p a b", b=128), op=ALU.mult),
                        reads=[("tmp", t)] + [k for fc in range(bq * 4, bq * 4 + 4) for k in gran("hid", fc, c0, c0 + 128)],
                        writes=[k for fc in range(bq * 4, bq * 4 + 4) for k in gran("hid", fc, c0, c0 + 128)])

            emit_v(chunks[0])
            for qi, q in enumerate(chunks):
                if qi + 1 < len(chunks):
                    emit_v(chunks[qi + 1])
                emit_spatial(q)
            ring_state["tmp"] = 0
            out_proj(tiles, (lambda nn: ((awo_d[j, nn], 2048),)), 16, 16,
                     (lambda kk, lo, hi: U[:, kk, lo:hi]), (lambda kk, lo, hi: gran("hid", kk, lo, hi)), l, ci, 16,
                     nt0=nt0)

        PADW = 16

        def pool_weights():
            return [(pw_d[:, :], 2048)]

        def phase_pool(g, l):
            ci = g.ci
            tiles = even_tiles(*g.win_m[l])
            L = g.L
            Lp = L + 2 * PADW
            nseg = g.nseg
            H32f = H32.rearrange("p a b -> p (a b)")
            W = nseg * Lp

            def hb(c):
                return H32f[:, c * W:(c + 1) * W].rearrange("p (s l) -> p s l", l=Lp)

            SA = H32f[:, 8 * W: 9 * W].rearrange("p (s l) -> p s l", l=Lp)
            SB = H32f[:, 9 * W: 10 * W].rearrange("p (s l) -> p s l", l=Lp)
            wlo, whi = g.win_m[l]
            hall = H32f[:, 0:8 * W].rearrange("p (s l) -> p s l", l=Lp)
            zl = PADW + (wlo if nseg == 1 else 0)
            zr = PADW + (whi if nseg == 1 else L)
            hidk = [k for jj in range(NJ) for k in gran("hid", jj, 0, g.cols)]
            P.op("pool", lambda e: e.memset(hall[:, :, 0:zl], 0.0), reads=[], writes=["h32all"] + hidk)
            P.op("pool", lambda e: e.memset(hall[:, :, zr:Lp], 0.0), reads=[], writes=["h32all"] + hidk)
            P.op("pool", lambda e: e.memset(SA[:, :, 0:1], 0.0), reads=[], writes=[("sab", 0)])

            def h32_out(c, lo, hi):
                if nseg > 1:
                    return hb(c)[:, lo // L: hi // L, PADW:PADW + L]
                return hb(c)[:, 0, PADW + lo:PADW + hi]

            def tmp3(ap, n):
                return v3(ap, g, n)

            def nm_out(c, lo, hi):
                return h32_out(c, lo, hi)

            def nm_in_fix(fnlam):
                return fnlam

            norm_mod(g, tiles, G1[:, l, ci, :], (lambda c: MOD[:, l, c:c + 1, ci]),
                     [("g1", l), ("mod1", l), "h32all", "g1f"], h32_out, (lambda c, lo, hi: [("h32", c)]), tmp_view=tmp3,
                     flagged=((G1F, SH1F) if g.lat else None))
            s = getw((pw_d[:, :], 2048))
            IC = ICL if g.lat else ICC
            for c in range(NCH):
                gi = c // 2
                w = (2, 4, 8, 16)[gi]
                src = hb(c)
                bufs = [SA, SB]
                cur = src
                cur_key = ("h32", c)
                r0 = 1
                r1 = Lp
                steps = [(1, 0), (1, 1), (2, 2), (4, 4)][: gi + 1]
                for si, (sl, sr) in enumerate(steps):
                    dst = bufs[si % 2]
                    dkey = ("sab", si % 2)
                    a = r0 + sl if si > 0 else 1
                    bnd = (r1 - sr) if si > 0 else Lp
                    if si == 0:
                        a, bnd = 1, Lp
                    P.op("dve", lambda e, cur=cur, dst=dst, a=a, bnd=bnd, sl=sl, sr=sr: e.tensor_tensor(
                        out=dst[:, :, a:bnd], in0=cur[:, :, a - sl:bnd - sl], in1=cur[:, :, a + sr:bnd + sr], op=ALU.add),
                        reads=[cur_key], writes=[dkey])
                    cur = dst
                    cur_key = dkey
                    r0, r1 = a, bnd
                for (lo, hi) in tiles:
                    n = hi - lo
                    if nseg > 1:
                        sv = cur[:, lo // L: hi // L, PADW:PADW + L]
                    else:
                        sv = cur[:, 0, PADW + lo:PADW + hi]
                    P.op("dve", lambda e, sv=sv, c=c, lo=lo, hi=hi, n=n, w=w: e.scalar_tensor_tensor(
                        out=v3(H[:, c, lo:hi], g, n), in0=sv, scalar=1.0 / w, in1=h32_out(c, lo, hi),
                        op0=ALU.mult, op1=ALU.subtract),
                        reads=[cur_key, ("h32", c)], writes=gran("h", c, lo, hi))
                if nseg > 1:
                    locs = [(0, 0), (L - 16, 1)]
                    for (off, wh) in locs:
                        t = nxt("tmp", NTMP)
                        P.op("dve", lambda e, cur=cur, off=off, wh=wh, t=t, gi=gi: e.tensor_tensor(
                            out=TMPR[:, t, 0:nseg * 16].rearrange("p (s l) -> p s l", l=16),
                            in0=cur[:, :, PADW + off:PADW + off + 16],
                            in1=IC[:, wh, gi, :].unsqueeze(1).broadcast_to([128, nseg, 16]), op=ALU.mult),
                            reads=[cur_key, "icc"], writes=[("tmp", t)])
                        P.op("dve", lambda e, off=off, t=t, c=c: e.tensor_tensor(
                            out=H[:, c, 0:g.cols].rearrange("p (s l) -> p s l", l=L)[:, :, off:off + 16],
                            in0=TMPR[:, t, 0:nseg * 16].rearrange("p (s l) -> p s l", l=16),
                            in1=hb(c)[:, :, PADW + off:PADW + off + 16], op=ALU.subtract),
                            reads=[("tmp", t), ("h32", c)], writes=gran("h", c, 0, g.cols))
                else:
                    for (off, wh) in ((256, 0), (752, 1)):
                        t = nxt("tmp", NTMP)
                        P.op("dve", lambda e, cur=cur, off=off, wh=wh, t=t, gi=gi: e.tensor_tensor(
                            out=TMPR[:, t, 0:16], in0=cur[:, 0, PADW + off:PADW + off + 16],
                            in1=IC[:, wh, gi, :], op=ALU.mult),
                            reads=[cur_key, "icl"], writes=[("tmp", t)])
                        P.op("dve", lambda e, off=off, t=t, c=c: e.tensor_tensor(
                            out=H[:, c, off:off + 16], in0=TMPR[:, t, 0:16],
                            in1=hb(c)[:, 0, PADW + off:PADW + off + 16], op=ALU.subtract),
                            reads=[("tmp", t), ("h32", c)], writes=gran("h", c, off, off + 16))
            for oc8 in range(NCH):
                gi = oc8 // 2
                for (lo, hi) in tiles:
                    n = hi - lo
                    b = ps_alloc()
                    fns = [(lambda e, ic=ic, b=b, lo=lo, hi=hi, n=n, oc8=oc8, gi=gi: e.matmul(
                        PS[:, b * 512: b * 512 + n], WSL[:, s, (oc8 * 2 + ic) * 128:(oc8 * 2 + ic + 1) * 128],
                        H[:, gi * 2 + ic, lo:hi], start=(ic == 0), stop=(ic == 1))) for ic in range(2)]
                    rd = [("w", s)] + gran("h", gi * 2, lo, hi) + gran("h", gi * 2 + 1, lo, hi)
                    P.mm(fns, reads=rd, writes=pskeys(b))
                    t = nxt("tmp", NTMP)
                    P.op("dve", lambda e, b=b, t=t, n=n, oc8=oc8: e.tensor_scalar(
                        out=TMPR[:, t, 0:n], in0=PS[:, b * 512: b * 512 + n], scalar1=PA[:, ci, oc8:oc8 + 1],
                        scalar2=PBB[:, ci, oc8:oc8 + 1], op0=ALU.mult, op1=ALU.add),
                        reads=pskeys(b) + ["pa", "pbb"], writes=[("tmp", t)])
                    P.op("dve", lambda e, t=t, lo=lo, hi=hi, n=n, oc8=oc8: e.tensor_tensor(
                        out=X[:, oc8, lo:hi], in0=X[:, oc8, lo:hi], in1=TMPR[:, t, 0:n], op=ALU.add),
                        reads=[("tmp", t)] + gran("x", oc8, lo, hi), writes=gran("x", oc8, lo, hi))

        def attn_weights(g):
            ws = [(wq_d[c], 1024) for c in range(8)]
            ws += [(wk_d[c], 1024) for c in range(4)]
            ws += [(wvt_d[:, :], 2048)]
            if not g.lat:
                ws += [(wvf_d[c], 1024) for c in range(2)]
            ws += [(wo_d[c], 1024) for c in range(8)]
            return ws

        def phase_attn(g, l, nt0=None):
            ci = g.ci
            qtiles = even_tiles(*g.qwin)
            norm_mod(g, g.tiles, G1[:, l, ci, :], (lambda c: MOD[:, l, c: c + 1, ci]),
                     [("g1", l), ("mod1", l)], h_out, h_keys)
            QT = HID
            KT0 = 8
            O0 = 12
            nkt_lat = g.cols // 128
            VE3 = VE.rearrange("p (t k d) -> p t k d", k=4, d=192)
            P.op("pool", lambda e: e.memset(VE, 1.0), reads=[], writes=["ve", "scr"])
            if g.lat:
                P.dma("sp", lambda e: e.dma_start(out=COS, in_=cos_d), reads=["scr"], writes=["cos"])
                P.dma("sp", lambda e: e.dma_start(out=SIN, in_=sin_d), reads=["scr"], writes=["sin"])
                P.dma("pool", lambda e: e.dma_start(out=HID[:, 20, 0:1024], in_=ck_d.rearrange("p a b -> p (a b)")),
                      reads=[], writes=["ckt"] + gran("hid", 20, 0, 1024))
                for kt in range(2):
                    P.dma("pool", lambda e, kt=kt: e.dma_start(
                        out=VE3[:, 9 + kt, :, 64:128], in_=cv_d[:, kt, :].rearrange("p (k d) -> p k d", d=64)),
                        reads=["ve", "scr"], writes=[("vet", 9 + kt)])

            qk_units = [("q", c, ti, lo, hi) for c in range(8) for ti, (lo, hi) in enumerate(qtiles)]
            qk_units += [("k", c, ti, lo, hi) for c in range(4) for ti, (lo, hi) in enumerate(g.tiles)]
            qst = {}
            qslot = {}

            def qk_A(u):
                kind, c, ti, lo, hi = qk_units[u]
                n = hi - lo
                if ti == 0:
                    qslot[(kind, c)] = getw(((wq_d if kind == "q" else wk_d)[c], 1024))
                s = qslot[(kind, c)]
                b = ps_alloc()
                fns = [(lambda e, kc=kc, b=b, lo=lo, hi=hi, n=n, s=s: e.matmul(
                    PS[:, b * 512: b * 512 + n], WSL[:, s, kc * 128:(kc + 1) * 128], H[:, kc, lo:hi],
                    start=(kc == 0), stop=(kc == 7))) for kc in range(8)]
                rd = [("w", s)]
                for kc in range(8):
                    rd += gran("h", kc, lo, hi)
                P.mm(fns, reads=rd, writes=pskeys(b))
                sq = nxt("sq", NSQ)
                P.op("act", lambda e, sq=sq, b=b, n=n: e.activation(out=SQ[:, sq, 0:n], in_=PS[:, b * 512: b * 512 + n],
                                                                    func=AF.Square),
                     reads=pskeys(b), writes=[("sq", sq)])
                qst[u] = {"b": b, "sq": sq}

            def qk_B(u):
                kind, c, ti, lo, hi = qk_units[u]
                n = hi - lo
                dst = c if kind == "q" else KT0 + c
                gsc = GQ if kind == "q" else GK
                b = qst[u]["b"]
                sq = qst[u]["sq"]
                b2 = ps_alloc()
                P.mm([lambda e, sq=sq, b2=b2, n=n: e.matmul(PS[:, b2 * 512: b2 * 512 + n], BLK[:, :], SQ[:, sq, 0:n],
                                                            start=True, stop=True)],
                     reads=[("sq", sq), "blk"], writes=pskeys(b2))
                r = nxt("rs", NRS)
                P.op("act", lambda e, r=r, b2=b2, n=n: e.activation(
                    out=RS[:, r, 0:n], in_=PS[:, b2 * 512: b2 * 512 + n], func=AF.Ln, scale=1.0 / 64.0,
                    bias=EPST[:, 0:1]), reads=pskeys(b2) + ["epst"], writes=[("rs", r)])
                P.op("act", lambda e, r=r, n=n: e.activation(out=RS[:, r, 0:n], in_=RS[:, r, 0:n], func=AF.Exp,
                                                             scale=-0.5), reads=[("rs", r)], writes=[("rs", r)])
                if not g.lat and kind == "q":
                    P.op("dve", lambda e, b=b, r=r, lo=lo, hi=hi, n=n, dst=dst, gsc=gsc: e.scalar_tensor_tensor(
                        out=HID[:, dst, lo:hi], in0=PS[:, b * 512: b * 512 + n], scalar=gsc[:, 0:1],
                        in1=RS[:, r, 0:n], op0=ALU.mult, op1=ALU.mult),
                        reads=pskeys(b) + [("rs", r), "gq"], writes=gran("hid", dst, lo, hi))
                    return
                t = nxt("tmp", NTMP)
                P.op("dve", lambda e, b=b, r=r, t=t, n=n, gsc=gsc: e.scalar_tensor_tensor(
                    out=TMPR[:, t, 0:n], in0=PS[:, b * 512: b * 512 + n], scalar=gsc[:, 0:1],
                    in1=RS[:, r, 0:n], op0=ALU.mult, op1=ALU.mult),
                    reads=pskeys(b) + [("rs", r), "gq", "gk"], writes=[("tmp", t)])
                if not g.lat:
                    P.op("act", lambda e, t=t, lo=lo, hi=hi, n=n, dst=dst: e.activation(
                        out=HID[:, dst, lo:hi], in_=TMPR[:, t, 0:n], func=AF.Copy),
                        reads=[("tmp", t)], writes=gran("hid", dst, lo, hi))
                    P.dma("sp", lambda e, t=t, c=c, lo=lo, hi=hi, n=n: e.dma_start(
                        out=nk_d[c * 64:(c + 1) * 64, lo:hi], in_=TMPR[0:64, t, 0:n]),
                        reads=[("tmp", t)], writes=[])
                    return
                sq2 = nxt("sq", NSQ)
                P.op("act", lambda e, t=t, sq2=sq2, n=n: e.activation(out=SQ[:, sq2, 0:n], in_=TMPR[:, t, 0:n],
                                                                      func=AF.Copy),
                     reads=[("tmp", t)], writes=[("sq", sq2)])
                qst[u]["t"] = t
                qst[u]["sq2"] = sq2

            def qk_C(u):
                kind, c, ti, lo, hi = qk_units[u]
                n = hi - lo
                dst = c if kind == "q" else KT0 + c
                t = qst[u]["t"]
                sq2 = qst[u]["sq2"]
                b3 = ps_alloc()
                P.mm([lambda e, sq2=sq2, b3=b3, n=n: e.matmul(PS[:, b3 * 512: b3 * 512 + n], PM[:, :],
                                                              SQ[:, sq2, 0:n], start=True, stop=True)],
                     reads=[("sq", sq2), "pm"], writes=pskeys(b3))
                t2 = nxt("tmp", NTMP)
                P.op("dve", lambda e, b3=b3, t2=t2, lo=lo, hi=hi, n=n: e.tensor_tensor(
                    out=TMPR[:, t2, 0:n], in0=PS[:, b3 * 512: b3 * 512 + n], in1=SIN[:, lo:hi], op=ALU.mult),
                    reads=pskeys(b3) + ["sin", "scr"], writes=[("tmp", t2)])
                P.op("pool", lambda e, t=t, lo=lo, hi=hi, n=n: e.tensor_tensor(
                    out=TMPR[:, t, 0:n], in0=TMPR[:, t, 0:n], in1=COS[:, lo:hi], op=ALU.mult),
                    reads=[("tmp", t), "cos", "scr"], writes=[("tmp", t)])
                P.op("dve", lambda e, t=t, t2=t2, lo=lo, hi=hi, n=n, dst=dst: e.tensor_tensor(
                    out=HID[:, dst, lo:hi], in0=TMPR[:, t, 0:n], in1=TMPR[:, t2, 0:n], op=ALU.add),
                    reads=[("tmp", t), ("tmp", t2)], writes=gran("hid", dst, lo, hi))

            nu = len(qk_units)
            for i in range(nu + 2):
                if i < nu:
                    qk_A(i)
                if 0 <= i - 1 < nu:
                    qk_B(i - 1)
                if g.lat and 0 <= i - 2 < nu:
                    qk_C(i - 2)
            s = getw((wvt_d[:, :], 2048))
            for q in range(nkt_lat):
                c0 = q * 128
                b = ps_alloc()
                fns = [(lambda e, kc=kc, b=b, c0=c0, s=s: e.matmul(
                    PS[:, b * 512: b * 512 + 256], H[:, kc, c0:c0 + 128], WSL[:, s, kc * 256:(kc + 1) * 256],
                    start=(kc == 0), stop=(kc == 7))) for kc in range(8)]
                rd = [("w", s)]
                for kc in range(8):
                    rd += gran("h", kc, c0, c0 + 128)
                P.mm(fns, reads=rd, writes=pskeys(b))
                P.op("act", lambda e, b=b, q=q: e.activation(
                    out=VE3[:, q, :, 64:128], in_=PS[:, b * 512: b * 512 + 256].rearrange("p (k d) -> p k d", d=64),
                    func=AF.Copy), reads=pskeys(b) + ["ve", "scr"], writes=[("vet", q)])
            if not g.lat:
                for c in range(2):
                    s = getw((wvf_d[c], 1024))
                    for (lo, hi) in g.tiles:
                        n = hi - lo
                        b = ps_alloc()
                        fns = [(lambda e, kc=kc, b=b, lo=lo, hi=hi, n=n, s=s: e.matmul(
                            PS[:, b * 512: b * 512 + n], WSL[:, s, kc * 128:(kc + 1) * 128], H[:, kc, lo:hi],
                            start=(kc == 0), stop=(kc == 7))) for kc in range(8)]
                        rd = [("w", s)]
                        for kc in range(8):
                            rd += gran("h", kc, lo, hi)
                        P.mm(fns, reads=rd, writes=pskeys(b))
                        t = nxt("tmp", NTMP)
                        P.op("act", lambda e, b=b, t=t, n=n: e.activation(out=TMPR[:, t, 0:n], in_=PS[:, b * 512: b * 512 + n],
                                                                          func=AF.Copy),
                             reads=pskeys(b), writes=[("tmp", t)])
                        P.dma("sp", lambda e, t=t, c=c, lo=lo, hi=hi, n=n: e.dma_start(
                            out=nv_d[c * 128:(c + 1) * 128, lo:hi], in_=TMPR[:, t, 0:n]),
                            reads=[("tmp", t)], writes=[])

            PT2 = PT.rearrange("p (s c) -> p s c", s=2)
            if DEBUG_STOP is not None and DEBUG_STOP[1] == "attn_ve":
                Xf = X[:, :, :].rearrange("p a b -> p (a b)")
                P.op("dve", lambda e: e.tensor_copy(out=Xf[:, 0:8448], in_=VE),
                     reads=[("vet", t_) for t_ in range(11)] + ["ve"], writes=[k for c in range(8) for k in gran("x", c, 0, g.cols)])
                return

            def ve_lhs(tile, kv, h):
                if h % 2 == 0:
                    return VE3[:, tile, kv, 64:192]
                return VE3[:, tile, kv, 0:128]

            def normalize(b, h, qlo, nq):
                off = (h % 2) * 64
                dn = 64 - off
                t = nxt("tmp", NTMP)
                P.op("dve", lambda e, b=b, t=t, nq=nq, h=h, off=off, dn=dn: e.tensor_scalar(
                    out=TMPR[off:off + 64, t, 0:nq], in0=PS[dn:dn + 64, b * 512: b * 512 + nq],
                    scalar1=ESK[dn:dn + 64, h:h + 1], scalar2=None, op0=ALU.add),
                    reads=pskeys(b) + ["esk"], writes=[("tmp", t)])
                P.op("act", lambda e, t=t, nq=nq, off=off: e.activation(
                    out=TMPR[off:off + 64, t, 0:nq], in_=TMPR[off:off + 64, t, 0:nq], func=AF.Ln),
                    reads=[("tmp", t)], writes=[("tmp", t)])
                P.op("act", lambda e, t=t, nq=nq, off=off: e.activation(
                    out=TMPR[off:off + 64, t, 0:nq], in_=TMPR[off:off + 64, t, 0:nq], func=AF.Exp, scale=-1.0),
                    reads=[("tmp", t)], writes=[("tmp", t)])
                P.op("dve", lambda e, b=b, t=t, nq=nq, h=h, off=off, qlo=qlo: e.tensor_tensor(
                    out=HID[off:off + 64, O0 + h // 2, qlo:qlo + nq], in0=PS[off:off + 64, b * 512: b * 512 + nq],
                    in1=TMPR[off:off + 64, t, 0:nq], op=ALU.mult),
                    reads=pskeys(b) + [("tmp", t)], writes=gran("hid", O0 + h // 2, qlo, qlo + nq))

            if not g.lat:
                units = [(sq_i, h) for sq_i in range(4) for h in range(16)]
                st = {}

                def c_s1(u):
                    sq_i, h = units[u]
                    base = sq_i * 256
                    kv = h // 4
                    off = (h % 2) * 64
                    b = ps_alloc()
                    fns = [(lambda e, kt=kt, b=b, off=off, kv=kv, h=h, base=base: e.matmul(
                        PS[:, b * 512 + kt * 256: b * 512 + (kt + 1) * 256],
                        HID[off:off + 64, KT0 + kv, base + kt * 128: base + (kt + 1) * 128],
                        HID[off:off + 64, h // 2, base:base + 256], start=True, stop=True)) for kt in range(2)]
                    rd = gran("hid", KT0 + kv, base, base + 256) + gran("hid", h // 2, base, base + 256)
                    P.mm(fns, reads=rd, writes=pskeys(b))
                    ps_ = nxt("pt", 2)
                    P.op("act", lambda e, b=b, ps_=ps_: e.activation(
                        out=PT2[:, ps_, 0:512], in_=PS[:, b * 512:(b + 1) * 512], func=AF.Exp, scale=0.125),
                        reads=pskeys(b) + ["scr"], writes=[("pt", ps_, 0), "cos", "sin"])
                    st[u] = ps_

                def c_s2(u):
                    sq_i, h = units[u]
                    base = sq_i * 256
                    kv = h // 4
                    ps_ = st.pop(u)
                    b2 = ps_alloc()
                    fns = [(lambda e, kt=kt, b2=b2, kv=kv, ps_=ps_, sq_i=sq_i, h=h: e.matmul(
                        PS[:, b2 * 512: b2 * 512 + 256], ve_lhs(sq_i * 2 + kt, kv, h),
                        PT2[:, ps_, kt * 256:(kt + 1) * 256], start=(kt == 0), stop=(kt == 1))) for kt in range(2)]
                    P.mm(fns, reads=[("pt", ps_, 0), ("vet", sq_i * 2), ("vet", sq_i * 2 + 1), "scr"], writes=pskeys(b2))
                    normalize(b2, h, base, 256)

                c_s1(0)
                for u in range(len(units)):
                    if u + 1 < len(units):
                        c_s1(u + 1)
                    c_s2(u)
            else:
                CKT = HID[:, 20, 0:1024].rearrange("p (k t) -> p k t", t=256)
                units = [(n0, nbk, h) for (n0, nbk) in ((1, 4), (5, 3)) for h in range(16)]
                st = {}

                def l_s1a(u):
                    n0, nbk, h = units[u]
                    qlo = n0 * 128
                    nq = nbk * 128
                    kv = h // 4
                    off = (h % 2) * 64
                    ps_ = nxt("pt", 2)
                    st[u] = ps_
                    for rel in range(3):
                        b = ps_alloc()
                        fns = []
                        for i in range(nbk):
                            kt = n0 + i - 1 + rel
                            fns.append(lambda e, i=i, kt=kt, b=b, off=off, kv=kv, h=h, n0=n0: e.matmul(
                                PS[:, b * 512 + i * 128: b * 512 + (i + 1) * 128],
                                HID[off:off + 64, KT0 + kv, kt * 128:(kt + 1) * 128],
                                HID[off:off + 64, h // 2, (n0 + i) * 128:(n0 + i + 1) * 128], start=True, stop=True))
                        rd = gran("hid", KT0 + kv, (n0 - 1 + rel) * 128, (n0 + nbk - 1 + rel) * 128) + \
                            gran("hid", h // 2, qlo, qlo + nq)
                        P.mm(fns, reads=rd, writes=pskeys(b))
                        runs = []
                        for i in range(nbk):
                            kt = n0 + i - 1 + rel
                            fl = 0 if kt < 2 else (1 if kt >= 6 else None)
                            if runs and runs[-1][2] == fl:
                                runs[-1][1] = i + 1
                            else:
                                runs.append([i, i + 1, fl])
                        for (i0, i1, fl) in runs:
                            bias = ZB[:, 0:1] if fl is None else NEGB[:, fl:fl + 1]
                            P.op("act", lambda e, b=b, i0=i0, i1=i1, rel=rel, ps_=ps_, bias=bias: e.activation(
                                out=PT2[:, ps_, rel * 512 + i0 * 128: rel * 512 + i1 * 128],
                                in_=PS[:, b * 512 + i0 * 128: b * 512 + i1 * 128], func=AF.Exp,
                                scale=0.125, bias=bias),
                                reads=pskeys(b) + ["negb", "zb", "scr"], writes=[("pt", ps_, rel), "cos", "sin"])
                        if rel != 1:
                            mi = 0 if rel == 0 else 1
                            P.op("dve", lambda e, rel=rel, ps_=ps_, mi=mi, nq=nq, nbk=nbk: e.tensor_tensor(
                                out=PT2[:, ps_, rel * 512: rel * 512 + nq].rearrange("p (a b) -> p a b", b=128),
                                in0=PT2[:, ps_, rel * 512: rel * 512 + nq].rearrange("p (a b) -> p a b", b=128),
                                in1=WMASK[:, mi, :].unsqueeze(1).broadcast_to([128, nbk, 128]), op=ALU.mult),
                                reads=[("pt", ps_, rel), "wmask"], writes=[("pt", ps_, rel)])

                def l_s1b(u):
                    n0, nbk, h = units[u]
                    qlo = n0 * 128
                    nq = nbk * 128
                    kv = h // 4
                    off = (h % 2) * 64
                    ps_ = st[u]
                    for kt in range(2):
                        b = ps_alloc()
                        P.mm([lambda e, kt=kt, b=b, off=off, kv=kv, h=h, nq=nq, qlo=qlo: e.matmul(
                            PS[:, b * 512: b * 512 + nq], CKT[off:off + 64, kv, kt * 128:(kt + 1) * 128],
                            HID[off:off + 64, h // 2, qlo:qlo + nq], start=True, stop=True)],
                            reads=["ckt"] + gran("hid", h // 2, qlo, qlo + nq), writes=pskeys(b))
                        P.op("act", lambda e, b=b, kt=kt, ps_=ps_, nq=nq: e.activation(
                            out=PT2[:, ps_, (3 + kt) * 512:(3 + kt) * 512 + nq], in_=PS[:, b * 512: b * 512 + nq],
                            func=AF.Exp, scale=0.125), reads=pskeys(b) + ["scr"],
                            writes=[("pt", ps_, 3 + kt), "cos", "sin"])

                def l_s2(u):
                    n0, nbk, h = units[u]
                    qlo = n0 * 128
                    nq = nbk * 128
                    kv = h // 4
                    ps_ = st.pop(u)
                    b2 = ps_alloc()
                    fns = []
                    for kt in range(2):
                        fns.append(lambda e, kt=kt, b2=b2, kv=kv, ps_=ps_, h=h, nq=nq: e.matmul(
                            PS[:, b2 * 512: b2 * 512 + nq], ve_lhs(9 + kt, kv, h),
                            PT2[:, ps_, (3 + kt) * 512:(3 + kt) * 512 + nq], start=(kt == 0), stop=False))
                    for rel in range(3):
                        for i in range(nbk):
                            kt = n0 + i - 1 + rel
                            fns.append(lambda e, i=i, kt=kt, rel=rel, b2=b2, kv=kv, ps_=ps_, h=h, nbk=nbk: e.matmul(
                                PS[:, b2 * 512 + i * 128: b2 * 512 + (i + 1) * 128], ve_lhs(kt, kv, h),
                                PT2[:, ps_, rel * 512 + i * 128: rel * 512 + (i + 1) * 128], start=False,
                                stop=(rel == 2 and i == nbk - 1)))
                    rd = [("pt", ps_, r_) for r_ in range(5)] + [("vet", t_) for t_ in range(11)] + ["scr"]
                    P.mm(fns, reads=rd, writes=pskeys(b2))
                    normalize(b2, h, qlo, nq)

                l_s1a(0)
                l_s1b(0)
                for u in range(len(units)):
                    if u + 1 < len(units):
                        l_s1a(u + 1)
                    l_s2(u)
                    if u + 1 < len(units):
                        l_s1b(u + 1)
            if DEBUG_STOP is not None and DEBUG_STOP[1] in ("attn_o", "attn_q", "attn_k"):
                src0 = {"attn_o": O0, "attn_q": 0, "attn_k": KT0}[DEBUG_STOP[1]]
                for c in range(8 if src0 != KT0 else 4):
                    P.op("dve", lambda e, c=c: e.tensor_copy(out=X[:, c, 0:g.cols], in_=HID[:, src0 + c, 0:g.cols]),
                         reads=gran("hid", src0 + c, 0, g.cols), writes=gran("x", c, 0, g.cols))
                return
            out_proj(qtiles, (lambda nn: ((wo_d[nn], 1024),)), 8, 8,
                     (lambda kk, lo, hi: HID[:, O0 + kk, lo:hi]), (lambda kk, lo, hi: gran("hid", O0 + kk, lo, hi)),
                     l, ci, 16, nt0=nt0)

        groups = [GC] + ([GL] if RUN_LAT else [])

        def run_all():
            ps_state["i"] = 0
            for k_ in ring_state:
                ring_state[k_] = 0
            del ada_pending[:]
            phase_consts()
            ada_enqueue(0)
            for _ in range(8):
                ada_step()
            load_wv(0)
            for gi_, g in enumerate(groups):
                if gi_ > 0:
                    phase_load_x(g)
                stop = False
                def mixer_tile0(l_):
                    if l_ >= DEPTH:
                        return None
                    if l_ % 3 == 2:
                        return g.tiles[0]
                    return even_tiles(*g.win_m[l_])[0]

                for l in range(DEPTH):
                    kind = l % 3
                    f_t0 = even_tiles(*g.win_f[l])[0]
                    if kind == 0:
                        phase_gmlp(g, l, nt0=f_t0)
                    elif kind == 1:
                        phase_pool(g, l)
                    else:
                        phase_attn(g, l, nt0=f_t0)
                    if DEBUG_STOP is not None and DEBUG_STOP[0] == l and DEBUG_STOP[1] != "ffn":
                        stop = True
                        break
                    if l == 2:
                        load_wv(1)
                    if l == 3 and gi_ + 1 < len(groups):
                        load_wv(0)
                    phase_ffn(g, l, nt0=(None if DEBUG_STOP is not None else mixer_tile0(l + 1)))
                    if DEBUG_STOP == (l, "ffn"):
                        stop = True
                        break
                ada_drain()
                phase_store_x(g)

        P.dry = True
        run_all()
        P.dry = False
        run_all()
        assert wstate["used"] == len(wstate["list"]), (wstate["used"], len(wstate["list"]))

        sem_names = ["pe", "act", "dve", "pool"]
        sems = {}
        for nme in sem_names:
            sems[nme] = es.enter_context(nc.semaphore("s_" + nme))
        for qn in ("sp", "pool"):
            for i in range(NRING):
                sems[(qn, i)] = es.enter_context(nc.semaphore(f"d_{qn}_{i}"))
        print("sbuf bytes remaining", nc.sbuf_bytes_remaining, "counts", P.cnt, flush=True)
        block = es.enter_context(nc.Block())

        def runner(engname):
            def run(e):
                for item in P.q[engname]:
                    if item[0] == "wait":
                        e.wait_ge(sems[item[1]], item[2])
                    elif item[0] == "op":
                        ins = item[1](e)
                        if item[2]:
                            ins.then_inc(sems[engname], 1)
                    else:
                        ins = item[1](e)
                        ins.then_inc(sems[item[2]], 16)
                if engname in ("sp", "pool"):
                    for i in range(NRING):
                        v = P.ring[engname][i]
                        if v:
                            e.wait_ge(sems[(engname, i)], v)
            return run

        block.sync(runner("sp"))
        block.gpsimd(runner("pool"))
        block.scalar(runner("act"))
        block.vector(runner("dve"))
        block.tensor(runner("pe"))
    return nc


def _tile_w(w, kdim_chunks, cols_per):
    K, N = w.shape
    kc = K // 128
    nb = N // cols_per
    t = w.reshape(kc, 128, nb, cols_per).transpose(2, 1, 0, 3)
    return np.ascontiguousarray(t).reshape(nb, 128, kc * cols_per)


def _pp(v):
    return np.ascontiguousarray(v.reshape(-1, 128).T)


_NC_CACHE = {}


def kernel(x_prompt, x_sample, cache_k, cache_v, c, c_ctx,
           w_ada, b_ada, g_mix, g_ffn, w_ffn_in, ffn_conv_w, ffn_conv_b, w_ffn_out,
           a_w_in, a_g_v, a_w_s, a_b_s, a_w_out,
           p_w, p_b, p_scale,
           c_w_qkv, c_g_q, c_g_k, c_sink, c_w_o):
    f = np.float32
    A = lambda z: np.ascontiguousarray(np.asarray(z, dtype=f))
    x_prompt, x_sample, cache_k, cache_v, c, c_ctx = map(A, (x_prompt, x_sample, cache_k, cache_v, c, c_ctx))
    w_ada, b_ada, g_mix, g_ffn, w_ffn_in = map(A, (w_ada, b_ada, g_mix, g_ffn, w_ffn_in))
    ffn_conv_w, ffn_conv_b, w_ffn_out = map(A, (ffn_conv_w, ffn_conv_b, w_ffn_out))
    a_w_in, a_g_v, a_w_s, a_b_s, a_w_out = map(A, (a_w_in, a_g_v, a_w_s, a_b_s, a_w_out))
    p_w, p_b, p_scale = map(A, (p_w, p_b, p_scale))
    c_w_qkv, c_g_q, c_g_k, c_sink, c_w_o = map(A, (c_w_qkv, c_g_q, c_g_k, c_sink, c_w_o))

    shared = {}
    shared["w_ada_t"] = np.stack([_tile_w(w_ada[l], 8, 128).reshape(24, 2, 128, 1024).transpose(0, 2, 1, 3)
                                  .reshape(24, 128, 2048) for l in range(DEPTH)])
    ba = b_ada.reshape(DEPTH, 48, 128).transpose(2, 0, 1)
    shared["b_ada_t"] = np.ascontiguousarray(np.repeat(ba[:, :, :, None], 2, axis=3))
    shared["g_mix_t"] = np.ascontiguousarray(g_mix.reshape(DEPTH, 8, 128).transpose(2, 0, 1))
    shared["g_ffn_t"] = np.ascontiguousarray(g_ffn.reshape(DEPTH, 8, 128).transpose(2, 0, 1))
    shared["w_ffn_in_t"] = np.stack([_tile_w(w_ffn_in[l], 8, 128) for l in range(DEPTH)])
    shared["conv_w_t"] = np.ascontiguousarray(ffn_conv_w.reshape(DEPTH, 3, NJ, 128).transpose(3, 0, 1, 2))
    shared["conv_b_t"] = np.ascontiguousarray(ffn_conv_b.reshape(DEPTH, NJ, 128).transpose(2, 0, 1))
    wo = []
    for l in range(DEPTH):
        t = w_ffn_out[l].reshape(NJ, 128, 8, 128).transpose(2, 1, 0, 3)
        t = t.reshape(8, 128, 2, 11 * 128).transpose(0, 2, 1, 3).reshape(16, 128, 11 * 128)
        wo.append(t)
    shared["w_ffn_out_t"] = np.ascontiguousarray(np.stack(wo))
    shared["a_w_u_t"] = np.stack([_tile_w(a_w_in[j][:, :2048], 8, 128) for j in range(2)])
    shared["a_w_v_t"] = np.stack([_tile_w(a_w_in[j][:, 2048:], 8, 512) for j in range(2)])
    shared["a_g_v_t"] = np.ascontiguousarray(a_g_v.reshape(2, 16, 128).transpose(2, 0, 1))
    shared["a_w_s_t"] = np.ascontiguousarray(a_w_s.transpose(0, 3, 1, 2)).reshape(2, 128, 8 * 128)
    shared["a_b_s_t"] = np.ascontiguousarray(np.broadcast_to(a_b_s[None], (128, 2, 8, 128)))
    shared["a_w_out_t"] = np.stack([np.ascontiguousarray(a_w_out[j].reshape(16, 128, 8, 128).transpose(2, 1, 0, 3))
                                    .reshape(8, 128, 16 * 128) for j in range(2)])
    pw = p_w[0].reshape(4, 2, 128, 2, 128).transpose(2, 0, 3, 1, 4)
    shared["p_w_t"] = np.ascontiguousarray(pw).reshape(128, 8 * 2 * 128)
    shared["p_b_t"] = _pp(p_b[0])
    shared["p_scale_t"] = _pp(p_scale[0])
    wqkv = c_w_qkv[0]
    shared["c_w_q_t"] = _tile_w(wqkv[:, :1024], 8, 128)
    kd = np.concatenate([np.concatenate([wqkv[:, 1024 + kh * 64:1024 + (kh + 1) * 64]] * 2, axis=1) for kh in range(4)],
                        axis=1)
    shared["c_w_k_t"] = _tile_w(np.ascontiguousarray(kd), 8, 128)
    shared["c_w_vt_t"] = _tile_w(np.ascontiguousarray(wqkv[:, 1280:1536]), 8, 256)[0]
    shared["c_w_vf_t"] = _tile_w(np.ascontiguousarray(wqkv[:, 1280:1536]), 8, 128)
    shared["c_w_o_t"] = _tile_w(c_w_o[0], 8, 128)
    shared["c_g_q_t"] = np.ascontiguousarray(np.tile(c_g_q[0], 2)[:, None])
    shared["c_g_k_t"] = np.ascontiguousarray(np.tile(c_g_k[0], 2)[:, None])
    shared["c_sink_t"] = np.ascontiguousarray(np.broadcast_to(c_sink[0][None], (128, 16)))
    shared["ones_t"] = np.ones((128, 128), f)
    blk = np.zeros((128, 128), f)
    blk[:64, :64] = 1
    blk[64:, 64:] = 1
    shared["blk_t"] = blk
    pm = np.zeros((128, 128), f)
    for m in range(128):
        if m % 32 < 16:
            pm[m + 16, m] = -1.0
        else:
            pm[m - 16, m] = 1.0
    shared["pm_t"] = pm
    kk = np.arange(128)[:, None]
    qq = np.arange(128)[None, :]
    shared["wmask_t"] = np.ascontiguousarray(np.stack([(kk >= qq), (kk <= qq)], axis=1).astype(f))

    def ic_tables(S):
        out = np.zeros((2, 4, 16), f)
        for gi, w in enumerate((2, 4, 8, 16)):
            half = w // 2
            for i in range(16):
                t = i
                lo = max(t - half, 0); hi = min(t + half - 1, S - 1)
                out[0, gi, i] = np.float32(1.0) / np.float32(hi - lo + 1)
                t = S - 16 + i
                lo = max(t - half, 0); hi = min(t + half - 1, S - 1)
                out[1, gi, i] = np.float32(1.0) / np.float32(hi - lo + 1)
        return out

    ic_c = ic_tables(256)
    ic_l = ic_tables(2048)
    mid = np.zeros((4, 16), f)
    for gi, w in enumerate((2, 4, 8, 16)):
        mid[gi, :] = np.float32(1.0) / np.float32(w)
    shared["ic_ctx_t"] = np.ascontiguousarray(np.broadcast_to(ic_c[None], (128, 2, 4, 16)))

    inv = (10000.0 ** (-np.arange(16, dtype=np.float32) / 16)).astype(f)

    in_maps = []
    for core in range(8):
        m = dict(shared)
        b_lat = core // 4
        cq = core % 4
        xc = x_prompt[4 * core:4 * core + 4].reshape(1024, D)
        m["xc"] = np.ascontiguousarray(xc.T.reshape(8, 128, 1024).transpose(1, 0, 2))
        start = 512 * cq - 256
        pos = start + np.arange(LAT_COLS)
        valid = (pos >= 0) & (pos < 2048)
        xl = np.zeros((LAT_COLS, D), f)
        xl[valid] = x_sample[b_lat][pos[valid]]
        m["xl"] = np.ascontiguousarray(xl.T.reshape(8, 128, LAT_COLS).transpose(1, 0, 2))
        cond2 = np.stack([c_ctx, c[b_lat]], axis=1)
        m["condT"] = np.ascontiguousarray(cond2.reshape(8, 128, 2).transpose(1, 0, 2))
        row = (pos // 64).astype(f)
        col = (pos % 64).astype(f)
        p = np.arange(128)
        fr = p % 16
        is_col = ((p % 64) // 32) == 1
        ang = np.where(is_col[:, None], col[None, :] * inv[fr][:, None], row[None, :] * inv[fr][:, None]).astype(f)
        m["cos_t"] = np.cos(ang).astype(f)
        m["sin_t"] = np.sin(ang).astype(f)
        fl = np.array([1.0 if cq > 0 else 0.0, 1.0 if cq < 3 else 0.0], f)
        m["flags_t"] = np.ascontiguousarray(np.broadcast_to(fl[None], (128, 2)))
        icl = np.stack([ic_l[0] if cq == 0 else mid, ic_l[1] if cq == 3 else mid], axis=0)
        m["ic_lat_t"] = np.ascontiguousarray(np.broadcast_to(icl[None], (128, 2, 4, 16)))
        ck = cache_k[b_lat, 0]
        ckt = ck.transpose(1, 2, 0)
        m["cache_k_t"] = np.ascontiguousarray(np.concatenate([ckt, ckt], axis=1).transpose(1, 0, 2))
        cv = cache_v[b_lat, 0].reshape(2, 128, 256)
        m["cache_v_t"] = np.ascontiguousarray(cv.transpose(1, 0, 2))
        in_maps.append(m)

    if "nc" not in _NC_CACHE:
        _NC_CACHE["nc"] = build_program()
    nc = _NC_CACHE["nc"]
    res = run_bass_kernel_spmd(nc, in_maps, core_ids=list(range(8)))

    y_prompt = np.zeros((32, 256, D), f)
    y_sample = np.zeros((2, 2048, D), f)
    new_k = np.zeros((32, 1, 256, 4, 64), f)
    new_v = np.zeros((32, 1, 256, 4, 64), f)
    for core in range(8):
        r = res.results[core]
        yc = np.asarray(r["yc"]).transpose(1, 0, 2).reshape(D, 1024).T
        y_prompt[4 * core:4 * core + 4] = yc.reshape(4, 256, D)
        nk = np.asarray(r["nk"]).T.reshape(4, 256, 4, 64)
        nv = np.asarray(r["nv"]).T.reshape(4, 256, 4, 64)
        new_k[4 * core:4 * core + 4, 0] = nk
        new_v[4 * core:4 * core + 4, 0] = nv
        yl = np.asarray(r["yl"]).transpose(1, 0, 2).reshape(D, LAT_COLS).T
        b_lat = core // 4
        cq = core % 4
        lo_col = 256 if cq == 0 else 257
        hi_col = 768 if cq == 3 else 769
        tok0 = 512 * cq - 256
        y_sample[b_lat, tok0 + lo_col: tok0 + hi_col] = yl[lo_col:hi_col]
    return (y_prompt, y_sample, new_k, new_v)
```

```python
import contextlib
import numpy as np
import concourse.bass as bass
import concourse.mybir as mybir
from concourse.bass_utils import run_bass_kernel_spmd

F32 = mybir.dt.float32
BF16 = mybir.dt.bfloat16
AF = mybir.ActivationFunctionType
ALU = mybir.AluOpType
AX = mybir.AxisListType

D = 1024
DEPTH = 4
NCH = 8
DFF = 2816
NJ = 22
EPS = 1e-6
CTX_COLS = 1024
LAT_COLS = 1152
NSLOT = 5
SLOT_E = 2048
NRING = 16
PREFETCH = NSLOT - 1

RUN_LAT = True
DEBUG_STOP = None


class Plan:
    def __init__(self):
        self.q = {e: [] for e in ("pe", "act", "dve", "pool", "sp")}
        self.cnt = {e: 0 for e in ("pe", "act", "dve", "pool")}
        self.waited = {e: {} for e in self.q}
        self.lastw = {}
        self.readers = {}
        self.ring = {"sp": [0] * NRING, "pool": [0] * NRING}
        self.ring_i = {"sp": 0, "pool": 0}
        self.dry = False

    def _wait(self, eng, semkey, val):
        if semkey == "pe" and eng == "pe":
            return
        if self.waited[eng].get(semkey, 0) >= val:
            return
        self.waited[eng][semkey] = val
        self.q[eng].append(("wait", semkey, val))

    def _sync(self, eng, reads, writes):
        for k in reads:
            lw = self.lastw.get(k)
            if lw:
                self._wait(eng, lw[0], lw[1])
        for k in writes:
            lw = self.lastw.get(k)
            if lw:
                self._wait(eng, lw[0], lw[1])
            for sk, v in self.readers.get(k, {}).items():
                self._wait(eng, sk, v)

    def _commit(self, ev, reads, writes):
        sk, v = ev
        for k in reads:
            d = self.readers.setdefault(k, {})
            if d.get(sk, 0) < v:
                d[sk] = v
        for k in writes:
            self.lastw[k] = ev
            self.readers[k] = {}

    def op(self, eng, fn, reads=(), writes=()):
        if self.dry:
            return
        reads = list(reads)
        writes = list(writes)
        self._sync(eng, reads, writes)
        self.cnt[eng] += 1
        self.q[eng].append(("op", fn, True))
        self._commit((eng, self.cnt[eng]), reads, writes)

    def mm(self, fns, reads=(), writes=()):
        if self.dry:
            return
        reads = list(reads)
        writes = list(writes)
        self._sync("pe", reads, writes)
        for f in fns[:-1]:
            self.q["pe"].append(("op", f, False))
        self.cnt["pe"] += 1
        self.q["pe"].append(("op", fns[-1], True))
        self._commit(("pe", self.cnt["pe"]), reads, writes)

    def dma(self, queue, fn, reads=(), writes=()):
        if self.dry:
            return
        reads = list(reads)
        writes = list(writes)
        self._sync(queue, reads, writes)
        i = self.ring_i[queue]
        self.ring_i[queue] += 1
        slot = i % NRING
        prev = self.ring[queue][slot]
        semkey = (queue, slot)
        if prev:
            self._wait(queue, semkey, prev)
        val = prev + 16
        self.ring[queue][slot] = val
        self.q[queue].append(("dma", fn, semkey))
        self._commit((semkey, val), reads, writes)


def gran(buf, c, lo, hi):
    return [(buf, c, g) for g in range(lo // 128, (hi - 1) // 128 + 1)]


def split_tiles(lo, hi, maxn=512):
    out = []
    a = lo
    while a < hi:
        b = min(hi, (a // maxn + 1) * maxn)
        out.append((a, b))
        a = b
    return out


def even_tiles(lo, hi, maxn=512):
    n = -(-(hi - lo) // maxn)
    step = -(-(hi - lo) // n)
    step += step % 2
    out = []
    a = lo
    while a < hi:
        out.append((a, min(hi, a + step)))
        a += step
    return out


def build_program():
    nc = bass.Bass("TRN2", target_bir_lowering=False)
    P = Plan()
    dram_in = {}

    def din(name, shape):
        t = nc.dram_tensor(name, list(shape), F32, kind="ExternalInput")
        dram_in[name] = tuple(shape)
        return t.ap()

    def dout(name, shape):
        return nc.dram_tensor(name, list(shape), F32, kind="ExternalOutput").ap()

    xc_d = din("xc", [128, NCH, CTX_COLS])
    xl_d = din("xl", [128, NCH, LAT_COLS])
    cond_d = din("condT", [128, NCH, 2])
    wada_d = din("w_ada_t", [DEPTH, 24, 128, 2 * 8 * 128])
    bada_d = din("b_ada_t", [128, DEPTH, 48, 2])
    gmix_d = din("g_mix_t", [128, DEPTH, NCH])
    gffn_d = din("g_ffn_t", [128, DEPTH, NCH])
    wfin_d = din("w_ffn_in_t", [DEPTH, 44, 128, 8 * 128])
    cw_d = din("conv_w_t", [128, DEPTH, 3, NJ])
    cb_d = din("conv_b_t", [128, DEPTH, NJ])
    wfout_d = din("w_ffn_out_t", [DEPTH, 16, 128, 11 * 128])
    awu_d = din("a_w_u_t", [2, 16, 128, 8 * 128])
    awv_d = din("a_w_v_t", [2, 4, 128, 8 * 512])
    agv_d = din("a_g_v_t", [128, 2, 16])
    aws_d = din("a_w_s_t", [2, 128, 8 * 128])
    abs_d = din("a_b_s_t", [128, 2, 8, 128])
    awo_d = din("a_w_out_t", [2, 8, 128, 16 * 128])
    pw_d = din("p_w_t", [128, 8 * 2 * 128])
    pb_d = din("p_b_t", [128, NCH])
    psc_d = din("p_scale_t", [128, NCH])
    wq_d = din("c_w_q_t", [8, 128, 8 * 128])
    wk_d = din("c_w_k_t", [4, 128, 8 * 128])
    wvt_d = din("c_w_vt_t", [128, 8 * 256])
    wvf_d = din("c_w_vf_t", [2, 128, 8 * 128])
    wo_d = din("c_w_o_t", [8, 128, 8 * 128])
    gq_d = din("c_g_q_t", [128, 1])
    gk_d = din("c_g_k_t", [128, 1])
    sink_d = din("c_sink_t", [128, 16])
    ones_d = din("ones_t", [128, 128])
    blk_d = din("blk_t", [128, 128])
    pm_d = din("pm_t", [128, 128])
    cos_d = din("cos_t", [128, LAT_COLS])
    sin_d = din("sin_t", [128, LAT_COLS])
    wmask_d = din("wmask_t", [128, 2, 128])
    flags_d = din("flags_t", [128, 2])
    icc_d = din("ic_ctx_t", [128, 2, 4, 16])
    icl_d = din("ic_lat_t", [128, 2, 4, 16])
    ck_d = din("cache_k_t", [128, 4, 256])
    cv_d = din("cache_v_t", [128, 2, 256])

    yc_d = dout("yc", [128, NCH, CTX_COLS])
    yl_d = dout("yl", [128, NCH, LAT_COLS])
    nk_d = dout("nk", [256, CTX_COLS])
    nv_d = dout("nv", [256, CTX_COLS])

    es = contextlib.ExitStack()

    def sb(name, shape, dt):
        return es.enter_context(nc.sbuf_tensor(name, list(shape), dt))

    with es:
        X = sb("X", [128, NCH, LAT_COLS], F32)
        H = sb("H", [128, NCH, LAT_COLS], BF16)
        HID = sb("HID", [128, NJ, LAT_COLS], BF16)
        SCR = sb("SCR", [128, 8192], F32)
        AS = sb("AS", [128, 2, 1160], F32)
        NTMP = 6
        TMPR = sb("TMPR", [128, NTMP, 512], F32)
        NSQ = 4
        SQ = sb("SQ", [128, NSQ, 512], BF16)
        NRS = 3
        RS = sb("RS", [128, NRS, 512], F32)
        WSL = sb("WSL", [128, NSLOT, SLOT_E], BF16)
        MOD = sb("MOD", [128, DEPTH, 48, 2], F32)
        BADA = sb("BADA", [128, DEPTH, 48, 2], F32)
        G1 = sb("G1", [128, DEPTH, 2, NCH], F32)
        G2 = sb("G2", [128, DEPTH, 2, NCH], F32)
        GMIX = sb("GMIX", [128, DEPTH, NCH], F32)
        GFFN = sb("GFFN", [128, DEPTH, NCH], F32)
        CW = sb("CW", [128, DEPTH, 3, NJ], F32)
        CB = sb("CB", [128, DEPTH, NJ], F32)
        CONDT = sb("CONDT", [128, NCH, 2], F32)
        SCOND = sb("SCOND", [128, NCH, 2], BF16)
        AGV = sb("AGV", [128, 2, 16], F32)
        ABS = sb("ABS", [128, 2, 8, 128], F32)
        WST = sb("WST", [128, 2, 8 * 128], BF16)
        PB = sb("PB", [128, NCH], F32)
        PSC = sb("PSC", [128, NCH], F32)
        PA = sb("PA", [128, 2, NCH], F32)
        PBB = sb("PBB", [128, 2, NCH], F32)
        G1F = sb("G1F", [128, 2, NCH], F32)
        SH1F = sb("SH1F", [128, 2, NCH], F32)
        GQ = sb("GQ", [128, 1], F32)
        GK = sb("GK", [128, 1], F32)
        SINK = sb("SINK", [128, 16], F32)
        ESK = sb("ESK", [128, 16], F32)
        ONES = sb("ONES", [128, 128], BF16)
        BLK = sb("BLK", [128, 128], BF16)
        PM = sb("PM", [128, 128], BF16)
        WMASK = sb("WMASK", [128, 2, 128], F32)
        FLAGS = sb("FLAGS", [128, 2], F32)
        NEGB = sb("NEGB", [128, 2], F32)
        ZB = sb("ZB", [128, 1], F32)
        EPST = sb("EPST", [128, 1], F32)
        ICC = sb("ICC", [128, 2, 4, 16], F32)
        ICL = sb("ICL", [128, 2, 4, 16], F32)
        SS = sb("SS", [128, 8], F32)
        PS = es.enter_context(nc.psum_tensor("PS", [128, 4096], F32))

        WV = SCR[:, 0:8192].bitcast(BF16)
        COS = SCR[:, 0:LAT_COLS]
        SIN = SCR[:, LAT_COLS:2 * LAT_COLS]
        PT = SCR[:, 0:2560].bitcast(BF16)
        VE = SCR[:, 2560:2560 + 4224].bitcast(BF16)
        H32 = HID[:, :, :].bitcast(F32)

        ps_state = {"i": 0}

        NPSB = 7

        def ps_alloc(n=1):
            i = ps_state["i"]
            if i + n > NPSB:
                i = 0
            ps_state["i"] = (i + n) % NPSB
            return i

        def psb(b, n=512, nb=1):
            return PS[:, b * 512: b * 512 + (n if nb == 1 else nb * 512)]

        def pskeys(b, nb=1):
            return [("ps", b + i) for i in range(nb)]

        ring_state = {"tmp": 0, "sq": 0, "rs": 0, "pt": 0}

        def nxt(name, n):
            i = ring_state[name]
            ring_state[name] = (i + 1) % n
            return i

        sched = []
        wstate = {"issued": 0, "used": 0, "list": []}

        def w_issue_upto(k):
            L = wstate["list"]
            while wstate["issued"] < min(k, len(L)):
                i = wstate["issued"]
                ap, ne = L[i]
                s = i % NSLOT
                P.dma("pool", (lambda e, s=s, ap=ap, ne=ne: e.dma_start(out=WSL[:, s, 0:ne], in_=ap)),
                      reads=[], writes=[("w", s)])
                wstate["issued"] += 1

        def getw(*aps):
            n = len(aps)
            if P.dry:
                wstate["list"].extend(aps)
                return 0 if n == 1 else [0] * n
            i = wstate["used"]
            wstate["used"] += n
            w_issue_upto(i + NSLOT)
            if n == 1:
                return i % NSLOT
            return [(i + k) % NSLOT for k in range(n)]

        def load_f32(dst, src, key):
            P.dma("sp", (lambda e: e.dma_start(out=dst, in_=src)), reads=[], writes=[key])

        def load_bf16(dst, src, key):
            P.dma("pool", (lambda e: e.dma_start(out=dst, in_=src)), reads=[], writes=[key])

        def phase_consts():
            load_f32(CONDT[:, :, :], cond_d, "condt")
            load_f32(BADA[:, :, :, :], bada_d, "bada")
            load_f32(GMIX[:, :, :], gmix_d, "gmix")
            load_f32(GFFN[:, :, :], gffn_d, "gffn")
            phase_load_x(groups[0])
            load_f32(CW[:, :, :, :], cw_d, "cw")
            load_f32(CB[:, :, :], cb_d, "cb")
            load_f32(AGV[:, :, :], agv_d, "agv")
            load_f32(ABS[:, :, :, :], abs_d, "abs")
            load_f32(PB[:, :], pb_d, "pb")
            load_f32(PSC[:, :], psc_d, "psc")
            load_f32(GQ[:, :], gq_d, "gq")
            load_f32(GK[:, :], gk_d, "gk")
            load_f32(SINK[:, :], sink_d, "sink")
            load_f32(WMASK[:, :, :], wmask_d, "wmask")
            load_f32(FLAGS[:, :], flags_d, "flags")
            load_f32(ICC[:, :, :, :], icc_d, "icc")
            load_f32(ICL[:, :, :, :], icl_d, "icl")
            load_bf16(ONES[:, :], ones_d, "ones")
            load_bf16(BLK[:, :], blk_d, "blk")
            load_bf16(PM[:, :], pm_d, "pm")
            for j in range(2):
                load_bf16(WST[:, j, :], aws_d[j], "wst")
            P.op("dve", lambda e: e.memset(ZB[:, :], 0.0), writes=["zb"])
            P.op("dve", lambda e: e.memset(EPST[:, :], float(EPS)), writes=["epst"])
            P.op("act", lambda e: e.activation(out=SCOND[:, :, :], in_=CONDT[:, :, :], func=AF.Silu),
                 reads=["condt"], writes=["scond"])
            P.op("act", lambda e: e.activation(out=ESK[:, :], in_=SINK[:, :], func=AF.Exp),
                 reads=["sink"], writes=["esk"])
            P.op("dve", lambda e: e.tensor_scalar(out=NEGB[:, :], in0=FLAGS[:, :], scalar1=30000.0, scalar2=-30000.0,
                                                  op0=ALU.mult, op1=ALU.add), reads=["flags"], writes=["negb"])


        ada_pending = []

        def ada_enqueue(l):
            for i in range(24):
                ada_pending.append((l, i))

        def ada_step():
            if not ada_pending:
                return
            l, i = ada_pending.pop(0)
            b = 7
            s = getw((wada_d[l, i], 2048))
            for h in range(2):
                cc = i * 2 + h
                fns = []
                for kc in range(8):
                    fns.append(lambda e, s=s, h=h, kc=kc, cc=cc, b=b: e.matmul(
                        PS[:, b * 512 + cc * 2: b * 512 + cc * 2 + 2],
                        WSL[:, s, (h * 8 + kc) * 128:(h * 8 + kc + 1) * 128],
                        SCOND[:, kc, :], start=(kc == 0), stop=(kc == 7)))
                P.mm(fns, reads=[("w", s), "scond"], writes=pskeys(b))
            if i == 7:
                ada_finish1(l, b)
            if i == 23:
                ada_finish2(l, b)

        def ada_drain():
            while ada_pending:
                ada_step()

        def ada_finish1(l, b):
            P.op("dve", lambda e, b=b, l=l: e.tensor_tensor(
                out=MOD[:, l, 0:16, :], in0=PS[:, b * 512: b * 512 + 32].rearrange("p (c t) -> p c t", t=2),
                in1=BADA[:, l, 0:16, :], op=ALU.add), reads=pskeys(b) + ["bada"], writes=[("mod1", l)])
            for ci in range(2):
                P.op("dve", lambda e, l=l, ci=ci: e.scalar_tensor_tensor(
                    out=G1[:, l, ci, :], in0=MOD[:, l, 8:16, ci], scalar=1.0, in1=GMIX[:, l, :],
                    op0=ALU.add, op1=ALU.mult), reads=[("mod1", l), "gmix"], writes=[("g1", l)])
            if l == 1:
                ada_masked_tables()

        def ada_masked_tables():
            for fl in range(2):
                P.op("dve", lambda e, fl=fl: e.tensor_scalar(out=G1F[:, fl, :], in0=G1[:, 1, 1, :], scalar1=FLAGS[:, fl:fl + 1],
                                                             scalar2=None, op0=ALU.mult),
                     reads=[("g1", 1), "flags"], writes=["g1f"])
                P.op("dve", lambda e, fl=fl: e.tensor_scalar(out=SH1F[:, fl, :], in0=MOD[:, 1, 0:8, 1], scalar1=FLAGS[:, fl:fl + 1],
                                                             scalar2=None, op0=ALU.mult),
                     reads=[("mod1", 1), "flags"], writes=["g1f"])

        def ada_finish2(l, b):
            P.op("dve", lambda e, b=b, l=l: e.tensor_tensor(
                out=MOD[:, l, 16:48, :], in0=PS[:, b * 512 + 32: b * 512 + 96].rearrange("p (c t) -> p c t", t=2),
                in1=BADA[:, l, 16:48, :], op=ALU.add), reads=pskeys(b) + ["bada"], writes=[("mod", l)])
            for ci in range(2):
                P.op("dve", lambda e, l=l, ci=ci: e.scalar_tensor_tensor(
                    out=G2[:, l, ci, :], in0=MOD[:, l, 32:40, ci], scalar=1.0, in1=GFFN[:, l, :],
                    op0=ALU.add, op1=ALU.mult), reads=[("mod", l), "gffn"], writes=[("g2", l)])
            if l == 1:
                for ci in range(2):
                    P.op("dve", lambda e, ci=ci: e.tensor_tensor(out=PA[:, ci, :], in0=MOD[:, 1, 16:24, ci],
                                                                  in1=PSC[:, :], op=ALU.mult),
                         reads=[("mod", 1), "psc"], writes=["pa"])
                    P.op("dve", lambda e, ci=ci: e.tensor_tensor(out=PBB[:, ci, :], in0=PA[:, ci, :],
                                                                  in1=PB[:, :], op=ALU.mult),
                         reads=["pa", "pb"], writes=["pbb"])

        class Grp:
            pass

        GC = Grp()
        GC.name = "c"; GC.ci = 0; GC.cols = CTX_COLS; GC.nseg = 4; GC.L = 256
        GC.tiles = [(0, 512), (512, 1024)]
        GC.x_d = xc_d; GC.y_d = yc_d; GC.lat = False
        GL = Grp()
        GL.name = "l"; GL.ci = 1; GL.cols = LAT_COLS; GL.nseg = 1; GL.L = LAT_COLS
        GL.tiles = [(0, 512), (512, 1024), (1024, 1152)]
        GL.x_d = xl_d; GL.y_d = yl_d; GL.lat = True
        GC.win_f = [(0, 1024)] * 4
        GC.win_m = [(0, 1024)] * 4
        GC.qwin = (0, 1024)
        GL.win_f = [(117, 1035), (126, 1026), (255, 897), (254, 771)]
        GL.win_m = [(0, 1152), (118, 1034), (0, 1152), (256, 896)]
        GL.qwin = (128, 1024)

        def v3(ap2d, g, n):
            if g.nseg > 1:
                return ap2d.rearrange("p (s l) -> p s l", l=g.L)
            return ap2d

        def as_view(g, par, lo, hi, sh):
            if g.nseg > 1:
                base = AS[:, par, 0:g.nseg * (g.L + 2)].rearrange("p (s l) -> p s l", l=g.L + 2)
                return base[:, lo // g.L: hi // g.L, 1 + sh: 1 + sh + g.L]
            return AS[:, par, 1 + lo + sh: 1 + hi + sh]

        def as_keys(par, lo, hi):
            return [("as", par, k) for k in range(max(lo, 0) // 128, (hi - 1) // 128 + 1)]

        def phase_load_x(g):
            for c in range(NCH):
                P.dma("sp", (lambda e, c=c: e.dma_start(out=X[:, c, 0:g.cols], in_=g.x_d[:, c, :])),
                      reads=[], writes=gran("x", c, 0, g.cols))
            P.op("pool", lambda e: e.memset(AS[:, :, :], 0.0), writes=as_keys(0, 0, 1160) + as_keys(1, 0, 1160))

        def phase_store_x(g):
            for c in range(NCH):
                P.dma("sp", (lambda e, c=c: e.dma_start(out=g.y_d[:, c, :], in_=X[:, c, 0:g.cols])),
                      reads=gran("x", c, 0, g.cols), writes=[])

        pre_stats = {}

        def norm_stats_tile(lo, hi, all_act):
            n = hi - lo
            b = ps_alloc()
            for c in range(NCH):
                s = nxt("sq", NSQ)
                if c % 2 == 0 or all_act:
                    P.op("act", lambda e, s=s, c=c, lo=lo, hi=hi, n=n: e.activation(
                        out=SQ[:, s, 0:n], in_=X[:, c, lo:hi], func=AF.Square),
                        reads=gran("x", c, lo, hi), writes=[("sq", s)])
                else:
                    P.op("dve", lambda e, s=s, c=c, lo=lo, hi=hi, n=n: e.tensor_tensor(
                        out=SQ[:, s, 0:n], in0=X[:, c, lo:hi], in1=X[:, c, lo:hi], op=ALU.mult),
                        reads=gran("x", c, lo, hi), writes=[("sq", s)])
                P.mm([lambda e, s=s, c=c, b=b, n=n: e.matmul(PS[:, b * 512: b * 512 + n], ONES[:, :], SQ[:, s, 0:n],
                                                              start=(c == 0), stop=(c == NCH - 1))],
                     reads=[("sq", s), "ones"], writes=pskeys(b))
            r = nxt("rs", NRS)
            P.op("act", lambda e, r=r, b=b, n=n: e.activation(
                out=RS[:, r, 0:n], in_=PS[:, b * 512: b * 512 + n], func=AF.Ln, scale=1.0 / D,
                bias=EPST[:, 0:1]), reads=pskeys(b) + ["epst"], writes=[("rs", r)])
            P.op("act", lambda e, r=r, n=n: e.activation(
                out=RS[:, r, 0:n], in_=RS[:, r, 0:n], func=AF.Exp, scale=-0.5),
                reads=[("rs", r)], writes=[("rs", r)])
            return r

        def norm_stats(tiles):
            out = []
            for ti_, (lo, hi) in enumerate(tiles):
                if (lo, hi) in pre_stats:
                    out.append(pre_stats.pop((lo, hi)))
                else:
                    out.append(norm_stats_tile(lo, hi, ti_ == 0))
            return out

        def pre_norm_a(nt0):
            lo, hi = nt0
            n = hi - lo
            for c in range(NCH):
                P.op("act", lambda e, c=c, lo=lo, hi=hi, n=n: e.activation(
                    out=H[:, c, 0:n], in_=X[:, c, lo:hi], func=AF.Square),
                    reads=gran("x", c, lo, hi), writes=gran("h", c, 0, n))

        def pre_norm_b(nt0):
            lo, hi = nt0
            n = hi - lo
            b = ps_alloc()
            for c in range(NCH):
                P.mm([lambda e, c=c, b=b, n=n: e.matmul(PS[:, b * 512: b * 512 + n], ONES[:, :], H[:, c, 0:n],
                                                        start=(c == 0), stop=(c == NCH - 1))],
                     reads=gran("h", c, 0, n) + ["ones"], writes=pskeys(b))
            r = nxt("rs", NRS)
            P.op("act", lambda e, r=r, b=b, n=n: e.activation(
                out=RS[:, r, 0:n], in_=PS[:, b * 512: b * 512 + n], func=AF.Ln, scale=1.0 / D,
                bias=EPST[:, 0:1]), reads=pskeys(b) + ["epst"], writes=[("rs", r)])
            P.op("act", lambda e, r=r, n=n: e.activation(
                out=RS[:, r, 0:n], in_=RS[:, r, 0:n], func=AF.Exp, scale=-0.5),
                reads=[("rs", r)], writes=[("rs", r)])
            pre_stats[nt0] = r

        def norm_mod(g, tiles, gsc, shf, dep_keys, out_ap_fn, out_keys_fn, tmp_view=None, flagged=None):
            rs = norm_stats(tiles)
            for ti, (lo, hi) in enumerate(tiles):
                n = hi - lo
                r = rs[ti]
                for c in range(NCH):
                    t = nxt("tmp", NTMP)
                    P.op("dve", lambda e, t=t, c=c, r=r, lo=lo, hi=hi, n=n: e.tensor_tensor(
                        out=TMPR[:, t, 0:n], in0=X[:, c, lo:hi], in1=RS[:, r, 0:n], op=ALU.mult),
                        reads=gran("x", c, lo, hi) + [("rs", r)], writes=[("tmp", t)])
                    if flagged is not None:
                        gf, sf = flagged
                        for (a0, a1, fl) in ((0, 256, 0), (256, 768, None), (768, LAT_COLS, 1)):
                            s0, s1 = max(a0, lo), min(a1, hi)
                            if s0 >= s1:
                                continue
                            sc_ap = gsc[:, c:c + 1] if fl is None else gf[:, fl, c:c + 1]
                            bi_ap = shf(c) if fl is None else sf[:, fl, c:c + 1]
                            P.op("act", lambda e, t=t, c=c, s0=s0, s1=s1, lo=lo, sc_ap=sc_ap, bi_ap=bi_ap: e.activation(
                                out=out_ap_fn(c, s0, s1), in_=TMPR[:, t, s0 - lo:s1 - lo],
                                func=AF.Identity, scale=sc_ap, bias=bi_ap),
                                reads=[("tmp", t)] + dep_keys, writes=out_keys_fn(c, s0, s1))
                        continue
                    P.op("act", lambda e, t=t, c=c, lo=lo, hi=hi, n=n: e.activation(
                        out=out_ap_fn(c, lo, hi),
                        in_=(TMPR[:, t, 0:n] if tmp_view is None else tmp_view(TMPR[:, t, 0:n], n)),
                        func=AF.Identity, scale=gsc[:, c:c + 1], bias=shf(c)),
                        reads=[("tmp", t)] + dep_keys, writes=out_keys_fn(c, lo, hi))

        def h_out(c, lo, hi):
            return H[:, c, lo:hi]

        def h_keys(c, lo, hi):
            return gran("h", c, lo, hi)

        def out_proj(tiles, wspec, nk, per_slot, src, src_keys, l, ci, gate0, nt0=None):
            def one(nn, slots, lo, hi):
                n = hi - lo
                b = ps_alloc()
                fns = []
                for kk in range(nk):
                    sl = slots[kk // per_slot]
                    ko = kk % per_slot
                    fns.append(lambda e, sl=sl, ko=ko, kk=kk, b=b, lo=lo, hi=hi, n=n: e.matmul(
                        PS[:, b * 512: b * 512 + n], WSL[:, sl, ko * 128:(ko + 1) * 128], src(kk, lo, hi),
                        start=(kk == 0), stop=(kk == nk - 1)))
                rd = [("w", sl) for sl in slots]
                for kk in range(nk):
                    rd += src_keys(kk, lo, hi)
                P.mm(fns, reads=rd, writes=pskeys(b))
                P.op("dve", lambda e, b=b, lo=lo, hi=hi, n=n, nn=nn: e.scalar_tensor_tensor(
                    out=X[:, nn, lo:hi], in0=PS[:, b * 512: b * 512 + n], scalar=MOD[:, l, gate0 + nn: gate0 + nn + 1, ci],
                    in1=X[:, nn, lo:hi], op0=ALU.mult, op1=ALU.add),
                    reads=pskeys(b) + gran("x", nn, lo, hi) + [("mod", l)], writes=gran("x", nn, lo, hi))

            def as_list(x):
                return x if isinstance(x, list) else [x]

            for nn in range(NCH - 2):
                slots = as_list(getw(*wspec(nn)))
                for (lo, hi) in tiles:
                    one(nn, slots, lo, hi)
            sp6 = wspec(NCH - 2)
            sp7 = wspec(NCH - 1)
            allslots = as_list(getw(*(tuple(sp6) + tuple(sp7))))
            sl6 = allslots[:len(sp6)]
            sl7 = allslots[len(sp6):]
            hooked = 0
            for k_, (lo, hi) in enumerate(tiles):
                one(NCH - 2, sl6, lo, hi)
                if hooked == 1:
                    pre_norm_b(nt0)
                    hooked = 2
                one(NCH - 1, sl7, lo, hi)
                if (nt0 is not None and hooked == 0 and k_ + 1 < len(tiles)
                        and tiles[0][0] <= nt0[0] and nt0[1] <= hi):
                    pre_norm_a(nt0)
                    hooked = 1

        def ffn_weights(l):
            ws = [(wfin_d[l, m], 1024) for m in range(44)]
            ws += [(wfout_d[l, i], 1408) for i in range(16)]
            return ws

        def evac_a(g, b, par, lo, hi):
            n = hi - lo
            if not g.lat:
                P.op("act", lambda e: e.activation(out=as_view(g, par, lo, hi, 0), in_=v3(PS[:, b * 512: b * 512 + n], g, n),
                                                   func=AF.Copy),
                     reads=pskeys(b), writes=as_keys(par, lo, hi))
                return
            pieces = []
            for (a0, a1, fl) in ((0, 256, 0), (256, 768, None), (768, LAT_COLS, 1)):
                s0, s1 = max(a0, lo), min(a1, hi)
                if s0 < s1:
                    pieces.append((s0, s1, fl))
            for (s0, s1, fl) in pieces:
                if fl is None:
                    P.op("act", lambda e, s0=s0, s1=s1: e.activation(
                        out=AS[:, par, 1 + s0: 1 + s1], in_=PS[:, b * 512 + s0 - lo: b * 512 + s1 - lo], func=AF.Copy),
                        reads=pskeys(b), writes=as_keys(par, s0, s1))
                else:
                    P.op("act", lambda e, s0=s0, s1=s1, fl=fl: e.activation(
                        out=AS[:, par, 1 + s0: 1 + s1], in_=PS[:, b * 512 + s0 - lo: b * 512 + s1 - lo], func=AF.Copy,
                        scale=FLAGS[:, fl:fl + 1]),
                        reads=pskeys(b) + ["flags"], writes=as_keys(par, s0, s1))

        def phase_ffn(g, l, nt0=None):
            ci = g.ci
            tiles = even_tiles(*g.win_f[l])
            if not g.lat and l + 1 < DEPTH:
                ada_enqueue(l + 1)
            norm_mod(g, tiles, G2[:, l, ci, :], (lambda c: MOD[:, l, 24 + c: 25 + c, ci]),
                     [("g2", l), ("mod", l)], h_out, h_keys)
            pending_gelu = []

            def flush_gelu():
                for (t3, lo, hi, n, jj) in pending_gelu:
                    P.op("act", lambda e, t3=t3, lo=lo, hi=hi, n=n, jj=jj: e.activation(
                        out=HID[:, jj, lo:hi], in_=TMPR[:, t3, 0:n], func=AF.Gelu_apprx_tanh),
                        reads=[("tmp", t3)], writes=gran("hid", jj, lo, hi))
                del pending_gelu[:]

            for j in range(NJ):
                s = getw((wfin_d[l, j], 1024))
                par = j % 2
                c1slots = []
                for (lo, hi) in tiles:
                    n = hi - lo
                    b = ps_alloc()
                    fns = [(lambda e, kc=kc, b=b, lo=lo, hi=hi, n=n, s=s: e.matmul(
                        PS[:, b * 512: b * 512 + n], WSL[:, s, kc * 128:(kc + 1) * 128], H[:, kc, lo:hi],
                        start=(kc == 0), stop=(kc == 7))) for kc in range(8)]
                    if j == 0:
                        for kc in range(8):
                            P.mm([fns[kc]], reads=[("w", s)] + gran("h", kc, lo, hi), writes=pskeys(b))
                    else:
                        rd = [("w", s)]
                        for kc in range(8):
                            rd += gran("h", kc, lo, hi)
                        P.mm(fns, reads=rd, writes=pskeys(b))
                    evac_a(g, b, par, lo, hi)
                    t1 = nxt("tmp", NTMP)
                    c1slots.append(t1)
                    P.op("act", lambda e, t1=t1, b=b, n=n, j=j: e.activation(
                        out=TMPR[:, t1, 0:n], in_=PS[:, b * 512: b * 512 + n], func=AF.Identity,
                        scale=CW[:, l, 1, j:j + 1], bias=CB[:, l, j:j + 1]),
                        reads=pskeys(b) + ["cw", "cb"], writes=[("tmp", t1)])
                flush_gelu()
                for ti, (lo, hi) in enumerate(tiles):
                    n = hi - lo
                    t3 = c1slots[ti]
                    P.op("dve", lambda e, t3=t3, lo=lo, hi=hi, n=n, j=j, par=par: e.scalar_tensor_tensor(
                        out=v3(TMPR[:, t3, 0:n], g, n), in0=as_view(g, par, lo, hi, -1), scalar=CW[:, l, 0, j:j + 1],
                        in1=v3(TMPR[:, t3, 0:n], g, n), op0=ALU.mult, op1=ALU.add),
                        reads=as_keys(par, lo - 1, hi) + [("tmp", t3), "cw"], writes=[("tmp", t3)])
                    P.op("dve", lambda e, t3=t3, lo=lo, hi=hi, n=n, j=j, par=par: e.scalar_tensor_tensor(
                        out=v3(TMPR[:, t3, 0:n], g, n), in0=as_view(g, par, lo, hi, 1), scalar=CW[:, l, 2, j:j + 1],
                        in1=v3(TMPR[:, t3, 0:n], g, n), op0=ALU.mult, op1=ALU.add),
                        reads=as_keys(par, lo, hi + 1) + [("tmp", t3), "cw"], writes=[("tmp", t3)])
                    pending_gelu.append((t3, lo, hi, n, j))
                ada_step()
            flush_gelu()
            for j in range(NJ):
                s = getw((wfin_d[l, 22 + j], 1024))
                for (lo, hi) in tiles:
                    n = hi - lo
                    b = ps_alloc()
                    fns = [(lambda e, kc=kc, b=b, lo=lo, hi=hi, n=n, s=s: e.matmul(
                        PS[:, b * 512: b * 512 + n], WSL[:, s, kc * 128:(kc + 1) * 128], H[:, kc, lo:hi],
                        start=(kc == 0), stop=(kc == 7))) for kc in range(8)]
                    rd = [("w", s)]
                    for kc in range(8):
                        rd += gran("h", kc, lo, hi)
                    P.mm(fns, reads=rd, writes=pskeys(b))
                    P.op("dve", lambda e, b=b, lo=lo, hi=hi, n=n, j=j: e.tensor_tensor(
                        out=HID[:, j, lo:hi], in0=HID[:, j, lo:hi], in1=PS[:, b * 512: b * 512 + n], op=ALU.mult),
                        reads=gran("hid", j, lo, hi) + pskeys(b), writes=gran("hid", j, lo, hi))
                ada_step()
            out_proj(tiles, (lambda nn: ((wfout_d[l, 2 * nn], 1408), (wfout_d[l, 2 * nn + 1], 1408))), NJ, 11,
                     (lambda kk, lo, hi: HID[:, kk, lo:hi]), (lambda kk, lo, hi: gran("hid", kk, lo, hi)), l, ci, 40,
                     nt0=nt0)
            ada_drain()

        def gmlp_weights(j):
            ws = [(awu_d[j, m], 1024) for m in range(16)]
            ws += [(awo_d[j, nn], 2048) for nn in range(NCH)]
            return ws

        def load_wv(j):
            for cb in range(4):
                P.dma("pool", (lambda e, cb=cb: e.dma_start(out=WV[:, cb * 4096:(cb + 1) * 4096], in_=awv_d[j, cb])),
                      reads=[], writes=["scr"])

        def phase_gmlp(g, l, nt0=None):
            ci = g.ci
            j = l // 3
            tiles = even_tiles(*g.win_m[l])
            norm_mod(g, tiles, G1[:, l, ci, :], (lambda c: MOD[:, l, c: c + 1, ci]),
                     [("g1", l), ("mod1", l)], h_out, h_keys)
            U = HID
            for m in range(16):
                s = getw((awu_d[j, m], 1024))
                for (lo, hi) in tiles:
                    n = hi - lo
                    b = ps_alloc()
                    fns = [(lambda e, kc=kc, b=b, lo=lo, hi=hi, n=n, s=s: e.matmul(
                        PS[:, b * 512: b * 512 + n], WSL[:, s, kc * 128:(kc + 1) * 128], H[:, kc, lo:hi],
                        start=(kc == 0), stop=(kc == 7))) for kc in range(8)]
                    if m == 0:
                        for kc in range(8):
                            P.mm([fns[kc]], reads=[("w", s)] + gran("h", kc, lo, hi), writes=pskeys(b))
                    else:
                        rd = [("w", s)]
                        for kc in range(8):
                            rd += gran("h", kc, lo, hi)
                        P.mm(fns, reads=rd, writes=pskeys(b))
                    P.op("act", lambda e, b=b, lo=lo, hi=hi, n=n, m=m: e.activation(
                        out=U[:, m, lo:hi], in_=PS[:, b * 512: b * 512 + n], func=AF.Gelu_apprx_tanh),
                        reads=pskeys(b), writes=gran("hid", m, lo, hi))
                ada_step()
            ada_drain()
            HF = HID[:, 16:22, :].rearrange("p a b -> p (a b)")
            VGb = [HF[:, 0:2048], HF[:, 2048:4096]]
            WSQ = [HF[:, 4096:5120], HF[:, 5120:6144]]
            JUNK = TMPR[:, 0:2, :].rearrange("p a b -> p (a b)").bitcast(BF16)
            chunks = list(range(g.win_m[l][0] // 128, g.win_m[l][1] // 128))

            def emit_v(q):
                pp = q % 2
                c0 = q * 128
                b4 = ps_alloc(4)
                for cb in range(4):
                    fns = [(lambda e, kc=kc, cb=cb, b4=b4, c0=c0: e.matmul(
                        PS[:, (b4 + cb) * 512:(b4 + cb + 1) * 512], H[:, kc, c0:c0 + 128],
                        WV[:, cb * 4096 + kc * 512: cb * 4096 + (kc + 1) * 512],
                        start=(kc == 0), stop=(kc == 7))) for kc in range(8)]
                    rd = ["scr"]
                    for kc in range(8):
                        rd += gran("h", kc, c0, c0 + 128)
                    P.mm(fns, reads=rd, writes=pskeys(b4 + cb))
                P.op("act", lambda e, b4=b4, pp=pp: e.activation(out=VGb[pp], in_=PS[:, b4 * 512:(b4 + 4) * 512],
                                                                 func=AF.Gelu_apprx_tanh),
                     reads=pskeys(b4, 4), writes=[("vgb", pp)])
                P.op("act", lambda e, pp=pp: e.activation(out=JUNK, in_=VGb[pp], func=AF.Square,
                                                          accum_out=SS[:, pp * 4: pp * 4 + 1]),
                     reads=[("vgb", pp)], writes=[("tmp", 0), ("tmp", 1), ("ss", pp)])
                P.op("act", lambda e, pp=pp: e.activation(out=SS[:, pp * 4 + 1: pp * 4 + 2], in_=SS[:, pp * 4: pp * 4 + 1],
                                                          func=AF.Sqrt, scale=1.0 / 2048.0, bias=EPST[:, 0:1]),
                     reads=[("ss", pp), "epst"], writes=[("ss1", pp)])
                P.op("dve", lambda e, pp=pp: e.reciprocal(out=SS[:, pp * 4 + 2: pp * 4 + 3], in_=SS[:, pp * 4 + 1: pp * 4 + 2]),
                     reads=[("ss1", pp)], writes=[("ss2", pp)])
                P.op("dve", lambda e, pp=pp: e.tensor_scalar(out=WSQ[pp], in0=WST[:, j, :], scalar1=SS[:, pp * 4 + 2: pp * 4 + 3],
                                                             scalar2=None, op0=ALU.mult),
                     reads=[("ss2", pp), "wst"], writes=[("wsq", pp)])

            def emit_spatial(q):
                pp = q % 2
                c0 = q * 128
                for bq in range(4):
                    b = ps_alloc()
                    fns = []
                    for i in range(4):
                        fc = bq * 4 + i
                        fns.append(lambda e, fc=fc, i=i, b=b, pp=pp: e.matmul(
                            PS[:, b * 512 + i * 128: b * 512 + (i + 1) * 128], VGb[pp][:, fc * 128:(fc + 1) * 128],
                            WSQ[pp][:, (fc // 2) * 128:(fc // 2 + 1) * 128], start=True, stop=True))
                    P.mm(fns, reads=[("vgb", pp), ("wsq", pp)], writes=pskeys(b))
                    t = 4 + (bq % 2)
                    for i in range(4):
                        fc = bq * 4 + i
                        P.op("dve", lambda e, fc=fc, i=i, b=b, t=t: e.scalar_tensor_tensor(
                            out=TMPR[:, t, i * 128:(i + 1) * 128], in0=PS[:, b * 512 + i * 128: b * 512 + (i + 1) * 128],
                            scalar=AGV[:, j, fc:fc + 1], in1=ABS[:, j, fc // 2, :], op0=ALU.mult, op1=ALU.add),
                            reads=pskeys(b) + ["agv", "abs"], writes=[("tmp", t)])
                    P.op("dve", lambda e, bq=bq, t=t, c0=c0: e.tensor_tensor(
                        out=U[:, bq * 4:(bq + 1) * 4, c0:c0 + 128], in0=U[:, bq * 4:(bq + 1) * 4, c0:c0 + 128],
                        in1=TMPR[:, t, :].rearrange("p (a b) -> p a b", b=128), op=ALU.mult),
                        reads=[("tmp", t)] + [k for fc in range(bq * 4, bq * 4 + 4) for k in gran("hid", fc, c0, c0 + 128)],
                        writes=[k for fc in range(bq * 4, bq * 4 + 4) for k in gran("hid", fc, c0, c0 + 128)])

            emit_v(chunks[0])
            for qi, q in enumerate(chunks):
                if qi + 1 < len(chunks):
                    emit_v(chunks[qi + 1])
                emit_spatial(q)
            ring_state["tmp"] = 0
            out_proj(tiles, (lambda nn: ((awo_d[j, nn], 2048),)), 16, 16,
                     (lambda kk, lo, hi: U[:, kk, lo:hi]), (lambda kk, lo, hi: gran("hid", kk, lo, hi)), l, ci, 16,
                     nt0=nt0)

        PADW = 16

        def pool_weights():
            return [(pw_d[:, :], 2048)]

        def phase_pool(g, l):
            ci = g.ci
            tiles = even_tiles(*g.win_m[l])
            L = g.L
            Lp = L + 2 * PADW
            nseg = g.nseg
            H32f = H32.rearrange("p a b -> p (a b)")
            W = nseg * Lp

            def hb(c):
                return H32f[:, c * W:(c + 1) * W].rearrange("p (s l) -> p s l", l=Lp)

            SA = H32f[:, 8 * W: 9 * W].rearrange("p (s l) -> p s l", l=Lp)
            SB = H32f[:, 9 * W: 10 * W].rearrange("p (s l) -> p s l", l=Lp)
            wlo, whi = g.win_m[l]
            hall = H32f[:, 0:8 * W].rearrange("p (s l) -> p s l", l=Lp)
            zl = PADW + (wlo if nseg == 1 else 0)
            zr = PADW + (whi if nseg == 1 else L)
            hidk = [k for jj in range(NJ) for k in gran("hid", jj, 0, g.cols)]
            P.op("pool", lambda e: e.memset(hall[:, :, 0:zl], 0.0), reads=[], writes=["h32all"] + hidk)
            P.op("pool", lambda e: e.memset(hall[:, :, zr:Lp], 0.0), reads=[], writes=["h32all"] + hidk)
            P.op("pool", lambda e: e.memset(SA[:, :, 0:1], 0.0), reads=[], writes=[("sab", 0)])

            def h32_out(c, lo, hi):
                if nseg > 1:
                    return hb(c)[:, lo // L: hi // L, PADW:PADW + L]
                return hb(c)[:, 0, PADW + lo:PADW + hi]

            def tmp3(ap, n):
                return v3(ap, g, n)

            def nm_out(c, lo, hi):
                return h32_out(c, lo, hi)

            def nm_in_fix(fnlam):
                return fnlam

            norm_mod(g, tiles, G1[:, l, ci, :], (lambda c: MOD[:, l, c:c + 1, ci]),
                     [("g1", l), ("mod1", l), "h32all", "g1f"], h32_out, (lambda c, lo, hi: [("h32", c)]), tmp_view=tmp3,
                     flagged=((G1F, SH1F) if g.lat else None))
            s = getw((pw_d[:, :], 2048))
            IC = ICL if g.lat else ICC
            for c in range(NCH):
                gi = c // 2
                w = (2, 4, 8, 16)[gi]
                src = hb(c)
                bufs = [SA, SB]
                cur = src
                cur_key = ("h32", c)
                r0 = 1
                r1 = Lp
                steps = [(1, 0), (1, 1), (2, 2), (4, 4)][: gi + 1]
                for si, (sl, sr) in enumerate(steps):
                    dst = bufs[si % 2]
                    dkey = ("sab", si % 2)
                    a = r0 + sl if si > 0 else 1
                    bnd = (r1 - sr) if si > 0 else Lp
                    if si == 0:
                        a, bnd = 1, Lp
                    P.op("dve", lambda e, cur=cur, dst=dst, a=a, bnd=bnd, sl=sl, sr=sr: e.tensor_tensor(
                        out=dst[:, :, a:bnd], in0=cur[:, :, a - sl:bnd - sl], in1=cur[:, :, a + sr:bnd + sr], op=ALU.add),
                        reads=[cur_key], writes=[dkey])
                    cur = dst
                    cur_key = dkey
                    r0, r1 = a, bnd
                for (lo, hi) in tiles:
                    n = hi - lo
                    if nseg > 1:
                        sv = cur[:, lo // L: hi // L, PADW:PADW + L]
                    else:
                        sv = cur[:, 0, PADW + lo:PADW + hi]
                    P.op("dve", lambda e, sv=sv, c=c, lo=lo, hi=hi, n=n, w=w: e.scalar_tensor_tensor(
                        out=v3(H[:, c, lo:hi], g, n), in0=sv, scalar=1.0 / w, in1=h32_out(c, lo, hi),
                        op0=ALU.mult, op1=ALU.subtract),
                        reads=[cur_key, ("h32", c)], writes=gran("h", c, lo, hi))
                if nseg > 1:
                    locs = [(0, 0), (L - 16, 1)]
                    for (off, wh) in locs:
                        t = nxt("tmp", NTMP)
                        P.op("dve", lambda e, cur=cur, off=off, wh=wh, t=t, gi=gi: e.tensor_tensor(
                            out=TMPR[:, t, 0:nseg * 16].rearrange("p (s l) -> p s l", l=16),
                            in0=cur[:, :, PADW + off:PADW + off + 16],
                            in1=IC[:, wh, gi, :].unsqueeze(1).broadcast_to([128, nseg, 16]), op=ALU.mult),
                            reads=[cur_key, "icc"], writes=[("tmp", t)])
                        P.op("dve", lambda e, off=off, t=t, c=c: e.tensor_tensor(
                            out=H[:, c, 0:g.cols].rearrange("p (s l) -> p s l", l=L)[:, :, off:off + 16],
                            in0=TMPR[:, t, 0:nseg * 16].rearrange("p (s l) -> p s l", l=16),
                            in1=hb(c)[:, :, PADW + off:PADW + off + 16], op=ALU.subtract),
                            reads=[("tmp", t), ("h32", c)], writes=gran("h", c, 0, g.cols))
                else:
                    for (off, wh) in ((256, 0), (752, 1)):
                        t = nxt("tmp", NTMP)
                        P.op("dve", lambda e, cur=cur, off=off, wh=wh, t=t, gi=gi: e.tensor_tensor(
                            out=TMPR[:, t, 0:16], in0=cur[:, 0, PADW + off:PADW + off + 16],
                            in1=IC[:, wh, gi, :], op=ALU.mult),
                            reads=[cur_key, "icl"], writes=[("tmp", t)])
                        P.op("dve", lambda e, off=off, t=t, c=c: e.tensor_tensor(
                            out=H[:, c, off:off + 16], in0=TMPR[:, t, 0:16],
                            in1=hb(c)[:, 0, PADW + off:PADW + off + 16], op=ALU.subtract),
                            reads=[("tmp", t), ("h32", c)], writes=gran("h", c, off, off + 16))
            for oc8 in range(NCH):
                gi = oc8 // 2
                for (lo, hi) in tiles:
                    n = hi - lo
                    b = ps_alloc()
                    fns = [(lambda e, ic=ic, b=b, lo=lo, hi=hi, n=n, oc8=oc8, gi=gi: e.matmul(
                        PS[:, b * 512: b * 512 + n], WSL[:, s, (oc8 * 2 + ic) * 128:(oc8 * 2 + ic + 1) * 128],
                        H[:, gi * 2 + ic, lo:hi], start=(ic == 0), stop=(ic == 1))) for ic in range(2)]
                    rd = [("w", s)] + gran("h", gi * 2, lo, hi) + gran("h", gi * 2 + 1, lo, hi)
                    P.mm(fns, reads=rd, writes=pskeys(b))
                    t = nxt("tmp", NTMP)
                    P.op("dve", lambda e, b=b, t=t, n=n, oc8=oc8: e.tensor_scalar(
                        out=TMPR[:, t, 0:n], in0=PS[:, b * 512: b * 512 + n], scalar1=PA[:, ci, oc8:oc8 + 1],
                        scalar2=PBB[:, ci, oc8:oc8 + 1], op0=ALU.mult, op1=ALU.add),
                        reads=pskeys(b) + ["pa", "pbb"], writes=[("tmp", t)])
                    P.op("dve", lambda e, t=t, lo=lo, hi=hi, n=n, oc8=oc8: e.tensor_tensor(
                        out=X[:, oc8, lo:hi], in0=X[:, oc8, lo:hi], in1=TMPR[:, t, 0:n], op=ALU.add),
                        reads=[("tmp", t)] + gran("x", oc8, lo, hi), writes=gran("x", oc8, lo, hi))

        def attn_weights(g):
            ws = [(wq_d[c], 1024) for c in range(8)]
            ws += [(wk_d[c], 1024) for c in range(4)]
            ws += [(wvt_d[:, :], 2048)]
            if not g.lat:
                ws += [(wvf_d[c], 1024) for c in range(2)]
            ws += [(wo_d[c], 1024) for c in range(8)]
            return ws

        def phase_attn(g, l, nt0=None):
            ci = g.ci
            qtiles = even_tiles(*g.qwin)
            norm_mod(g, g.tiles, G1[:, l, ci, :], (lambda c: MOD[:, l, c: c + 1, ci]),
                     [("g1", l), ("mod1", l)], h_out, h_keys)
            QT = HID
            KT0 = 8
            O0 = 12
            nkt_lat = g.cols // 128
            VE3 = VE.rearrange("p (t k d) -> p t k d", k=4, d=192)
            P.op("pool", lambda e: e.memset(VE, 1.0), reads=[], writes=["ve", "scr"])
            if g.lat:
                P.dma("sp", lambda e: e.dma_start(out=COS, in_=cos_d), reads=["scr"], writes=["cos"])
                P.dma("sp", lambda e: e.dma_start(out=SIN, in_=sin_d), reads=["scr"], writes=["sin"])
                P.dma("pool", lambda e: e.dma_start(out=HID[:, 20, 0:1024], in_=ck_d.rearrange("p a b -> p (a b)")),
                      reads=[], writes=["ckt"] + gran("hid", 20, 0, 1024))
                for kt in range(2):
                    P.dma("pool", lambda e, kt=kt: e.dma_start(
                        out=VE3[:, 9 + kt, :, 64:128], in_=cv_d[:, kt, :].rearrange("p (k d) -> p k d", d=64)),
                        reads=["ve", "scr"], writes=[("vet", 9 + kt)])

            qk_units = [("q", c, ti, lo, hi) for c in range(8) for ti, (lo, hi) in enumerate(qtiles)]
            qk_units += [("k", c, ti, lo, hi) for c in range(4) for ti, (lo, hi) in enumerate(g.tiles)]
            qst = {}
            qslot = {}

            def qk_A(u):
                kind, c, ti, lo, hi = qk_units[u]
                n = hi - lo
                if ti == 0:
                    qslot[(kind, c)] = getw(((wq_d if kind == "q" else wk_d)[c], 1024))
                s = qslot[(kind, c)]
                b = ps_alloc()
                fns = [(lambda e, kc=kc, b=b, lo=lo, hi=hi, n=n, s=s: e.matmul(
                    PS[:, b * 512: b * 512 + n], WSL[:, s, kc * 128:(kc + 1) * 128], H[:, kc, lo:hi],
                    start=(kc == 0), stop=(kc == 7))) for kc in range(8)]
                rd = [("w", s)]
                for kc in range(8):
                    rd += gran("h", kc, lo, hi)
                P.mm(fns, reads=rd, writes=pskeys(b))
                sq = nxt("sq", NSQ)
                P.op("act", lambda e, sq=sq, b=b, n=n: e.activation(out=SQ[:, sq, 0:n], in_=PS[:, b * 512: b * 512 + n],
                                                                    func=AF.Square),
                     reads=pskeys(b), writes=[("sq", sq)])
                qst[u] = {"b": b, "sq": sq}

            def qk_B(u):
                kind, c, ti, lo, hi = qk_units[u]
                n = hi - lo
                dst = c if kind == "q" else KT0 + c
                gsc = GQ if kind == "q" else GK
                b = qst[u]["b"]
                sq = qst[u]["sq"]
                b2 = ps_alloc()
                P.mm([lambda e, sq=sq, b2=b2, n=n: e.matmul(PS[:, b2 * 512: b2 * 512 + n], BLK[:, :], SQ[:, sq, 0:n],
                                                            start=True, stop=True)],
                     reads=[("sq", sq), "blk"], writes=pskeys(b2))
                r = nxt("rs", NRS)
                P.op("act", lambda e, r=r, b2=b2, n=n: e.activation(
                    out=RS[:, r, 0:n], in_=PS[:, b2 * 512: b2 * 512 + n], func=AF.Ln, scale=1.0 / 64.0,
                    bias=EPST[:, 0:1]), reads=pskeys(b2) + ["epst"], writes=[("rs", r)])
                P.op("act", lambda e, r=r, n=n: e.activation(out=RS[:, r, 0:n], in_=RS[:, r, 0:n], func=AF.Exp,
                                                             scale=-0.5), reads=[("rs", r)], writes=[("rs", r)])
                if not g.lat and kind == "q":
                    P.op("dve", lambda e, b=b, r=r, lo=lo, hi=hi, n=n, dst=dst, gsc=gsc: e.scalar_tensor_tensor(
                        out=HID[:, dst, lo:hi], in0=PS[:, b * 512: b * 512 + n], scalar=gsc[:, 0:1],
                        in1=RS[:, r, 0:n], op0=ALU.mult, op1=ALU.mult),
                        reads=pskeys(b) + [("rs", r), "gq"], writes=gran("hid", dst, lo, hi))
                    return
                t = nxt("tmp", NTMP)
                P.op("dve", lambda e, b=b, r=r, t=t, n=n, gsc=gsc: e.scalar_tensor_tensor(
                    out=TMPR[:, t, 0:n], in0=PS[:, b * 512: b * 512 + n], scalar=gsc[:, 0:1],
                    in1=RS[:, r, 0:n], op0=ALU.mult, op1=ALU.mult),
                    reads=pskeys(b) + [("rs", r), "gq", "gk"], writes=[("tmp", t)])
                if not g.lat:
                    P.op("act", lambda e, t=t, lo=lo, hi=hi, n=n, dst=dst: e.activation(
                        out=HID[:, dst, lo:hi], in_=TMPR[:, t, 0:n], func=AF.Copy),
                        reads=[("tmp", t)], writes=gran("hid", dst, lo, hi))
                    P.dma("sp", lambda e, t=t, c=c, lo=lo, hi=hi, n=n: e.dma_start(
                        out=nk_d[c * 64:(c + 1) * 64, lo:hi], in_=TMPR[0:64, t, 0:n]),
                        reads=[("tmp", t)], writes=[])
                    return
                sq2 = nxt("sq", NSQ)
                P.op("act", lambda e, t=t, sq2=sq2, n=n: e.activation(out=SQ[:, sq2, 0:n], in_=TMPR[:, t, 0:n],
                                                                      func=AF.Copy),
                     reads=[("tmp", t)], writes=[("sq", sq2)])
                qst[u]["t"] = t
                qst[u]["sq2"] = sq2

            def qk_C(u):
                kind, c, ti, lo, hi = qk_units[u]
                n = hi - lo
                dst = c if kind == "q" else KT0 + c
                t = qst[u]["t"]
                sq2 = qst[u]["sq2"]
                b3 = ps_alloc()
                P.mm([lambda e, sq2=sq2, b3=b3, n=n: e.matmul(PS[:, b3 * 512: b3 * 512 + n], PM[:, :],
                                                              SQ[:, sq2, 0:n], start=True, stop=True)],
                     reads=[("sq", sq2), "pm"], writes=pskeys(b3))
                t2 = nxt("tmp", NTMP)
                P.op("dve", lambda e, b3=b3, t2=t2, lo=lo, hi=hi, n=n: e.tensor_tensor(
                    out=TMPR[:, t2, 0:n], in0=PS[:, b3 * 512: b3 * 512 + n], in1=SIN[:, lo:hi], op=ALU.mult),
                    reads=pskeys(b3) + ["sin", "scr"], writes=[("tmp", t2)])
                P.op("pool", lambda e, t=t, lo=lo, hi=hi, n=n: e.tensor_tensor(
                    out=TMPR[:, t, 0:n], in0=TMPR[:, t, 0:n], in1=COS[:, lo:hi], op=ALU.mult),
                    reads=[("tmp", t), "cos", "scr"], writes=[("tmp", t)])
                P.op("dve", lambda e, t=t, t2=t2, lo=lo, hi=hi, n=n, dst=dst: e.tensor_tensor(
                    out=HID[:, dst, lo:hi], in0=TMPR[:, t, 0:n], in1=TMPR[:, t2, 0:n], op=ALU.add),
                    reads=[("tmp", t), ("tmp", t2)], writes=gran("hid", dst, lo, hi))

            nu = len(qk_units)
            for i in range(nu + 2):
                if i < nu:
                    qk_A(i)
                if 0 <= i - 1 < nu:
                    qk_B(i - 1)
                if g.lat and 0 <= i - 2 < nu:
                    qk_C(i - 2)
            s = getw((wvt_d[:, :], 2048))
            for q in range(nkt_lat):
                c0 = q * 128
                b = ps_alloc()
                fns = [(lambda e, kc=kc, b=b, c0=c0, s=s: e.matmul(
                    PS[:, b * 512: b * 512 + 256], H[:, kc, c0:c0 + 128], WSL[:, s, kc * 256:(kc + 1) * 256],
                    start=(kc == 0), stop=(kc == 7))) for kc in range(8)]
                rd = [("w", s)]
                for kc in range(8):
                    rd += gran("h", kc, c0, c0 + 128)
                P.mm(fns, reads=rd, writes=pskeys(b))
                P.op("act", lambda e, b=b, q=q: e.activation(
                    out=VE3[:, q, :, 64:128], in_=PS[:, b * 512: b * 512 + 256].rearrange("p (k d) -> p k d", d=64),
                    func=AF.Copy), reads=pskeys(b) + ["ve", "scr"], writes=[("vet", q)])
            if not g.lat:
                for c in range(2):
                    s = getw((wvf_d[c], 1024))
                    for (lo, hi) in g.tiles:
                        n = hi - lo
                        b = ps_alloc()
                        fns = [(lambda e, kc=kc, b=b, lo=lo, hi=hi, n=n, s=s: e.matmul(
                            PS[:, b * 512: b * 512 + n], WSL[:, s, kc * 128:(kc + 1) * 128], H[:, kc, lo:hi],
                            start=(kc == 0), stop=(kc == 7))) for kc in range(8)]
                        rd = [("w", s)]
                        for kc in range(8):
                            rd += gran("h", kc, lo, hi)
                        P.mm(fns, reads=rd, writes=pskeys(b))
                        t = nxt("tmp", NTMP)
                        P.op("act", lambda e, b=b, t=t, n=n: e.activation(out=TMPR[:, t, 0:n], in_=PS[:, b * 512: b * 512 + n],
                                                                          func=AF.Copy),
                             reads=pskeys(b), writes=[("tmp", t)])
                        P.dma("sp", lambda e, t=t, c=c, lo=lo, hi=hi, n=n: e.dma_start(
                            out=nv_d[c * 128:(c + 1) * 128, lo:hi], in_=TMPR[:, t, 0:n]),
                            reads=[("tmp", t)], writes=[])

            PT2 = PT.rearrange("p (s c) -> p s c", s=2)
            if DEBUG_STOP is not None and DEBUG_STOP[1] == "attn_ve":
                Xf = X[:, :, :].rearrange("p a b -> p (a b)")
                P.op("dve", lambda e: e.tensor_copy(out=Xf[:, 0:8448], in_=VE),
                     reads=[("vet", t_) for t_ in range(11)] + ["ve"], writes=[k for c in range(8) for k in gran("x", c, 0, g.cols)])
                return

            def ve_lhs(tile, kv, h):
                if h % 2 == 0:
                    return VE3[:, tile, kv, 64:192]
                return VE3[:, tile, kv, 0:128]

            def normalize(b, h, qlo, nq):
                off = (h % 2) * 64
                dn = 64 - off
                t = nxt("tmp", NTMP)
                P.op("dve", lambda e, b=b, t=t, nq=nq, h=h, off=off, dn=dn: e.tensor_scalar(
                    out=TMPR[off:off + 64, t, 0:nq], in0=PS[dn:dn + 64, b * 512: b * 512 + nq],
                    scalar1=ESK[dn:dn + 64, h:h + 1], scalar2=None, op0=ALU.add),
                    reads=pskeys(b) + ["esk"], writes=[("tmp", t)])
                P.op("act", lambda e, t=t, nq=nq, off=off: e.activation(
                    out=TMPR[off:off + 64, t, 0:nq], in_=TMPR[off:off + 64, t, 0:nq], func=AF.Ln),
                    reads=[("tmp", t)], writes=[("tmp", t)])
                P.op("act", lambda e, t=t, nq=nq, off=off: e.activation(
                    out=TMPR[off:off + 64, t, 0:nq], in_=TMPR[off:off + 64, t, 0:nq], func=AF.Exp, scale=-1.0),
                    reads=[("tmp", t)], writes=[("tmp", t)])
                P.op("dve", lambda e, b=b, t=t, nq=nq, h=h, off=off, qlo=qlo: e.tensor_tensor(
                    out=HID[off:off + 64, O0 + h // 2, qlo:qlo + nq], in0=PS[off:off + 64, b * 512: b * 512 + nq],
                    in1=TMPR[off:off + 64, t, 0:nq], op=ALU.mult),
                    reads=pskeys(b) + [("tmp", t)], writes=gran("hid", O0 + h // 2, qlo, qlo + nq))

            if not g.lat:
                units = [(sq_i, h) for sq_i in range(4) for h in range(16)]
                st = {}

                def c_s1(u):
                    sq_i, h = units[u]
                    base = sq_i * 256
                    kv = h // 4
                    off = (h % 2) * 64
                    b = ps_alloc()
                    fns = [(lambda e, kt=kt, b=b, off=off, kv=kv, h=h, base=base: e.matmul(
                        PS[:, b * 512 + kt * 256: b * 512 + (kt + 1) * 256],
                        HID[off:off + 64, KT0 + kv, base + kt * 128: base + (kt + 1) * 128],
                        HID[off:off + 64, h // 2, base:base + 256], start=True, stop=True)) for kt in range(2)]
                    rd = gran("hid", KT0 + kv, base, base + 256) + gran("hid", h // 2, base, base + 256)
                    P.mm(fns, reads=rd, writes=pskeys(b))
                    ps_ = nxt("pt", 2)
                    P.op("act", lambda e, b=b, ps_=ps_: e.activation(
                        out=PT2[:, ps_, 0:512], in_=PS[:, b * 512:(b + 1) * 512], func=AF.Exp, scale=0.125),
                        reads=pskeys(b) + ["scr"], writes=[("pt", ps_, 0), "cos", "sin"])
                    st[u] = ps_

                def c_s2(u):
                    sq_i, h = units[u]
                    base = sq_i * 256
                    kv = h // 4
                    ps_ = st.pop(u)
                    b2 = ps_alloc()
                    fns = [(lambda e, kt=kt, b2=b2, kv=kv, ps_=ps_, sq_i=sq_i, h=h: e.matmul(
                        PS[:, b2 * 512: b2 * 512 + 256], ve_lhs(sq_i * 2 + kt, kv, h),
                        PT2[:, ps_, kt * 256:(kt + 1) * 256], start=(kt == 0), stop=(kt == 1))) for kt in range(2)]
                    P.mm(fns, reads=[("pt", ps_, 0), ("vet", sq_i * 2), ("vet", sq_i * 2 + 1), "scr"], writes=pskeys(b2))
                    normalize(b2, h, base, 256)

                c_s1(0)
                for u in range(len(units)):
                    if u + 1 < len(units):
                        c_s1(u + 1)
                    c_s2(u)
            else:
                CKT = HID[:, 20, 0:1024].rearrange("p (k t) -> p k t", t=256)
                units = [(n0, nbk, h) for (n0, nbk) in ((1, 4), (5, 3)) for h in range(16)]
                st = {}

                def l_s1a(u):
                    n0, nbk, h = units[u]
                    qlo = n0 * 128
                    nq = nbk * 128
                    kv = h // 4
                    off = (h % 2) * 64
                    ps_ = nxt("pt", 2)
                    st[u] = ps_
                    for rel in range(3):
                        b = ps_alloc()
                        fns = []
                        for i in range(nbk):
                            kt = n0 + i - 1 + rel
                            fns.append(lambda e, i=i, kt=kt, b=b, off=off, kv=kv, h=h, n0=n0: e.matmul(
                                PS[:, b * 512 + i * 128: b * 512 + (i + 1) * 128],
                                HID[off:off + 64, KT0 + kv, kt * 128:(kt + 1) * 128],
                                HID[off:off + 64, h // 2, (n0 + i) * 128:(n0 + i + 1) * 128], start=True, stop=True))
                        rd = gran("hid", KT0 + kv, (n0 - 1 + rel) * 128, (n0 + nbk - 1 + rel) * 128) + \
                            gran("hid", h // 2, qlo, qlo + nq)
                        P.mm(fns, reads=rd, writes=pskeys(b))
                        runs = []
                        for i in range(nbk):
                            kt = n0 + i - 1 + rel
                            fl = 0 if kt < 2 else (1 if kt >= 6 else None)
                            if runs and runs[-1][2] == fl:
                                runs[-1][1] = i + 1
                            else:
                                runs.append([i, i + 1, fl])
                        for (i0, i1, fl) in runs:
                            bias = ZB[:, 0:1] if fl is None else NEGB[:, fl:fl + 1]
                            P.op("act", lambda e, b=b, i0=i0, i1=i1, rel=rel, ps_=ps_, bias=bias: e.activation(
                                out=PT2[:, ps_, rel * 512 + i0 * 128: rel * 512 + i1 * 128],
                                in_=PS[:, b * 512 + i0 * 128: b * 512 + i1 * 128], func=AF.Exp,
                                scale=0.125, bias=bias),
                                reads=pskeys(b) + ["negb", "zb", "scr"], writes=[("pt", ps_, rel), "cos", "sin"])
                        if rel != 1:
                            mi = 0 if rel == 0 else 1
                            P.op("dve", lambda e, rel=rel, ps_=ps_, mi=mi, nq=nq, nbk=nbk: e.tensor_tensor(
                                out=PT2[:, ps_, rel * 512: rel * 512 + nq].rearrange("p (a b) -> p a b", b=128),
                                in0=PT2[:, ps_, rel * 512: rel * 512 + nq].rearrange("p (a b) -> p a b", b=128),
                                in1=WMASK[:, mi, :].unsqueeze(1).broadcast_to([128, nbk, 128]), op=ALU.mult),
                                reads=[("pt", ps_, rel), "wmask"], writes=[("pt", ps_, rel)])

                def l_s1b(u):
                    n0, nbk, h = units[u]
                    qlo = n0 * 128
                    nq = nbk * 128
                    kv = h // 4
                    off = (h % 2) * 64
                    ps_ = st[u]
                    for kt in range(2):
                        b = ps_alloc()
                        P.mm([lambda e, kt=kt, b=b, off=off, kv=kv, h=h, nq=nq, qlo=qlo: e.matmul(
                            PS[:, b * 512: b * 512 + nq], CKT[off:off + 64, kv, kt * 128:(kt + 1) * 128],
                            HID[off:off + 64, h // 2, qlo:qlo + nq], start=True, stop=True)],
                            reads=["ckt"] + gran("hid", h // 2, qlo, qlo + nq), writes=pskeys(b))
                        P.op("act", lambda e, b=b, kt=kt, ps_=ps_, nq=nq: e.activation(
                            out=PT2[:, ps_, (3 + kt) * 512:(3 + kt) * 512 + nq], in_=PS[:, b * 512: b * 512 + nq],
                            func=AF.Exp, scale=0.125), reads=pskeys(b) + ["scr"],
                            writes=[("pt", ps_, 3 + kt), "cos", "sin"])

                def l_s2(u):
                    n0, nbk, h = units[u]
                    qlo = n0 * 128
                    nq = nbk * 128
                    kv = h // 4
                    ps_ = st.pop(u)
                    b2 = ps_alloc()
                    vkeys = [("vet", t_) for t_ in range(11)] + ["scr"]
                    for rel in range(3):
                        fns = []
                        for i in range(nbk):
                            kt = n0 + i - 1 + rel
                            fns.append(lambda e, i=i, kt=kt, rel=rel, b2=b2, kv=kv, ps_=ps_, h=h: e.matmul(
                                PS[:, b2 * 512 + i * 128: b2 * 512 + (i + 1) * 128], ve_lhs(kt, kv, h),
                                PT2[:, ps_, rel * 512 + i * 128: rel * 512 + (i + 1) * 128], start=(rel == 0),
                                stop=False))
                        P.mm(fns, reads=[("pt", ps_, rel)] + vkeys, writes=pskeys(b2))
                    fns = []
                    for kt in range(2):
                        fns.append(lambda e, kt=kt, b2=b2, kv=kv, ps_=ps_, h=h, nq=nq: e.matmul(
                            PS[:, b2 * 512: b2 * 512 + nq], ve_lhs(9 + kt, kv, h),
                            PT2[:, ps_, (3 + kt) * 512:(3 + kt) * 512 + nq], start=False, stop=(kt == 1)))
                    P.mm(fns, reads=[("pt", ps_, 3), ("pt", ps_, 4)] + vkeys, writes=pskeys(b2))
                    normalize(b2, h, qlo, nq)

                l_s1a(0)
                l_s1b(0)
                for u in range(len(units)):
                    if u + 1 < len(units):
                        l_s1a(u + 1)
                    l_s2(u)
                    if u + 1 < len(units):
                        l_s1b(u + 1)
            if DEBUG_STOP is not None and DEBUG_STOP[1] in ("attn_o", "attn_q", "attn_k"):
                src0 = {"attn_o": O0, "attn_q": 0, "attn_k": KT0}[DEBUG_STOP[1]]
                for c in range(8 if src0 != KT0 else 4):
                    P.op("dve", lambda e, c=c: e.tensor_copy(out=X[:, c, 0:g.cols], in_=HID[:, src0 + c, 0:g.cols]),
                         reads=gran("hid", src0 + c, 0, g.cols), writes=gran("x", c, 0, g.cols))
                return
            out_proj(qtiles, (lambda nn: ((wo_d[nn], 1024),)), 8, 8,
                     (lambda kk, lo, hi: HID[:, O0 + kk, lo:hi]), (lambda kk, lo, hi: gran("hid", O0 + kk, lo, hi)),
                     l, ci, 16, nt0=nt0)

        groups = [GC] + ([GL] if RUN_LAT else [])

        def run_all():
            ps_state["i"] = 0
            for k_ in ring_state:
                ring_state[k_] = 0
            del ada_pending[:]
            phase_consts()
            ada_enqueue(0)
            for _ in range(8):
                ada_step()
            load_wv(0)
            for gi_, g in enumerate(groups):
                if gi_ > 0:
                    phase_load_x(g)
                stop = False
                def mixer_tile0(l_):
                    if l_ >= DEPTH:
                        return None
                    if l_ % 3 == 2:
                        return g.tiles[0]
                    return even_tiles(*g.win_m[l_])[0]

                for l in range(DEPTH):
                    kind = l % 3
                    f_t0 = even_tiles(*g.win_f[l])[0]
                    if kind == 0:
                        phase_gmlp(g, l, nt0=f_t0)
                    elif kind == 1:
                        phase_pool(g, l)
                    else:
                        phase_attn(g, l, nt0=f_t0)
                    if DEBUG_STOP is not None and DEBUG_STOP[0] == l and DEBUG_STOP[1] != "ffn":
                        stop = True
                        break
                    if l == 2:
                        load_wv(1)
                    if l == 3 and gi_ + 1 < len(groups):
                        load_wv(0)
                    phase_ffn(g, l, nt0=(None if DEBUG_STOP is not None else mixer_tile0(l + 1)))
                    if DEBUG_STOP == (l, "ffn"):
                        stop = True
                        break
                ada_drain()
                phase_store_x(g)

        P.dry = True
        run_all()
        P.dry = False
        run_all()
        assert wstate["used"] == len(wstate["list"]), (wstate["used"], len(wstate["list"]))

        sem_names = ["pe", "act", "dve", "pool"]
        sems = {}
        for nme in sem_names:
            sems[nme] = es.enter_context(nc.semaphore("s_" + nme))
        for qn in ("sp", "pool"):
            for i in range(NRING):
                sems[(qn, i)] = es.enter_context(nc.semaphore(f"d_{qn}_{i}"))
        print("sbuf bytes remaining", nc.sbuf_bytes_remaining, "counts", P.cnt, flush=True)
        block = es.enter_context(nc.Block())

        def runner(engname):
            def run(e):
                for item in P.q[engname]:
                    if item[0] == "wait":
                        e.wait_ge(sems[item[1]], item[2])
                    elif item[0] == "op":
                        ins = item[1](e)
                        if item[2]:
                            ins.then_inc(sems[engname], 1)
                    else:
                        ins = item[1](e)
                        ins.then_inc(sems[item[2]], 16)
                if engname in ("sp", "pool"):
                    for i in range(NRING):
                        v = P.ring[engname][i]
                        if v:
                            e.wait_ge(sems[(engname, i)], v)
            return run

        block.sync(runner("sp"))
        block.gpsimd(runner("pool"))
        block.scalar(runner("act"))
        block.vector(runner("dve"))
        block.tensor(runner("pe"))
    return nc


def _tile_w(w, kdim_chunks, cols_per):
    K, N = w.shape
    kc = K // 128
    nb = N // cols_per
    t = w.reshape(kc, 128, nb, cols_per).transpose(2, 1, 0, 3)
    return np.ascontiguousarray(t).reshape(nb, 128, kc * cols_per)


def _pp(v):
    return np.ascontiguousarray(v.reshape(-1, 128).T)


_NC_CACHE = {}


def kernel(x_prompt, x_sample, cache_k, cache_v, c, c_ctx,
           w_ada, b_ada, g_mix, g_ffn, w_ffn_in, ffn_conv_w, ffn_conv_b, w_ffn_out,
           a_w_in, a_g_v, a_w_s, a_b_s, a_w_out,
           p_w, p_b, p_scale,
           c_w_qkv, c_g_q, c_g_k, c_sink, c_w_o):
    f = np.float32
    A = lambda z: np.ascontiguousarray(np.asarray(z, dtype=f))
    x_prompt, x_sample, cache_k, cache_v, c, c_ctx = map(A, (x_prompt, x_sample, cache_k, cache_v, c, c_ctx))
    w_ada, b_ada, g_mix, g_ffn, w_ffn_in = map(A, (w_ada, b_ada, g_mix, g_ffn, w_ffn_in))
    ffn_conv_w, ffn_conv_b, w_ffn_out = map(A, (ffn_conv_w, ffn_conv_b, w_ffn_out))
    a_w_in, a_g_v, a_w_s, a_b_s, a_w_out = map(A, (a_w_in, a_g_v, a_w_s, a_b_s, a_w_out))
    p_w, p_b, p_scale = map(A, (p_w, p_b, p_scale))
    c_w_qkv, c_g_q, c_g_k, c_sink, c_w_o = map(A, (c_w_qkv, c_g_q, c_g_k, c_sink, c_w_o))

    shared = {}
    shared["w_ada_t"] = np.stack([_tile_w(w_ada[l], 8, 128).reshape(24, 2, 128, 1024).transpose(0, 2, 1, 3)
                                  .reshape(24, 128, 2048) for l in range(DEPTH)])
    ba = b_ada.reshape(DEPTH, 48, 128).transpose(2, 0, 1)
    shared["b_ada_t"] = np.ascontiguousarray(np.repeat(ba[:, :, :, None], 2, axis=3))
    shared["g_mix_t"] = np.ascontiguousarray(g_mix.reshape(DEPTH, 8, 128).transpose(2, 0, 1))
    shared["g_ffn_t"] = np.ascontiguousarray(g_ffn.reshape(DEPTH, 8, 128).transpose(2, 0, 1))
    shared["w_ffn_in_t"] = np.stack([_tile_w(w_ffn_in[l], 8, 128) for l in range(DEPTH)])
    shared["conv_w_t"] = np.ascontiguousarray(ffn_conv_w.reshape(DEPTH, 3, NJ, 128).transpose(3, 0, 1, 2))
    shared["conv_b_t"] = np.ascontiguousarray(ffn_conv_b.reshape(DEPTH, NJ, 128).transpose(2, 0, 1))
    wo = []
    for l in range(DEPTH):
        t = w_ffn_out[l].reshape(NJ, 128, 8, 128).transpose(2, 1, 0, 3)
        t = t.reshape(8, 128, 2, 11 * 128).transpose(0, 2, 1, 3).reshape(16, 128, 11 * 128)
        wo.append(t)
    shared["w_ffn_out_t"] = np.ascontiguousarray(np.stack(wo))
    shared["a_w_u_t"] = np.stack([_tile_w(a_w_in[j][:, :2048], 8, 128) for j in range(2)])
    shared["a_w_v_t"] = np.stack([_tile_w(a_w_in[j][:, 2048:], 8, 512) for j in range(2)])
    shared["a_g_v_t"] = np.ascontiguousarray(a_g_v.reshape(2, 16, 128).transpose(2, 0, 1))
    shared["a_w_s_t"] = np.ascontiguousarray(a_w_s.transpose(0, 3, 1, 2)).reshape(2, 128, 8 * 128)
    shared["a_b_s_t"] = np.ascontiguousarray(np.broadcast_to(a_b_s[None], (128, 2, 8, 128)))
    shared["a_w_out_t"] = np.stack([np.ascontiguousarray(a_w_out[j].reshape(16, 128, 8, 128).transpose(2, 1, 0, 3))
                                    .reshape(8, 128, 16 * 128) for j in range(2)])
    pw = p_w[0].reshape(4, 2, 128, 2, 128).transpose(2, 0, 3, 1, 4)
    shared["p_w_t"] = np.ascontiguousarray(pw).reshape(128, 8 * 2 * 128)
    shared["p_b_t"] = _pp(p_b[0])
    shared["p_scale_t"] = _pp(p_scale[0])
    wqkv = c_w_qkv[0]
    shared["c_w_q_t"] = _tile_w(wqkv[:, :1024], 8, 128)
    kd = np.concatenate([np.concatenate([wqkv[:, 1024 + kh * 64:1024 + (kh + 1) * 64]] * 2, axis=1) for kh in range(4)],
                        axis=1)
    shared["c_w_k_t"] = _tile_w(np.ascontiguousarray(kd), 8, 128)
    shared["c_w_vt_t"] = _tile_w(np.ascontiguousarray(wqkv[:, 1280:1536]), 8, 256)[0]
    shared["c_w_vf_t"] = _tile_w(np.ascontiguousarray(wqkv[:, 1280:1536]), 8, 128)
    shared["c_w_o_t"] = _tile_w(c_w_o[0], 8, 128)
    shared["c_g_q_t"] = np.ascontiguousarray(np.tile(c_g_q[0], 2)[:, None])
    shared["c_g_k_t"] = np.ascontiguousarray(np.tile(c_g_k[0], 2)[:, None])
    shared["c_sink_t"] = np.ascontiguousarray(np.broadcast_to(c_sink[0][None], (128, 16)))
    shared["ones_t"] = np.ones((128, 128), f)
    blk = np.zeros((128, 128), f)
    blk[:64, :64] = 1
    blk[64:, 64:] = 1
    shared["blk_t"] = blk
    pm = np.zeros((128, 128), f)
    for m in range(128):
        if m % 32 < 16:
            pm[m + 16, m] = -1.0
        else:
            pm[m - 16, m] = 1.0
    shared["pm_t"] = pm
    kk = np.arange(128)[:, None]
    qq = np.arange(128)[None, :]
    shared["wmask_t"] = np.ascontiguousarray(np.stack([(kk >= qq), (kk <= qq)], axis=1).astype(f))

    def ic_tables(S):
        out = np.zeros((2, 4, 16), f)
        for gi, w in enumerate((2, 4, 8, 16)):
            half = w // 2
            for i in range(16):
                t = i
                lo = max(t - half, 0); hi = min(t + half - 1, S - 1)
                out[0, gi, i] = np.float32(1.0) / np.float32(hi - lo + 1)
                t = S - 16 + i
                lo = max(t - half, 0); hi = min(t + half - 1, S - 1)
                out[1, gi, i] = np.float32(1.0) / np.float32(hi - lo + 1)
        return out

    ic_c = ic_tables(256)
    ic_l = ic_tables(2048)
    mid = np.zeros((4, 16), f)
    for gi, w in enumerate((2, 4, 8, 16)):
        mid[gi, :] = np.float32(1.0) / np.float32(w)
    shared["ic_ctx_t"] = np.ascontiguousarray(np.broadcast_to(ic_c[None], (128, 2, 4, 16)))

    inv = (10000.0 ** (-np.arange(16, dtype=np.float32) / 16)).astype(f)

    in_maps = []
    for core in range(8):
        m = dict(shared)
        b_lat = core // 4
        cq = core % 4
        xc = x_prompt[4 * core:4 * core + 4].reshape(1024, D)
        m["xc"] = np.ascontiguousarray(xc.T.reshape(8, 128, 1024).transpose(1, 0, 2))
        start = 512 * cq - 256
        pos = start + np.arange(LAT_COLS)
        valid = (pos >= 0) & (pos < 2048)
        xl = np.zeros((LAT_COLS, D), f)
        xl[valid] = x_sample[b_lat][pos[valid]]
        m["xl"] = np.ascontiguousarray(xl.T.reshape(8, 128, LAT_COLS).transpose(1, 0, 2))
        cond2 = np.stack([c_ctx, c[b_lat]], axis=1)
        m["condT"] = np.ascontiguousarray(cond2.reshape(8, 128, 2).transpose(1, 0, 2))
        row = (pos // 64).astype(f)
        col = (pos % 64).astype(f)
        p = np.arange(128)
        fr = p % 16
        is_col = ((p % 64) // 32) == 1
        ang = np.where(is_col[:, None], col[None, :] * inv[fr][:, None], row[None, :] * inv[fr][:, None]).astype(f)
        m["cos_t"] = np.cos(ang).astype(f)
        m["sin_t"] = np.sin(ang).astype(f)
        fl = np.array([1.0 if cq > 0 else 0.0, 1.0 if cq < 3 else 0.0], f)
        m["flags_t"] = np.ascontiguousarray(np.broadcast_to(fl[None], (128, 2)))
        icl = np.stack([ic_l[0] if cq == 0 else mid, ic_l[1] if cq == 3 else mid], axis=0)
        m["ic_lat_t"] = np.ascontiguousarray(np.broadcast_to(icl[None], (128, 2, 4, 16)))
        ck = cache_k[b_lat, 0]
        ckt = ck.transpose(1, 2, 0)
        m["cache_k_t"] = np.ascontiguousarray(np.concatenate([ckt, ckt], axis=1).transpose(1, 0, 2))
        cv = cache_v[b_lat, 0].reshape(2, 128, 256)
        m["cache_v_t"] = np.ascontiguousarray(cv.transpose(1, 0, 2))
        in_maps.append(m)

    if "nc" not in _NC_CACHE:
        _NC_CACHE["nc"] = build_program()
    nc = _NC_CACHE["nc"]
    res = run_bass_kernel_spmd(nc, in_maps, core_ids=list(range(8)))

    y_prompt = np.zeros((32, 256, D), f)
    y_sample = np.zeros((2, 2048, D), f)
    new_k = np.zeros((32, 1, 256, 4, 64), f)
    new_v = np.zeros((32, 1, 256, 4, 64), f)
    for core in range(8):
        r = res.results[core]
        yc = np.asarray(r["yc"]).transpose(1, 0, 2).reshape(D, 1024).T
        y_prompt[4 * core:4 * core + 4] = yc.reshape(4, 256, D)
        nk = np.asarray(r["nk"]).T.reshape(4, 256, 4, 64)
        nv = np.asarray(r["nv"]).T.reshape(4, 256, 4, 64)
        new_k[4 * core:4 * core + 4, 0] = nk
        new_v[4 * core:4 * core + 4, 0] = nv
        yl = np.asarray(r["yl"]).transpose(1, 0, 2).reshape(D, LAT_COLS).T
        b_lat = core // 4
        cq = core % 4
        lo_col = 256 if cq == 0 else 257
        hi_col = 768 if cq == 3 else 769
        tok0 = 512 * cq - 256
        y_sample[b_lat, tok0 + lo_col: tok0 + hi_col] = yl[lo_col:hi_col]
    return (y_prompt, y_sample, new_k, new_v)
```
